# Optimizing a Trainium2 kernel written in Bass

```python
import math, functools
import jax, jax.numpy as jnp
from jax import lax
import numpy as np

D_MODEL = 1024
BATCH = 8
SEQ = 2048
DEPTH = 1
DEC_BATCH = 128
DEC_SEQ = 4
PAST_LEN = 2048
PAGE_SIZE = 128

D_SSM = D_MODEL // 2
D_ATTN = D_MODEL // 2
D_MIX = D_SSM + D_ATTN
SSM_GROUP = 16
N_SSM_GROUPS = D_SSM // SSM_GROUP
SSM_STATE = 64
HEAD_DIM = 64
N_HEADS = D_ATTN // HEAD_DIM
DILATED_PATTERNS = ((128, 1), (512, 4), (2048, 16))
WINDOW_MAX = 2048
BAND_BLOCK = 128
D_IN_PROJ = 2 * D_SSM + 4 * D_ATTN
RMS_EPS = 1e-6
DT_MIN = 1e-3
DT_MAX = 1e-1

kernel_name = "hymba_s5_longnet_decode_step"


def _rmsnorm(x, g):
    xf = x.astype(jnp.float32)
    xf = xf * lax.rsqrt(jnp.mean(xf * xf, axis=-1, keepdims=True) + RMS_EPS)
    return (xf * g.astype(jnp.float32)).astype(x.dtype)


def _complex_affine_combine(e1, e2):
    ar1, ai1, br1, bi1 = e1
    ar2, ai2, br2, bi2 = e2
    ar = ar2 * ar1 - ai2 * ai1
    ai = ar2 * ai1 + ai2 * ar1
    br = ar2 * br1 - ai2 * bi1 + br2
    bi = ar2 * bi1 + ai2 * br1 + bi2
    return ar, ai, br, bi


def _s5_scan(u, h0_re, h0_im, A_re, A_im, log_dt, B_re, B_im, C_re, C_im, D):
    f32 = jnp.float32
    n_, t_ = u.shape[0], u.shape[1]
    u = u.astype(f32).reshape(n_, t_, N_SSM_GROUPS, SSM_GROUP)
    A_re = A_re.astype(f32)
    A_im = A_im.astype(f32)
    dt = jnp.exp(log_dt.astype(f32))[:, None]
    mag = jnp.exp(A_re * dt)
    ang = A_im * dt
    lam_re = mag * jnp.cos(ang)
    lam_im = mag * jnp.sin(ang)
    den = A_re * A_re + A_im * A_im
    f_re = ((lam_re - 1.0) * A_re + lam_im * A_im) / den
    f_im = (lam_im * A_re - (lam_re - 1.0) * A_im) / den
    B_re = B_re.astype(f32)
    B_im = B_im.astype(f32)
    Bb_re = f_re[..., None] * B_re - f_im[..., None] * B_im
    Bb_im = f_re[..., None] * B_im + f_im[..., None] * B_re
    bu_re = jnp.einsum('btgc,gpc->btgp', u, Bb_re)
    bu_im = jnp.einsum('btgc,gpc->btgp', u, Bb_im)
    h0_re = h0_re.astype(f32)
    h0_im = h0_im.astype(f32)
    bu_re = bu_re.at[:, 0].add(lam_re * h0_re - lam_im * h0_im)
    bu_im = bu_im.at[:, 0].add(lam_re * h0_im + lam_im * h0_re)
    a_re = jnp.broadcast_to(lam_re, bu_re.shape)
    a_im = jnp.broadcast_to(lam_im, bu_im.shape)
    _, _, x_re, x_im = lax.associative_scan(_complex_affine_combine, (a_re, a_im, bu_re, bu_im), axis=1)
    y = (jnp.einsum('btgp,gcp->btgc', x_re, C_re.astype(f32))
         - jnp.einsum('btgp,gcp->btgc', x_im, C_im.astype(f32))
         + D.astype(f32).reshape(N_SSM_GROUPS, SSM_GROUP) * u)
    return y.reshape(n_, t_, D_SSM), x_re[:, -1], x_im[:, -1]


def _dilated_band(q, k, v, window, dil):
    b_, s_, h_, e_ = q.shape
    n = s_ // dil
    nb = -(-n // BAND_BLOCK)
    npad = nb * BAND_BLOCK
    reach = window // dil

    def to_blocks(t):
        t = t.reshape(b_, n, dil, h_, e_).transpose(0, 2, 1, 3, 4)
        t = jnp.pad(t, ((0, 0), (0, 0), (0, npad - n), (0, 0), (0, 0)))
        return t.reshape(b_, dil, nb, BAND_BLOCK, h_, e_)

    def with_prev(t):
        prev = jnp.pad(t[:, :, :-1], ((0, 0), (0, 0), (1, 0), (0, 0), (0, 0), (0, 0)))
        return jnp.concatenate([prev, t], axis=3)

    qb = to_blocks(q)
    kk = with_prev(to_blocks(k))
    vv = with_prev(to_blocks(v))
    s = jnp.einsum('brnqhe,brnkhe->brnhqk', qb, kk) * (HEAD_DIM ** -0.5)
    qi = jnp.arange(BAND_BLOCK)[None, :, None]
    ki = jnp.arange(2 * BAND_BLOCK)[None, None, :]
    blk = jnp.arange(nb)[:, None, None]
    dist = qi + BAND_BLOCK - ki
    u_k = blk * BAND_BLOCK - BAND_BLOCK + ki
    valid = (dist >= 0) & (dist <= reach) & (u_k >= 0)
    s = jnp.where(valid[None, None, :, None], s, -jnp.inf)
    m = jnp.max(s, axis=-1)
    p = jnp.exp(s - m[..., None])
    l = jnp.sum(p, axis=-1)
    o = jnp.einsum('brnhqk,brnkhe->brnqhe', p, vv)
    o = o.reshape(b_, dil, npad, h_, e_)[:, :, :n].transpose(0, 2, 1, 3, 4).reshape(b_, s_, h_, e_)

    def stat_back(t):
        t = t.transpose(0, 1, 2, 4, 3).reshape(b_, dil, npad, h_)[:, :, :n]
        return t.transpose(0, 2, 1, 3).reshape(b_, s_, h_)

    return o, stat_back(m), stat_back(l)


def _dilated_gather(q, k_all, v_all, window, dil):
    t_ = q.shape[1]
    l_ = k_all.shape[1]
    offs = jnp.arange(window // dil + 1) * dil
    idx = (l_ - t_ + jnp.arange(t_))[:, None] - offs[None, :]
    valid = idx >= 0
    idx_c = jnp.clip(idx, 0, l_ - 1)
    kg = k_all[:, idx_c]
    vg = v_all[:, idx_c]
    s = jnp.einsum('bthe,btkhe->bhtk', q, kg) * (HEAD_DIM ** -0.5)
    s = jnp.where(valid[None, None], s, -jnp.inf)
    m = jnp.max(s, axis=-1)
    p = jnp.exp(s - m[..., None])
    l = jnp.sum(p, axis=-1)
    o = jnp.einsum('bhtk,btkhe->bthe', p, vg)
    return o, m.transpose(0, 2, 1), l.transpose(0, 2, 1)


def _combine(parts):
    big_m = functools.reduce(jnp.maximum, [m for _, m, _ in parts])
    ws = [jnp.exp(m - big_m) for _, m, _ in parts]
    num = functools.reduce(jnp.add, [w[..., None] * o for w, (o, _, _) in zip(ws, parts)])
    den = functools.reduce(jnp.add, [w * l for w, (_, _, l) in zip(ws, parts)])
    return num / den[..., None]


def _prompt_attention(q, k, v):
    return _combine([_dilated_band(q, k, v, w, d) for w, d in DILATED_PATTERNS])


def _sample_attention(q, k, v, k_past, v_past):
    k_all = jnp.concatenate([k_past.astype(jnp.float32), k], axis=1)
    v_all = jnp.concatenate([v_past.astype(jnp.float32), v], axis=1)
    return _combine([_dilated_gather(q, k_all, v_all, w, d) for w, d in DILATED_PATTERNS])


def _layer(h, h0_re, h0_im, attend, norm_in_g, w_in, A_re, A_im, log_dt, B_re, B_im, C_re, C_im, D,
           w_glu, b_glu, norm_ssm_g, norm_attn_g, w_out):
    n_, t_ = h.shape[0], h.shape[1]
    proj = _rmsnorm(h, norm_in_g) @ w_in
    o1 = D_SSM
    o2 = o1 + D_SSM
    o3 = o2 + D_ATTN
    o4 = o3 + D_ATTN
    o5 = o4 + D_ATTN
    x_s, z_s = proj[..., :o1], proj[..., o1:o2]
    q, k, v, z_a = proj[..., o2:o3], proj[..., o3:o4], proj[..., o4:o5], proj[..., o5:]
    y_s, st_re, st_im = _s5_scan(x_s, h0_re, h0_im, A_re, A_im, log_dt, B_re, B_im, C_re, C_im, D)
    g = jax.nn.gelu(y_s)
    gl = g @ w_glu.astype(jnp.float32) + b_glu.astype(jnp.float32)
    y_s = gl[..., :D_SSM] * jax.nn.sigmoid(gl[..., D_SSM:])
    y_s = (y_s * jax.nn.silu(z_s.astype(jnp.float32))).astype(h.dtype)
    y_s = _rmsnorm(y_s, norm_ssm_g)
    qh = q.astype(jnp.float32).reshape(n_, t_, N_HEADS, HEAD_DIM)
    kh = k.astype(jnp.float32).reshape(n_, t_, N_HEADS, HEAD_DIM)
    vh = v.astype(jnp.float32).reshape(n_, t_, N_HEADS, HEAD_DIM)
    y_a = attend(qh, kh, vh).reshape(n_, t_, D_ATTN)
    y_a = (y_a * jax.nn.silu(z_a.astype(jnp.float32))).astype(h.dtype)
    y_a = _rmsnorm(y_a, norm_attn_g)
    out = jnp.concatenate([y_s, y_a], axis=-1) @ w_out
    return (h + out).astype(h.dtype), kh.astype(h.dtype), vh.astype(h.dtype), st_re, st_im


def setup_inputs(seed: int = 0) -> dict:
    key = jax.random.key(seed)
    ks = jax.random.split(key, 24)
    f32 = jnp.float32
    wbuf = min(WINDOW_MAX, PAST_LEN)
    nrm = lambda k, shp: jax.random.normal(k, shp, f32)
    return {
        "x_prompt": nrm(ks[0], (BATCH, SEQ, D_MODEL)),
        "x_sample": nrm(ks[1], (DEC_BATCH, DEC_SEQ, D_MODEL)),
        "cache_k": nrm(ks[2], (DEPTH, DEC_BATCH, wbuf, N_HEADS, HEAD_DIM)),
        "cache_v": nrm(ks[3], (DEPTH, DEC_BATCH, wbuf, N_HEADS, HEAD_DIM)),
        "state_ssm_re": 0.3 * nrm(ks[4], (DEPTH, DEC_BATCH, N_SSM_GROUPS, SSM_STATE)),
        "state_ssm_im": 0.3 * nrm(ks[5], (DEPTH, DEC_BATCH, N_SSM_GROUPS, SSM_STATE)),
        "norm_in_g": 1.0 + 0.02 * nrm(ks[6], (DEPTH, D_MODEL)),
        "w_in": nrm(ks[7], (DEPTH, D_MODEL, D_IN_PROJ)) * D_MODEL ** -0.5,
        "ssm_A_re": -0.5 + 0.01 * nrm(ks[8], (DEPTH, N_SSM_GROUPS, SSM_STATE)),
        "ssm_A_im": math.pi * jnp.arange(SSM_STATE, dtype=f32) + 0.01 * nrm(ks[9], (DEPTH, N_SSM_GROUPS, SSM_STATE)),
        "ssm_log_dt": jax.random.uniform(ks[10], (DEPTH, N_SSM_GROUPS), f32, math.log(DT_MIN), math.log(DT_MAX)),
        "ssm_B_re": nrm(ks[11], (DEPTH, N_SSM_GROUPS, SSM_STATE, SSM_GROUP)) * (2 * SSM_GROUP) ** -0.5,
        "ssm_B_im": nrm(ks[12], (DEPTH, N_SSM_GROUPS, SSM_STATE, SSM_GROUP)) * (2 * SSM_GROUP) ** -0.5,
        "ssm_C_re": nrm(ks[13], (DEPTH, N_SSM_GROUPS, SSM_GROUP, SSM_STATE)) * SSM_STATE ** -0.5,
        "ssm_C_im": nrm(ks[14], (DEPTH, N_SSM_GROUPS, SSM_GROUP, SSM_STATE)) * SSM_STATE ** -0.5,
        "ssm_D": nrm(ks[15], (DEPTH, D_SSM)),
        "w_glu": nrm(ks[16], (DEPTH, D_SSM, 2 * D_SSM)) * D_SSM ** -0.5,
        "b_glu": 0.01 * nrm(ks[17], (DEPTH, 2 * D_SSM)),
        "norm_ssm_g": 1.0 + 0.02 * nrm(ks[18], (DEPTH, D_SSM)),
        "norm_attn_g": 1.0 + 0.02 * nrm(ks[19], (DEPTH, D_ATTN)),
        "w_out": nrm(ks[20], (DEPTH, D_MIX, D_MODEL)) * D_MIX ** -0.5,
        "final_norm_g": 1.0 + 0.02 * nrm(ks[21], (D_MODEL,)),
    }


def reference(x_prompt, x_sample, cache_k, cache_v, state_ssm_re, state_ssm_im, norm_in_g, w_in,
              ssm_A_re, ssm_A_im, ssm_log_dt, ssm_B_re, ssm_B_im, ssm_C_re, ssm_C_im, ssm_D,
              w_glu, b_glu, norm_ssm_g, norm_attn_g, w_out, final_norm_g):
    hp, hs = x_prompt, x_sample
    wbuf_p = min(WINDOW_MAX, x_prompt.shape[1])
    kp_l, vp_l, rp_l, ip_l = [], [], [], []
    ks_l, vs_l, rs_l, is_l = [], [], [], []
    zero_state = jnp.zeros((x_prompt.shape[0], N_SSM_GROUPS, SSM_STATE), jnp.float32)
    for l in range(DEPTH):
        params = (norm_in_g[l], w_in[l], ssm_A_re[l], ssm_A_im[l], ssm_log_dt[l], ssm_B_re[l], ssm_B_im[l],
                  ssm_C_re[l], ssm_C_im[l], ssm_D[l], w_glu[l], b_glu[l], norm_ssm_g[l], norm_attn_g[l], w_out[l])
        hp, kp, vp, rp, ip = _layer(hp, zero_state, zero_state, _prompt_attention, *params)
        attend_s = functools.partial(_sample_attention, k_past=cache_k[l], v_past=cache_v[l])
        hs, kn, vn, rs, is_ = _layer(hs, state_ssm_re[l], state_ssm_im[l], attend_s, *params)
        kp_l.append(kp[:, -wbuf_p:])
        vp_l.append(vp[:, -wbuf_p:])
        rp_l.append(rp)
        ip_l.append(ip)
        ks_l.append(kn)
        vs_l.append(vn)
        rs_l.append(rs)
        is_l.append(is_)
    y_prompt = _rmsnorm(hp, final_norm_g)
    y_sample = _rmsnorm(hs, final_norm_g)
    return (y_prompt, y_sample,
            jnp.stack(kp_l), jnp.stack(vp_l), jnp.stack(rp_l), jnp.stack(ip_l),
            jnp.stack(ks_l), jnp.stack(vs_l), jnp.stack(rs_l), jnp.stack(is_l))
```

```python
import math
import numpy as np
import concourse.bass as bass
import concourse.mybir as mybir
from concourse.bass_utils import run_bass_kernel_spmd

F32 = mybir.dt.float32
BF16 = mybir.dt.bfloat16
I32 = mybir.dt.int32
AF = mybir.ActivationFunctionType
ALU = mybir.AluOpType
AX = mybir.AxisListType

NCORES = 8
D = 1024
T = 2048
NS = 64
TA = T + NS
NB = 16
L = 8
J = T // L
EPS = 1e-6
PI = math.pi
DEBUG = {}


class Op:
    __slots__ = ("stream", "fn", "deps", "is_dma", "slot", "val", "need_inc", "name")


class Prog:
    STREAMS = ["pe", "act", "dve", "pool", "sp"]

    def __init__(self, ndma=40):
        self.ops = {s: [] for s in self.STREAMS}
        self.st = {}
        self.ndma = ndma
        self.dma_last = [None] * ndma
        self.dma_cnt = [0] * ndma
        self.rr = 0
        self.rr_sw = 0
        self.nhw = ndma - 8
        self.all_dma_out = []

    def add(self, stream, fn, reads=(), writes=(), dma=False, name="", extra=()):
        op = Op()
        op.stream, op.fn, op.is_dma, op.need_inc, op.name = stream, fn, dma, False, name
        op.slot, op.val = None, 0
        deps = {}
        for k in reads:
            s = self.st.get(k)
            if s is not None and s[0] is not None:
                deps[id(s[0])] = (s[0], True)
        for k in writes:
            s = self.st.get(k)
            if s is not None:
                if s[0] is not None and id(s[0]) not in deps:
                    deps[id(s[0])] = (s[0], False)
                for r in s[1].values():
                    if id(r) not in deps:
                        deps[id(r)] = (r, False)
        final = []
        for d, raw in deps.values():
            if d is op:
                continue
            if (not d.is_dma) and (not dma) and d.stream == stream and stream == 'pe' and not raw:
                continue
            final.append(d)
        for d in extra:
            if d is not None:
                final.append(d)
        if dma:
            if stream == "pool":
                slot = self.nhw + self.rr_sw
                self.rr_sw = (self.rr_sw + 1) % (self.ndma - self.nhw)
            else:
                slot = self.rr
                self.rr = (self.rr + 1) % self.nhw
            prev = self.dma_last[slot]
            if prev is not None:
                final.append(prev)
            self.dma_cnt[slot] += 1
            op.slot, op.val = slot, 16 * self.dma_cnt[slot]
            self.dma_last[slot] = op
        for d in final:
            d.need_inc = True
        op.deps = final
        for k in reads:
            s = self.st.setdefault(k, [None, {}])
            key = (stream, id(op)) if dma else stream
            s[1][key] = op
        for k in writes:
            self.st[k] = [op, {}]
        self.ops[stream].append(op)
        return op

    def emit(self, nc, block, sems, dma_sems):
        for s in self.STREAMS:
            c = 0
            for op in self.ops[s]:
                if not op.is_dma and op.need_inc:
                    c += 1
                    op.val = c
        engs = {"pe": block.tensor, "act": block.scalar, "dve": block.vector, "pool": block.gpsimd, "sp": block.sync}

        def run_stream(s):
            def body(eng):
                waited = {}
                for op in self.ops[s]:
                    for d in op.deps:
                        if d.is_dma:
                            key, sem, val = ("d", d.slot), dma_sems[d.slot], d.val
                        else:
                            key, sem, val = ("c", d.stream), sems[d.stream], d.val
                        if waited.get(key, 0) < val:
                            eng.wait_ge(sem, val)
                            waited[key] = val
                    ins = op.fn(eng)
                    if op.is_dma:
                        ins.then_inc(dma_sems[op.slot], 16)
                    elif op.need_inc:
                        ins.then_inc(sems[s], 1)
                if s == "sp":
                    for slot in range(self.ndma):
                        if self.dma_cnt[slot] and waited.get(("d", slot), 0) < 16 * self.dma_cnt[slot]:
                            eng.wait_ge(dma_sems[slot], 16 * self.dma_cnt[slot])
            return body

        for s in self.STREAMS:
            engs[s](run_stream(s))


def build_program(dbg=None, dbg_bf16=(), stop_after=None, small_cache=False):
    dbg = dbg or {}
    nc = bass.Bass("TRN2", target_bir_lowering=False)
    P = Prog()

    def din(name, shape, dt=F32):
        return nc.dram_tensor(name, list(shape), dt, kind="ExternalInput").ap()

    def dout(name, shape, dt=F32):
        return nc.dram_tensor(name, list(shape), dt, kind="ExternalOutput").ap()

    xp = din("xp", [T, D]); xsm = din("xsm", [NS, D])
    ck = din("ck", [NB, 8 if small_cache else 2048, 512]); cv = din("cv", [NB, 8 if small_cache else 2048, 512])
    st_re = din("st_re", [NB, 2048]); st_im = din("st_im", [NB, 2048])
    g_in = din("g_in", [D]); w_in = din("w_in", [D, 3072])
    a_re = din("a_re", [2048]); a_im = din("a_im", [2048]); log_dt = din("log_dt", [32])
    b_re = din("b_re", [2048, 16]); b_im = din("b_im", [2048, 16])
    c_re = din("c_re", [512, 64]); c_im = din("c_im", [512, 64])
    d_skip = din("d_skip", [512]); w_glu = din("w_glu", [512, 1024]); b_glu = din("b_glu", [1024])
    g_ssm = din("g_ssm", [512]); g_attn = din("g_attn", [512]); w_out = din("w_out", [1024, 1024]); g_fin = din("g_fin", [D])

    y_p = dout("y_p", [T, D]); y_s = dout("y_s", [NS, D])
    k_p = dout("k_p", [T, 512]); v_p = dout("v_p", [T, 512])
    sre_p = dout("sre_p", [2048]); sim_p = dout("sim_p", [2048])
    k_s = dout("k_s", [NS, 512]); v_s = dout("v_s", [NS, 512])
    sre_s = dout("sre_s", [NB, 2048]); sim_s = dout("sim_s", [NB, 2048])
    dbg_out = {k: dout("dbg_" + k, shp, BF16 if k in dbg_bf16 else F32) for k, shp in dbg.items()}

    from contextlib import ExitStack
    es = ExitStack()

    def sb(name, shape, dt):
        return es.enter_context(nc.sbuf_tensor(name, list(shape), dt))

    def ps(name, shape, dt):
        return es.enter_context(nc.psum_tensor(name, list(shape), dt))

    class Stop(Exception):
        pass

    with es:
        XN = sb("XN", [128, 8, TA], BF16)
        FA = sb("FA", [128, 4, TA], BF16)
        FB = sb("FB", [128, 4, TA], BF16)
        FC = sb("FC", [128, 4, TA], BF16)
        FD = sb("FD", [128, 4, TA], BF16)
        VB = sb("VB", [128, 17, 544], BF16)
        VA = sb("VA", [128, 2, 16, 128], BF16)
        WST = sb("WST", [128, 4096], F32)
        WB = sb("WB", [128, 8192], BF16)
        STG = sb("STG", [128, 5120], F32)
        MASK = sb("MASK", [128, 19 * 128], BF16)
        IDB = sb("IDB", [128, 128], BF16)
        ONESB = sb("ONESB", [128, 128], BF16)
        SMALL = sb("SMALL", [128, 128], F32)
        SM2 = sb("SM2", [128, 64], F32)
        EPB = sb("EPB", [128, 8, 512], BF16)
        TMPF = sb("TMPF", [128, 4, 512], F32)
        XC = sb("XC", [128, 2, J], BF16)
        LAM = sb("LAM", [128, 40, 16], F32)
        MSK_S = sb("MSK_S", [128, 7, 4, 8], BF16)
        MSK_N = sb("MSK_N", [64, 512], BF16)
        PS = [ps("PS%d" % i, [128, 512], F32) for i in range(7)]
        PSB = ps("PSB", [128, 1024], BF16)

        XST = STG[:, 0:2048].rearrange("p (s d) -> p s d", s=2)
        XSB = STG[:, 2048:3072].bitcast(BF16).rearrange("p (s d) -> p s d", s=2)
        OST = STG[:, 3072:5120].rearrange("p (s d) -> p s d", s=2)
        S5F = STG[:, 0:3072].rearrange("p (k j) -> p k j", k=12)
        PT = STG[:, 3072:5120].bitcast(BF16).rearrange("p (t a j) -> p t a j", t=L, a=2)
        BKT = FD[:].rearrange("p a b -> p (a b)")[:, 0:4 * L * 2 * 128].rearrange("p (f k a c) -> p f k a c", f=4, k=L, a=2)
        CK = VB[:].rearrange("p a b -> p (a b)")[:, 0:16 * (L + 1) * 64].rearrange("p (i v a c) -> p i v a c", i=16, v=L + 1, a=2)
        GFB = MASK[:, 0:2048].bitcast(F32)
        KST = XN[:, 4:8, :].rearrange("p a b -> p (a b)")[:, 0:7168].rearrange("p (s t c) -> p s t c", s=2, t=7)
        KT = WST[:, 0:1792].bitcast(BF16).rearrange("p (c k) -> p c k", c=4)
        QB = TMPF[:, 0:2, :].rearrange("p a b -> p (a b)").bitcast(BF16).rearrange("p (c n) -> p c n", c=4)
        NT = 17
        SQ = lambda k: ("SMALL", k)

        def dma(q, out, in_, reads, writes, **kw):
            return P.add(q, lambda e: e.dma_start(out=out, in_=in_, **kw), reads=reads, writes=writes, dma=True)

        def act(out, in_, func, reads, writes, **kw):
            return P.add("act", lambda e: e.activation(out=out, in_=in_, func=func, **kw), reads=reads, writes=writes)

        def tt(eng, out, in0, in1, op, reads, writes):
            return P.add(eng, lambda e: e.tensor_tensor(out, in0, in1, op), reads=reads, writes=writes)

        def ts(eng, out, in0, s1, s2, op0, op1, reads, writes):
            if op1 is None:
                return P.add(eng, lambda e: e.tensor_single_scalar(out, in0, s1, op0), reads=reads, writes=writes)
            return P.add(eng, lambda e: e.tensor_scalar(out, in0, s1, s2, op0, op1), reads=reads, writes=writes)

        def stt(eng, out, in0, scalar, in1, op0, op1, reads, writes):
            return P.add(eng, lambda e: e.scalar_tensor_tensor(out, in0, scalar, in1, op0, op1), reads=reads, writes=writes)

        def cp(eng, out, in_, reads, writes):
            if eng == "act":
                return P.add("act", lambda e: e.copy(out, in_), reads=reads, writes=writes)
            return P.add(eng, lambda e: e.tensor_copy(out, in_), reads=reads, writes=writes)

        pe_last = [None, None]

        def mm(out, lhsT, rhs, start, stop, reads, writes, **kw):
            tp = kw.get("tile_position")
            extra = []
            if tp is not None and pe_last[0] is not None and tp != pe_last[0]:
                extra = [pe_last[1]]
            op = P.add("pe", lambda e: e.matmul(out, lhsT, rhs, start=start, stop=stop, **kw), reads=reads, writes=writes, extra=extra)
            pe_last[0], pe_last[1] = tp, op
            return op

        def tr(out, in_, ident, reads, writes):
            op = P.add("pe", lambda e: e.transpose(out, in_, ident), reads=reads, writes=writes)
            pe_last[0], pe_last[1] = None, op
            return op

        def mset(eng, ap, val, writes, reads=()):
            return P.add(eng, lambda e: e.memset(ap, val), reads=reads, writes=writes)

        def recip(out, in_, reads, writes):
            return P.add("dve", lambda e: e.reciprocal(out, in_), reads=reads, writes=writes)

        def dump(name, src, reads):
            if name in dbg_out:
                dma("sp", dbg_out[name], src, reads, [("dbg", name)])

        bar_n = [0]

        def barrier():
            n = bar_n[0]
            bar_n[0] += 1
            P.add("pe", lambda e: e.transpose(PSB[:, 896:1024], IDB[:], IDB[:]), reads=[("IDB",)], writes=[("PSB",), ("bar", n, "pe")])
            P.add("act", lambda e: e.copy(SM2[:, 56:57], ONESB[:, 0:1]), reads=[("ONESB",)], writes=[("bar", n, "act"), ("bscr", "act")])
            mset("dve", SM2[:, 58:59], 0.0, [("bar", n, "dve"), ("bscr", "dve")])
            mset("pool", SM2[:, 59:60], 0.0, [("bar", n, "pool"), ("bscr", "pool")])
            rk = [("bar", n, e_) for e_ in ("pe", "act", "dve", "pool")]
            dm = [d for d in P.dma_last if d is not None]
            P.add("pe", lambda e: e.transpose(PSB[:, 896:1024], IDB[:], IDB[:]), reads=rk + [("IDB",)], writes=[("PSB",)], extra=dm)
            P.add("act", lambda e: e.copy(SM2[:, 56:57], ONESB[:, 0:1]), reads=rk + [("ONESB",)], writes=[("bscr", "act")], extra=dm)
            mset("dve", SM2[:, 58:59], 0.0, [("bscr", "dve")], reads=rk)
            P.ops["dve"][-1].deps.extend(dm)
            mset("pool", SM2[:, 59:60], 0.0, [("bscr", "pool")], reads=rk)
            P.ops["pool"][-1].deps.extend(dm)
            P.add("sp", lambda e: e.nop(), reads=rk, extra=dm)
            for d in dm:
                d.need_inc = True

        C1, C2, PILO = 6.28125, 2 * PI - 6.28125, 3.1415925

        def sincos(dst_sin, dst_cos, src, rk, wk_s, wk_c, tmp, tk, ki, kk):
            ts("dve", ki, src, 1.0 / (2 * PI), None, ALU.mult, None, rk, [kk])
            stt("dve", tmp, ki, -C1, src, ALU.mult, ALU.add, [kk] + rk, [tk])
            stt("dve", tmp, ki, -C2, tmp, ALU.mult, ALU.add, [kk, tk], [tk])
            ts("dve", tmp, tmp, -PILO, PILO, ALU.max, ALU.min, [tk], [tk])
            act(dst_sin, tmp, AF.Sin, [tk], wk_s)
            ts("dve", ki, src, 1.0 / (2 * PI), 0.25, ALU.mult, ALU.add, rk, [kk])
            stt("dve", tmp, ki, -C1, src, ALU.mult, ALU.add, [kk] + rk, [tk])
            stt("dve", tmp, ki, -C2, tmp, ALU.mult, ALU.add, [kk, tk], [tk])
            ts("dve", tmp, tmp, PI / 2, PILO, ALU.add, ALU.min, [tk], [tk])
            act(dst_cos, tmp, AF.Sin, [tk], wk_c)

        def cmul(eng, ore, oim, are, aim, bre, bim, t1, t2, rk, wre, wim, tk1, tk2):
            tt(eng, t1, are, bre, ALU.mult, rk, [tk1])
            tt(eng, t2, aim, bim, ALU.mult, rk, [tk2])
            tt(eng, ore, t1, t2, ALU.subtract, [tk1, tk2], wre)
            tt(eng, t1, are, bim, ALU.mult, rk, [tk1])
            tt(eng, t2, aim, bre, ALU.mult, rk, [tk2])
            tt(eng, oim, t1, t2, ALU.add, [tk1, tk2], wim)

        def mult_mask(di, ti, t1, t2, out, kd, ki_, k1, k2, kout, eng="dve"):
            ts(eng, ti, di, 3, None, ALU.bitwise_and, None, [kd], [ki_])
            ts(eng, t1, ti, 0, None, ALU.is_equal, None, [ki_], [k1])
            ts(eng, t2, di, 512, None, ALU.is_le, None, [kd], [k2])
            tt(eng, t1, t1, t2, ALU.mult, [k1, k2], [k1])
            ts(eng, t2, di, 128, None, ALU.is_le, None, [kd], [k2])
            tt(eng, t1, t1, t2, ALU.add, [k1, k2], [k1])
            ts(eng, ti, di, 15, None, ALU.bitwise_and, None, [kd], [ki_])
            ts(eng, t2, ti, 0, None, ALU.is_equal, None, [ki_], [k2])
            tt(eng, t1, t1, t2, ALU.add, [k1, k2], [k1])
            ts(eng, t2, di, 0, None, ALU.is_ge, None, [kd], [k2])
            tt(eng, out, t1, t2, ALU.mult, [k1, k2], kout)

        try:
            mset("pool", ONESB[:], 1.0, [("ONESB",)])
            IIv = FD[:].rearrange("p a b -> p (a b)").bitcast(I32)
            MFv = [F_[:].rearrange("p a b -> p (a b)").bitcast(F32) for F_ in (FA, FB, FC)]
            P.add("pool", lambda e: e.iota(IIv[:, 0:128], [[1, 128]], base=0, channel_multiplier=-1), writes=[("II",)])
            ts("dve", IDB[:], IIv[:, 0:128], 0, None, ALU.is_equal, None, [("II",)], [("IDB",)])
            P.add("pool", lambda e: e.iota(IIv[:, 0:2432], [[1, 2432]], base=-384, channel_multiplier=-1), reads=[("IDB",)], writes=[("II",)])
            TIv = FA[:].rearrange("p a b -> p (a b)").bitcast(I32)
            mult_mask(IIv[:, 0:2432], TIv[:, 0:2432], MFv[1][:, 0:2432], MFv[2][:, 0:2432], MASK[:], ("II",), ("MF", 0), ("MF", 1), ("MF", 2), [("MASK",)])
            dump("mask", MASK[:], [("MASK",)])
            P.add("pool", lambda e: e.iota(IIv[:, 64:65], [[0, 1]], base=0, channel_multiplier=1), reads=[("MASK",)], writes=[("IIp",)])
            ts("dve", IIv[:, 65:66], IIv[:, 64:65], 3, None, ALU.bitwise_and, None, [("IIp",)], [("IIp3",)])
            ts("dve", IIv[:, 65:66], IIv[:, 65:66], 3, None, ALU.mult, None, [("IIp3",)], [("IIp3",)])
            P.add("pool", lambda e: e.iota(IIv[:, 0:12], [[-512, 3], [1, 4]], base=2048, channel_multiplier=-4), reads=[("MASK",)], writes=[("II",)])
            P.add("pool", lambda e: e.iota(IIv[:, 12:28], [[-128, 4], [1, 4]], base=512, channel_multiplier=-1), reads=[("MASK",)], writes=[("II",)])
            ts("dve", IIv[:, 0:12], IIv[:, 0:12], IIv[:, 65:66], None, ALU.add, None, [("II",), ("IIp3",)], [("II",)])
            mult_mask(IIv[:, 0:28], TIv[:, 0:28], MFv[1][:, 0:28], MFv[2][:, 0:28], MFv[1][:, 32:60], ("II",), ("MF", 0), ("MF", 1), ("MF", 2), [("MF", 3)])
            cp("dve", MSK_S[:], MFv[1][:, 32:60].rearrange("p (t j) -> p t j", t=7).unsqueeze(3).to_broadcast([128, 7, 4, 8]), [("MF", 3)], [("MSK_S",)])
            P.add("pool", lambda e: e.iota(IIv[0:64, 0:512], [[1, 512]], base=0, channel_multiplier=0), reads=[("MF", 3), ("MSK_S",)], writes=[("II",)])
            P.add("pool", lambda e: e.iota(IIv[0:64, 512:1024], [[0, 512]], base=0, channel_multiplier=1), reads=[("MF", 3), ("MSK_S",)], writes=[("II2",)])
            ci, pi_ = IIv[0:64, 0:512], IIv[0:64, 512:1024]
            cb, cj, pb_, pj = (IIv[0:64, 1024 + 512 * k_:1536 + 512 * k_] for k_ in range(4))
            ts("dve", cb, ci, 5, None, ALU.arith_shift_right, None, [("II",)], [("I", 0)])
            ts("dve", cj, ci, 3, 3, ALU.arith_shift_right, ALU.bitwise_and, [("II",)], [("I", 1)])
            ts("dve", pb_, pi_, 2, None, ALU.arith_shift_right, None, [("II2",)], [("I", 2)])
            ts("dve", pj, pi_, 3, None, ALU.bitwise_and, None, [("II2",)], [("I", 3)])
            m0, m1 = MFv[1][0:64, 0:512], MFv[2][0:64, 0:512]
            tt("dve", m0, cb, pb_, ALU.is_equal, [("I", 0), ("I", 2), ("MF", 3), ("MSK_S",)], [("MF", 1)])
            tt("dve", m1, pj, cj, ALU.is_le, [("I", 1), ("I", 3)], [("MF", 2)])
            tt("dve", m0, m0, m1, ALU.mult, [("MF", 1), ("MF", 2)], [("MF", 1)])
            tt("dve", m1, pj, cj, ALU.is_equal, [("I", 1), ("I", 3)], [("MF", 2)])
            ts("dve", m1, m1, 2.0, 1.0, ALU.mult, ALU.add, [("MF", 2)], [("MF", 2)])
            tt("dve", MSK_N[:], m0, m1, ALU.mult, [("MF", 1), ("MF", 2)], [("MSK_N",)])
            dump("msk_s", MSK_S[:].rearrange("p t j h -> p (t j h)"), [("MSK_S",)])
            dump("msk_n", MSK_N[:], [("MSK_N",)])
            dma("sp", SMALL[:, 0:8], g_in.rearrange("(k p) -> p k", p=128), [], [SQ("gin")], allow_slow_non_contiguous=True)
            dma("sp", SMALL[:, 8:12], d_skip.rearrange("(k p) -> p k", p=128), [], [SQ("dsk")], allow_slow_non_contiguous=True)
            dma("sp", SMALL[:, 12:20], b_glu.rearrange("(k p) -> p k", p=128), [], [SQ("bglu")], allow_slow_non_contiguous=True)
            dma("sp", SMALL[:, 20:24], g_ssm.rearrange("(k p) -> p k", p=128), [], [SQ("gssm")], allow_slow_non_contiguous=True)
            dma("sp", SMALL[:, 24:28], g_attn.rearrange("(k p) -> p k", p=128), [], [SQ("gattn")], allow_slow_non_contiguous=True)
            barrier()
            if stop_after == "p0":
                raise Stop()

            TRB = [PSB, PS[6][:].bitcast(BF16)]
            TRK = [("PSB",), ("PS", 6)]
            rows_of = lambda t_: 128 if t_ < 16 else NS
            x_src = lambda t_: xp[t_ * 128:(t_ + 1) * 128, :] if t_ < 16 else xsm[:, :]
            def p2_gen():
                for t_ in range(NT):
                    r = rows_of(t_)
                    s = t_ % 2
                    dma("sp", XST[:r, s, :], x_src(t_), [], [("XST", s)])
                    act(OST[:r, s, :], XST[:r, s, :], AF.Square, [("XST", s)], [("OST", s), SQ(("ss", t_))],
                        accum_out=SMALL[:r, 32 + t_:33 + t_])
                    ts("dve", SMALL[:r, 49 + t_:50 + t_], SMALL[:r, 32 + t_:33 + t_], 1.0 / D, EPS, ALU.mult, ALU.add,
                       [SQ(("ss", t_))], [SQ(("rs", t_))])
                    act(SMALL[:r, 49 + t_:50 + t_], SMALL[:r, 49 + t_:50 + t_], AF.Sqrt, [SQ(("rs", t_))], [SQ(("rs", t_))])
                    recip(SMALL[:r, 49 + t_:50 + t_], SMALL[:r, 49 + t_:50 + t_], [SQ(("rs", t_))], [SQ(("rs", t_))])
                    act(XSB[:r, s, :], XST[:r, s, :], AF.Copy, [("XST", s), SQ(("rs", t_))], [("XSB", s)],
                        scale=SMALL[:r, 49 + t_:50 + t_])
                    for kc in range(8):
                        tr(TRB[s][:, kc * 128:kc * 128 + r], XSB[:r, s, kc * 128:(kc + 1) * 128], IDB[:r, :r], [("XSB", s), ("IDB",)], [TRK[s]])
                    cp("dve" if t_ % 2 == 0 else "act", XN[:, :, t_ * 128:t_ * 128 + r],
                       TRB[s][:, :].rearrange("p (k t) -> p k t", k=8)[:, :, 0:r],
                       [TRK[s]], [("XN", kc, t_) for kc in range(8)])
                    yield
            p2 = p2_gen()

            def p2_step():
                try:
                    next(p2)
                except StopIteration:
                    pass

            LQ = lambda k: ("LAM", k)
            mset("pool", LAM[:], 0.0, [LQ(k) for k in range(40)] + [LQ((2, 0)), LQ((2, 1))])
            dma("sp", LAM[:, 0, :], a_re.rearrange("(i p) -> p i", p=128), [], [LQ(0)], allow_slow_non_contiguous=True)
            dma("sp", LAM[:, 1, :], a_im.rearrange("(i p) -> p i", p=128), [], [LQ(1)], allow_slow_non_contiguous=True)
            for gl in range(2):
                dma("sp", LAM[gl * 64:(gl + 1) * 64, 2, :], bass.AP(log_dt.tensor, gl, [[0, 64], [2, 16]]), [], [LQ((2, gl))],
                    allow_slow_non_contiguous=True)
            act(LAM[:, 2, :], LAM[:, 2, :], AF.Exp, [LQ((2, 0)), LQ((2, 1))], [LQ(2)])
            tt("dve", LAM[:, 3, :], LAM[:, 0, :], LAM[:, 2, :], ALU.mult, [LQ(0), LQ(2)], [LQ(3)])
            act(LAM[:, 4, :], LAM[:, 3, :], AF.Exp, [LQ(3)], [LQ(4)])
            act(LAM[:, 14, :], LAM[:, 3, :], AF.Exp, [LQ(3)], [LQ(14)], scale=float(L))
            tt("dve", LAM[:, 5, :], LAM[:, 1, :], LAM[:, 2, :], ALU.mult, [LQ(1), LQ(2)], [LQ(5)])
            ts("dve", LAM[:, 15, :], LAM[:, 5, :], float(L), None, ALU.mult, None, [LQ(5)], [LQ(15)])
            sincos(LAM[:, 8, :], LAM[:, 7, :], LAM[:, 5, :], [LQ(5)], [LQ(8)], [LQ(7)], LAM[:, 6, :], LQ(6),
                   LAM[:, 39, :].bitcast(I32), LQ(39))
            tt("dve", LAM[:, 9, :], LAM[:, 4, :], LAM[:, 7, :], ALU.mult, [LQ(4), LQ(7)], [LQ(9)])
            tt("dve", LAM[:, 10, :], LAM[:, 4, :], LAM[:, 8, :], ALU.mult, [LQ(4), LQ(8)], [LQ(10)])
            tt("dve", LAM[:, 11, :], LAM[:, 0, :], LAM[:, 0, :], ALU.mult, [LQ(0)], [LQ(11)])
            tt("dve", LAM[:, 6, :], LAM[:, 1, :], LAM[:, 1, :], ALU.mult, [LQ(1)], [LQ(6)])
            tt("dve", LAM[:, 11, :], LAM[:, 11, :], LAM[:, 6, :], ALU.add, [LQ(11), LQ(6)], [LQ(11)])
            recip(LAM[:, 11, :], LAM[:, 11, :], [LQ(11)], [LQ(11)])
            ts("dve", LAM[:, 6, :], LAM[:, 9, :], -1.0, None, ALU.add, None, [LQ(9)], [LQ(6)])
            tt("dve", LAM[:, 12, :], LAM[:, 6, :], LAM[:, 0, :], ALU.mult, [LQ(6), LQ(0)], [LQ(12)])
            tt("dve", LAM[:, 38, :], LAM[:, 10, :], LAM[:, 1, :], ALU.mult, [LQ(10), LQ(1)], [LQ(38)])
            tt("dve", LAM[:, 12, :], LAM[:, 12, :], LAM[:, 38, :], ALU.add, [LQ(12), LQ(38)], [LQ(12)])
            tt("dve", LAM[:, 12, :], LAM[:, 12, :], LAM[:, 11, :], ALU.mult, [LQ(12), LQ(11)], [LQ(12)])
            tt("dve", LAM[:, 13, :], LAM[:, 10, :], LAM[:, 0, :], ALU.mult, [LQ(10), LQ(0)], [LQ(13)])
            tt("dve", LAM[:, 38, :], LAM[:, 6, :], LAM[:, 1, :], ALU.mult, [LQ(6), LQ(1)], [LQ(38)])
            tt("dve", LAM[:, 13, :], LAM[:, 13, :], LAM[:, 38, :], ALU.subtract, [LQ(13), LQ(38)], [LQ(13)])
            tt("dve", LAM[:, 13, :], LAM[:, 13, :], LAM[:, 11, :], ALU.mult, [LQ(13), LQ(11)], [LQ(13)])
            mset("dve", LAM[:, 16, :], 1.0, [LQ(16)])
            mset("dve", LAM[:, 17, :], 0.0, [LQ(17)])
            for k in range(1, L + 1):
                cmul("dve", LAM[:, 16 + 2 * k, :], LAM[:, 17 + 2 * k, :], LAM[:, 14 + 2 * k, :], LAM[:, 15 + 2 * k, :],
                     LAM[:, 9, :], LAM[:, 10, :], LAM[:, 38, :], LAM[:, 39, :],
                     [LQ(14 + 2 * k), LQ(15 + 2 * k), LQ(9), LQ(10)], [LQ(16 + 2 * k)], [LQ(17 + 2 * k)], LQ(38), LQ(39))
                p2_step()
            dump("lam", LAM[:, :, :], [LQ(k) for k in range(40)])
            if stop_after == "lam":
                raise Stop()
            BNf = FC[:].rearrange("p a b -> p (a b)").bitcast(F32)
            BPf = FB[:].rearrange("p a b -> p (a b)")
            EPF1 = EPB[:].rearrange("p a b -> p (a b)").bitcast(F32)
            Braw = EPF1[:, 0:512].rearrange("p (a i c) -> p a i c", a=2, i=16)
            Btmp = EPF1[:, 512:1024].rearrange("p (a i c) -> p a i c", a=2, i=16)
            Bk = BNf[:, 0:L * 512].rearrange("p (k a i c) -> p k a i c", k=L, a=2, i=16)
            Bpad = BPf[:, 0:L * 1024].rearrange("p (k a f r c) -> p k a f r c", k=L, a=2, f=4, r=4)
            dma("sp", Braw[:, 0], b_re.rearrange("(i p) c -> p i c", p=128), [], [("Braw", 0)])
            dma("sp", Braw[:, 1], b_im.rearrange("(i p) c -> p i c", p=128), [], [("Braw", 1)])
            bc = lambda row: LAM[:, row, :].unsqueeze(2).to_broadcast([128, 16, 16])
            cmul("dve", Bk[:, 0, 0], Bk[:, 0, 1], Braw[:, 0], Braw[:, 1], bc(12), bc(13), Btmp[:, 0], Btmp[:, 1],
                 [("Braw", 0), ("Braw", 1), LQ(12), LQ(13)], [("Bk", 0, 0)], [("Bk", 0, 1)], ("Btmp", 0), ("Btmp", 1))
            for k in range(1, L):
                cmul("dve", Bk[:, k, 0], Bk[:, k, 1], Bk[:, k - 1, 0], Bk[:, k - 1, 1], bc(9), bc(10), Btmp[:, 0], Btmp[:, 1],
                     [("Bk", k - 1, 0), ("Bk", k - 1, 1), LQ(9), LQ(10)], [("Bk", k, 0)], [("Bk", k, 1)], ("Btmp", 0), ("Btmp", 1))
                p2_step()
            mset("pool", BPf[:, 0:L * 1024], 0.0, [("Bpad",)])
            for gl in range(2):
                for k in range(L):
                    cp("pool", Bpad[gl * 64:(gl + 1) * 64, k, :, :, :, gl * 16:(gl + 1) * 16].rearrange("p a f r c -> p a (f r) c"),
                       Bk[gl * 64:(gl + 1) * 64, k], [("Bk", k, 0), ("Bk", k, 1), ("Bpad",)], [("Bpad", gl, k)])
            gi_ = 0
            for f in range(4):
                for kk in range(0, L, 4):
                    tb, tk_ = TRB[gi_ % 2], TRK[gi_ % 2]
                    gi_ += 1
                    for k in range(kk, kk + 4):
                        for ri in range(2):
                            col = (k - kk) * 2 + ri
                            tr(tb[:, col * 128:(col + 1) * 128], Bpad[:, k, ri, f].rearrange("p r c -> p (r c)"), IDB[:],
                               [("Bpad", 0, k), ("Bpad", 1, k), ("IDB",)], [tk_])
                    cp("act", BKT[:, f, kk:kk + 4, :, :].rearrange("p k a c -> p (k a c)"), tb[:, :], [tk_],
                       [("BKT", f, k) for k in range(kk, kk + 4)])
            if stop_after == "B":
                dump("bkt", BKT.rearrange("p f k a c -> p (f k a c)"), [("BKT", f, k) for f in range(4) for k in range(L)])
                raise Stop()
            CN = WST[:, 0:1024].rearrange("p (a t c) -> p a t c", a=2, t=4)
            CNB = WB[:, 0:1024].rearrange("p (a t c) -> p a t c", a=2, t=4)
            CT = TMPF[:, 0:2, :].rearrange("p a (t c) -> p a t c", t=4)
            mset("pool", WST[:, 0:1024], 0.0, [("CN",)])
            for ri, csrc in enumerate((c_re, c_im)):
                cv4 = csrc.rearrange("(t r l c) n -> t r l c n", t=4, r=4, l=2)
                for r in range(4):
                    for gl in range(2):
                        p0 = r * 32 + gl * 16
                        dma("sp", CN[p0:p0 + 16, ri, :, gl * 64:(gl + 1) * 64], cv4[:, r, gl].rearrange("t c n -> c t n"),
                            [("CN",)], [("CNd", ri, r, gl)])
            cp("dve", CNB[:], CN[:], [("CN",)] + [("CNd", ri, r, gl) for ri in range(2) for r in range(4) for gl in range(2)], [("CNB",)])
            for ri in range(2):
                for rt in range(4):
                    col = ri * 4 + rt
                    tr(PSB[:, col * 128:(col + 1) * 128], CNB[:, ri, rt, :], IDB[:], [("CNB",), ("IDB",)], [("PSB",)])
            cp("dve", CT.rearrange("p a t c -> p (a t c)"), PSB[:, :], [("PSB",)], [("CT",)])
            CTK = [("CT",)]
            CTv = lambda ri: CT[:, ri].rearrange("p t (r c) -> p (t r) c", r=4)
            bc32 = lambda row: LAM[:, row, :].unsqueeze(2).to_broadcast([128, 16, 32])
            T1 = TMPF[:, 2, :].rearrange("p (i c) -> p i c", i=16)
            T2 = TMPF[:, 3, :].rearrange("p (i c) -> p i c", i=16)
            for var in range(L + 1):
                lr, li = 16 + 2 * var, 17 + 2 * var
                tt("dve", T1, CTv(0), bc32(lr), ALU.mult, CTK + [LQ(lr)], [("T1",)])
                tt("dve", T2, CTv(1), bc32(li), ALU.mult, CTK + [LQ(li)], [("T2",)])
                tt("dve", CK[:, :, var, 0, :], T1, T2, ALU.subtract, [("T1",), ("T2",)], [("CK", var, 0)])
                tt("dve", T1, CTv(0), bc32(li), ALU.mult, CTK + [LQ(li)], [("T1",)])
                tt("dve", T2, CTv(1), bc32(lr), ALU.mult, CTK + [LQ(lr)], [("T2",)])
                stt("dve", CK[:, :, var, 1, :], T1, -1.0, T2, ALU.mult, ALU.subtract, [("T1",), ("T2",)], [("CK", var, 1)])
                p2_step()
            dump("bkt", BKT.rearrange("p f k a c -> p (f k a c)"), [("BKT", f, k) for f in range(4) for k in range(L)])
            dump("ck", CK.rearrange("p i v a c -> p (i v a c)"), [("CK", v, a) for v in range(L + 1) for a in range(2)])
            for _ in p2:
                pass
            barrier()
            if stop_after == "params":
                raise Stop()

            dump("xn", XN[:, :, :], [("XN", kc, t_) for kc in range(8) for t_ in range(NT)])

            TBK = [(0, 512), (512, 512), (1024, 512), (1536, 512), (2048, NS)]
            tiles_of_block = lambda bi: list(range(4 * bi, 4 * bi + 4)) if bi < 4 else [16]
            psrr = [0]

            def next_ps(n=3):
                i = psrr[0] % n
                psrr[0] += 1
                return i
            WBv = WB[:, :].rearrange("p (s k n) -> p s k n", s=2, k=8)

            def load_w_group(gi, slot):
                dma("sp", WST[:, :].rearrange("p (k n) -> p k n", k=8),
                    w_in[:, gi * 512:(gi + 1) * 512].rearrange("(k p) n -> p k n", p=128), [], [("WST",)])
                for kc in range(8):
                    if kc % 2 == 0:
                        act(WBv[:, slot, kc, :], WST[:, kc * 512:(kc + 1) * 512], AF.Copy, [("WST",), SQ("gin")], [("WB", slot, kc)],
                            scale=SMALL[:, kc:kc + 1])
                    else:
                        ts("dve", WBv[:, slot, kc, :], WST[:, kc * 512:(kc + 1) * 512], SMALL[:, kc:kc + 1], None, ALU.mult, None,
                           [("WST",), SQ("gin")], [("WB", slot, kc)])

            def fm_keys(name, m, bi):
                return [(name, m, t_) for t_ in tiles_of_block(bi)]

            def proj_fm(slot, evac):
                for m in range(4):
                    for bi, (c0, n) in enumerate(TBK):
                        pi = next_ps()
                        for kc in range(8):
                            mm(PS[pi][:, 0:n], WBv[:, slot, kc, m * 128:(m + 1) * 128], XN[:, kc, c0:c0 + n], kc == 0, kc == 7,
                               [("WB", slot, kc)] + [("XN", kc, t_) for t_ in tiles_of_block(bi)], [("PS", pi)])
                        evac(m, bi, pi, c0, n)

            load_w_group(0, 0)
            load_w_group(1, 1)
            proj_fm(0, lambda m, bi, pi, c0, n: cp("dve", FA[:, m, c0:c0 + n], PS[pi][:, 0:n], [("PS", pi)], fm_keys("FA", m, bi)))
            proj_fm(1, lambda m, bi, pi, c0, n: act(FB[:, m, c0:c0 + n], PS[pi][:, 0:n], AF.Silu, [("PS", pi)], fm_keys("FB", m, bi)))
            dump("xs", FA[:, :, :], [("FA", m, t_) for m in range(4) for t_ in range(NT)])
            dump("zs", FB[:, :, :], [("FB", m, t_) for m in range(4) for t_ in range(NT)])
            barrier()
            if stop_after == "inproj1":
                raise Stop()

            YB = [PS[3], PS[4], PS[5], PS[6]]
            ybank = lambda t: (YB[t // 2], (t % 2) * J)
            xsv = lambda f: FA[:, f, 0:T].rearrange("p (j s) -> p j s", s=L)
            S5K = lambda k: ("S5F", k)
            FAK = lambda f: [("FA", f, t_) for t_ in range(16)]

            def gelu_to(yv, ykeys, dst, dkeys, s, n):
                x2 = TMPF[:, 1, s * 256:s * 256 + n]
                u = TMPF[:, 2, s * 256:s * 256 + n]
                k2, ku = ("TMPF", 1, s), ("TMPF", 2, s)
                act(x2, yv, AF.Square, ykeys, [k2])
                ts("pool", x2, x2, 0.044715, 1.0, ALU.mult, ALU.add, [k2], [k2])
                tt("pool", u, x2, yv, ALU.mult, [k2] + ykeys, [ku])
                act(u, u, AF.Sigmoid, [ku], [ku], scale=1.5957691216)
                tt("dve", dst, u, yv, ALU.mult, [ku] + ykeys, dkeys)

            JI = TMPF[:, 3, 0:J].bitcast(I32)
            P.add("pool", lambda e: e.iota(JI, [[1, J]], base=1, channel_multiplier=0), writes=[("JI",)])
            cp("dve", S5F[:, 11, :], JI, [("JI",)], [S5K(11)])
            XSD = WB[:, :].rearrange("p (f s j) -> p f s j", f=4, s=L)
            for f in range(4):
                cp("act" if f % 2 == 0 else "dve", XSD[:, f], xsv(f).rearrange("p j s -> p s j"), FAK(f), [("XSD", f)])
            PTB = [PT, WST[:, 0:2048].bitcast(BF16).rearrange("p (t a j) -> p t a j", t=L, a=2)]
            XCB = [XC, WST[:, 2048:2304].bitcast(BF16).rearrange("p (a j) -> p a j", a=2)]
            SBF = [S5F[:, 4:6, :], WST[:, 2304:2816].rearrange("p (a j) -> p a j", a=2)]

            def a_step(i):
                f, r = i // 4, i % 4
                pb = i % 2
                for t in range(L):
                    pi = next_ps()
                    for ri in range(2):
                        for s in range(t + 1):
                            mm(PS[pi][:, ri * J:(ri + 1) * J], BKT[32 * r:32 * r + 32, f, t - s, ri, :], XSD[32 * r:32 * r + 32, f, s, :],
                               s == 0, s == t, [("BKT", f, t - s), ("XSD", f)], [("PS", pi)], tile_position=(32 * r, 0))
                    cp("act", PTB[pb][:, t].rearrange("p a j -> p (a j)"), PS[pi][:, 0:2 * J], [("PS", pi)], [("PT", pb, t)])
                    if t == L - 1:
                        cp("act", SBF[pb].rearrange("p a j -> p (a j)"), PS[pi][:, 0:2 * J], [("PS", pi)], [("SBF", pb, 0), ("SBF", pb, 1)])

            EPBF = EPB[:].rearrange("p a b -> p (a b)").bitcast(F32).rearrange("p (k j) -> p k j", j=J)
            VAI = VA[:].rearrange("p a b c -> p (a b c)").bitcast(I32)[:, 0:J]
            TABK = lambda pb, w: ("TAB", pb, w)

            def sincos_ops(i):
                pb = i % 2
                cosT, sinT = EPBF[:, 2 * pb, :], EPBF[:, 2 * pb + 1, :]
                arg, tmp, ki = EPBF[:, 4, :], EPBF[:, 5, :], VAI
                ka, kt, kk = ("SC", "arg"), ("SC", "tmp"), ("SC", "ki")
                return [
                    lambda: ts("dve", arg, S5F[:, 11, :], LAM[:, 15, i:i + 1], None, ALU.mult, None, [S5K(11), LQ(15)], [ka]),
                    lambda: ts("dve", ki, arg, 1.0 / (2 * PI), None, ALU.mult, None, [ka], [kk]),
                    lambda: stt("dve", tmp, ki, -C1, arg, ALU.mult, ALU.add, [kk, ka], [kt]),
                    lambda: stt("dve", tmp, ki, -C2, tmp, ALU.mult, ALU.add, [kk, kt], [kt]),
                    lambda: ts("dve", tmp, tmp, -PILO, PILO, ALU.max, ALU.min, [kt], [kt]),
                    lambda: act(sinT, tmp, AF.Sin, [kt], [TABK(pb, 1)]),
                    lambda: ts("dve", ki, arg, 1.0 / (2 * PI), 0.25, ALU.mult, ALU.add, [ka], [kk]),
                    lambda: stt("dve", tmp, ki, -C1, arg, ALU.mult, ALU.add, [kk, ka], [kt]),
                    lambda: stt("dve", tmp, ki, -C2, tmp, ALU.mult, ALU.add, [kk, kt], [kt]),
                    lambda: ts("dve", tmp, tmp, PI / 2, PILO, ALU.add, ALU.min, [kt], [kt]),
                    lambda: act(cosT, tmp, AF.Sin, [kt], [TABK(pb, 0)]),
                ]

            def rot_ops(i):
                pb = i % 2
                cosT, sinT = EPBF[:, 2 * pb, :], EPBF[:, 2 * pb + 1, :]
                kc_, ks_ = TABK(pb, 0), TABK(pb, 1)
                Sre, Sim = SBF[pb][:, 0, :], SBF[pb][:, 1, :]
                kre, kim = ("SBF", pb, 0), ("SBF", pb, 1)
                a, b_, cre, cim, wre, wim = (S5F[:, k, :] for k in (6, 7, 8, 9, 10, 3))
                r8 = LAM[:, 14, i:i + 1].to_broadcast([128, J])
                return [
                    lambda: tt("dve", a, cosT, Sre, ALU.mult, [kc_, kre], [S5K(6)]),
                    lambda: tt("dve", b_, sinT, Sim, ALU.mult, [ks_, kim], [S5K(7)]),
                    lambda: tt("dve", cre, a, b_, ALU.add, [S5K(6), S5K(7)], [S5K(8)]),
                    lambda: tt("dve", a, cosT, Sim, ALU.mult, [kc_, kim], [S5K(6)]),
                    lambda: tt("dve", b_, sinT, Sre, ALU.mult, [ks_, kre], [S5K(7)]),
                    lambda: tt("dve", cim, a, b_, ALU.subtract, [S5K(6), S5K(7)], [S5K(9)]),
                    lambda: P.add("dve", lambda e: e.tensor_tensor_scan(wre, r8, cre, 0.0, ALU.mult, ALU.add), reads=[S5K(8), LQ(14)], writes=[S5K(10)]),
                    lambda: P.add("dve", lambda e: e.tensor_tensor_scan(wim, r8, cim, 0.0, ALU.mult, ALU.add), reads=[S5K(9), LQ(14)], writes=[S5K(3)]),
                    lambda: tt("dve", a, cosT, wre, ALU.mult, [kc_, S5K(10)], [S5K(6)]),
                    lambda: tt("dve", b_, sinT, wim, ALU.mult, [ks_, S5K(3)], [S5K(7)]),
                    lambda: tt("dve", cre, a, b_, ALU.subtract, [S5K(6), S5K(7)], [S5K(8)]),
                    lambda: tt("dve", a, sinT, wre, ALU.mult, [ks_, S5K(10)], [S5K(6)]),
                    lambda: tt("dve", b_, cosT, wim, ALU.mult, [kc_, S5K(3)], [S5K(7)]),
                    lambda: tt("dve", cim, a, b_, ALU.add, [S5K(6), S5K(7)], [S5K(9)]),
                    lambda: cp("act", SM2[:, i:i + 1], cre[:, J - 1:J], [S5K(8)], [("SM2", i)]),
                    lambda: cp("act", SM2[:, 16 + i:17 + i], cim[:, J - 1:J], [S5K(9)], [("SM2", 16 + i)]),
                    lambda: mset("pool", XCB[pb][:, :, 0:1], 0.0, [("XC", pb, "z")]),
                    lambda: cp("act", XCB[pb][:, 0, 1:J], cre[:, 0:J - 1], [S5K(8)], [("XC", pb, 0)]),
                    lambda: cp("act", XCB[pb][:, 1, 1:J], cim[:, 0:J - 1], [S5K(9)], [("XC", pb, 1)]),
                ]

            def chain(i):
                la = rot_ops(i)
                lb = sincos_ops(i + 1) if i + 1 < 16 else []
                for k in range(max(len(la), len(lb))):
                    if k < len(la):
                        la[k]()
                    if k < len(lb):
                        lb[k]()

            def d_step(i):
                f, r = i // 4, i % 4
                pb = i % 2
                for t in range(L):
                    bank, co = ybank(t)
                    for ri in range(2):
                        mm(bank[32 * r:32 * r + 32, co:co + J], CK[:, i, 0, ri, :], PTB[pb][:, t, ri, :], ri == 0, False,
                           [("CK", 0, ri), ("PT", pb, t)], [("YB", t // 2, r)], tile_position=(0, 32 * r))
                    for ri in range(2):
                        mm(bank[32 * r:32 * r + 32, co:co + J], CK[:, i, t + 1, ri, :], XCB[pb][:, ri, :], False, ri == 1,
                           [("CK", t + 1, ri), ("XC", pb, 0), ("XC", pb, 1), ("XC", pb, "z")], [("YB", t // 2, r)], tile_position=(0, 32 * r))
                if r == 3:
                    for t in range(L):
                        bank, co = ybank(t)
                        s = t % 2
                        yk = ("TMPF", 0, s)
                        yv = TMPF[:, 0, s * 256:s * 256 + J]
                        stt("dve", yv, xsv(f)[:, :, t], SMALL[:, 8 + f:9 + f], bank[:, co:co + J], ALU.mult, ALU.add,
                            FAK(f) + [SQ("dsk")] + [("YB", t // 2, r_) for r_ in range(4)], [yk])
                        gelu_to(yv, [yk], FC[:, f, 0:T].rearrange("p (j s) -> p j s", s=L)[:, :, t], [("FC", f, t_) for t_ in range(16)], s, J)

            for th in sincos_ops(0):
                th()
            a_step(0)
            for i in range(16):
                if i + 1 < 16:
                    a_step(i + 1)
                chain(i)
                d_step(i)
            dma("sp", sre_p.rearrange("(i p) -> p i", p=128), SM2[:, 0:16], [("SM2", i) for i in range(16)], [("out", "sre_p")],
                allow_slow_non_contiguous=True)
            dma("sp", sim_p.rearrange("(i p) -> p i", p=128), SM2[:, 16:32], [("SM2", 16 + i) for i in range(16)], [("out", "sim_p")],
                allow_slow_non_contiguous=True)
            barrier()
            if stop_after == "s5p":
                if "gel" in dbg_out:
                    dma("sp", dbg_out["gel"][:, :, 0:T], FC[:, :, 0:T], [], [("dbg", "gel")])
                raise Stop()
            SBU = STG[:, 3072:5120].rearrange("p (r a f n) -> p r a f n", r=4, a=2, f=4)
            SA = S5F[:, 0:2, :]
            SBs = S5F[:, 2:4, :]
            SXS = S5F[:, 8:12, :].rearrange("p a j -> p (a j)").bitcast(BF16).rearrange("p (a i n) -> p a i n", a=2, i=16)
            for r in range(4):
                for ri in range(2):
                    for f in range(4):
                        bk = 3 + r
                        q_ = ri * 4 + f
                        mm(PS[bk][:, q_ * 64:q_ * 64 + 64], BKT[32 * r:32 * r + 32, f, 0, ri, :], FA[32 * r:32 * r + 32, f, T:TA],
                           True, True, [("BKT", f, 0), ("FA", f, 16)], [("PS", bk)], tile_position=(32 * r, 0))
            for bk in range(4):
                cp("act" if bk % 2 == 0 else "dve", STG[:, 3072 + bk * 512:3072 + (bk + 1) * 512], PS[3 + bk][:, :], [("PS", 3 + bk)], [("SBU", bk)])
            if stop_after == "r1a":
                raise Stop()
            TK = lambda row: [("TMPF", row, 0), ("TMPF", row, 1)]
            NAT = TMPF[:, 0, :].rearrange("p (q c) -> p q c", q=4)
            HILO = TMPF[:, 1, :].bitcast(BF16).rearrange("p (a q c) -> p a q c", a=2, q=4)
            HIF = TMPF[:, 2, :]

            def split_hilo(src_flat, src_keys):
                cp("dve", HILO[:, 0].rearrange("p q c -> p (q c)"), src_flat, src_keys, [("HILO", 0)])
                cp("dve", HIF, HILO[:, 0].rearrange("p q c -> p (q c)"), [("HILO", 0)], TK(2))
                tt("dve", HILO[:, 1].rearrange("p q c -> p (q c)"), src_flat, HIF, ALU.subtract, src_keys + TK(2), [("HILO", 1)])
                for a_ in range(2):
                    for q_ in range(4):
                        tr(PSB[:, (a_ * 4 + q_) * 128:(a_ * 4 + q_ + 1) * 128], HILO[:, a_, q_, :], IDB[:], [("HILO", a_), ("IDB",)], [("PSB",)])

            def merge_hilo(dst_flat, dst_keys):
                cp("dve", dst_flat, PSB[:, 0:512], [("PSB",)], dst_keys)
                tt("dve", dst_flat, dst_flat, PSB[:, 512:1024], ALU.add, [("PSB",)] + dst_keys, dst_keys)

            for ri, src in enumerate((st_re, st_im)):
                for u in range(2):
                    dma("sp", NAT[:, ri * 2 + u, :], src[8 * u:8 * u + 8].rearrange("b (i p) -> (b i) p", p=128), [], [("NAT", ri * 2 + u)])
            split_hilo(TMPF[:, 0, :], [("NAT", q_) for q_ in range(4)])
            SAK = [("SA", 0), ("SA", 1)]
            merge_hilo(SA.rearrange("p a n -> p (a n)"), SAK)
            if stop_after == "r1b":
                raise Stop()
            lamb = lambda row: LAM[:, row, :].rearrange("p (f r) -> p f r", f=4).unsqueeze(1).to_broadcast([128, 16, 4, 4])
            v3 = lambda ap: ap.rearrange("p (b f r) -> p b f r", b=16, f=4)
            SBUK = [("SBU", bk) for bk in range(4)]
            cur, nxt = SA, SBs
            curk, nxtk = "SA", "SB"
            for j in range(4):
                ck_keys = [(curk, 0), (curk, 1)]
                cmul("dve", v3(S5F[:, 6, :]), v3(S5F[:, 7, :]), v3(cur[:, 0, :]), v3(cur[:, 1, :]), lamb(9), lamb(10),
                     v3(S5F[:, 4, :]), v3(S5F[:, 5, :]), ck_keys + [LQ(9), LQ(10)], [S5K(6)], [S5K(7)], S5K(4), S5K(5))
                for ri in range(2):
                    buj = SBU[:, :, ri].rearrange("p r f (b j) -> p b f r j", j=4)[:, :, :, :, j]
                    tt("dve", v3(nxt[:, ri, :]), v3(S5F[:, 6 + ri, :]), buj, ALU.add, [S5K(6 + ri)] + SBUK, [(nxtk, ri)])
                    cp("act", SXS[:, ri].rearrange("p (f r) (b j) -> p b f r j", j=4, f=4)[:, :, :, :, j], v3(nxt[:, ri, :]), [(nxtk, ri)], [("SXS", j)])
                cur, nxt, curk, nxtk = nxt, cur, nxtk, curk
            if stop_after == "r1c":
                raise Stop()
            split_hilo(cur.rearrange("p a n -> p (a n)"), [(curk, 0), (curk, 1)])
            merge_hilo(TMPF[:, 0, :], TK(0))
            for ri, dst in enumerate((sre_s, sim_s)):
                for u in range(2):
                    dma("sp", dst[8 * u:8 * u + 8].rearrange("b (i p) -> (b i) p", p=128), NAT[:, ri * 2 + u, :], TK(0), [("out", "ss", ri, u)])
            if stop_after == "r1d":
                raise Stop()
            SXSK = [("SXS", j) for j in range(4)]
            for i in [4 * f_ + r_ for r_ in range(4) for f_ in range(4)]:
                f, r = i // 4, i % 4
                for ri in range(2):
                    mm(PS[0][32 * r:32 * r + 32, f * 64:(f + 1) * 64], CK[:, i, 0, ri, :], SXS[:, ri, i, :], ri == 0, ri == 1,
                       [("CK", 0, ri)] + SXSK, [("PS", 0)], tile_position=(0, 32 * r))
            for f in range(4):
                s = f % 2
                yk = ("TMPF", 0, s)
                yv = TMPF[:, 0, s * 256:s * 256 + NS]
                stt("dve", yv, FA[:, f, T:TA], SMALL[:, 8 + f:9 + f], PS[0][:, f * 64:(f + 1) * 64], ALU.mult, ALU.add,
                    [("FA", f, 16), SQ("dsk"), ("PS", 0)], [yk])
                gelu_to(yv, [yk], FC[:, f, T:TA], [("FC", f, 16)], s, NS)
            if stop_after == "s5s":
                dump("gel", FC[:, :, :], [("FC", f, t_) for f in range(4) for t_ in range(NT)])
                raise Stop()

            dma("sp", WST[:, :].rearrange("p (k n) -> p k n", k=4), w_glu.rearrange("(k p) n -> p k n", p=128), [], [("WST",)])
            WG = WB[:, 0:4096].rearrange("p (k n) -> p k n", k=4)
            for kc in range(4):
                cp("act" if kc % 2 == 0 else "dve", WG[:, kc, :], WST[:, kc * 1024:(kc + 1) * 1024], [("WST",)], [("WB", 0, 2 * kc), ("WB", 0, 2 * kc + 1)])
            cnt = 0
            for m in range(4):
                for bi, (c0, n) in enumerate(TBK):
                    pa = next_ps()
                    for kc in range(4):
                        mm(PS[pa][:, 0:n], WG[:, kc, m * 128:(m + 1) * 128], FC[:, kc, c0:c0 + n], kc == 0, kc == 3,
                           [("WB", 0, 2 * kc), ("WB", 0, 2 * kc + 1)] + fm_keys("FC", kc, bi), [("PS", pa)])
                    pb = next_ps()
                    for kc in range(4):
                        mm(PS[pb][:, 0:n], WG[:, kc, 512 + m * 128:512 + (m + 1) * 128], FC[:, kc, c0:c0 + n], kc == 0, kc == 3,
                           [("WB", 0, 2 * kc), ("WB", 0, 2 * kc + 1)] + fm_keys("FC", kc, bi), [("PS", pb)])
                    pp = cnt % 2
                    cnt += 1
                    sgk = [("TMPF", 2 + pp, 0), ("TMPF", 2 + pp, 1)]
                    t1k = [("TMPF", pp, 0), ("TMPF", pp, 1)]
                    act(TMPF[:, 2 + pp, 0:n], PS[pb][:, 0:n], AF.Sigmoid, [("PS", pb), SQ("bglu")], sgk, bias=SMALL[:, 16 + m:17 + m])
                    stt("dve", TMPF[:, pp, 0:n], PS[pa][:, 0:n], SMALL[:, 12 + m:13 + m], TMPF[:, 2 + pp, 0:n], ALU.add, ALU.mult,
                        [("PS", pa), SQ("bglu")] + sgk, t1k)
                    tt("pool", FA[:, m, c0:c0 + n], TMPF[:, pp, 0:n], FB[:, m, c0:c0 + n], ALU.mult, t1k + fm_keys("FB", m, bi), fm_keys("FA", m, bi))
            dump("ys", FA[:, :, :], [("FA", m, t_) for m in range(4) for t_ in range(NT)])
            barrier()
            if stop_after == "glu":
                raise Stop()

            mset("pool", VA[:, 0, :, 64:128], 1.0, [("VAones", 0)])
            mset("pool", VA[:, 1, :, 0:64], 1.0, [("VAones", 1)])
            OSTH = lambda s2, h: OST[:, s2, h * 512:(h + 1) * 512]

            def proj_tm(slot, dst_p, dst_s, to_vb):
                for t_ in range(NT):
                    r = rows_of(t_)
                    pi = next_ps()
                    for kc in range(8):
                        mm(PS[pi][:r, :], XN[:, kc, t_ * 128:t_ * 128 + r], WBv[:, slot, kc, :], kc == 0, kc == 7,
                           [("WB", slot, kc), ("XN", kc, t_)], [("PS", pi)])
                    s2, h = (t_ // 2) % 2, t_ % 2
                    cp("act", OSTH(s2, h)[:r], PS[pi][:r, :], [("PS", pi)], [("OST", s2, h)])
                    if to_vb:
                        cp("pool", VB[:r, t_, 0:512], OSTH(s2, h)[:r], [("OST", s2, h)], [("VB", t_)])
                    dstap = dst_p[t_ * 128:(t_ + 1) * 128, :] if t_ < 16 else dst_s[:, :]
                    dma("pool", dstap, OSTH(s2, h)[:r], [("OST", s2, h)], [("out", "kv", id(dst_p), t_)])

            load_w_group(2, 0)
            load_w_group(3, 1)
            proj_fm(0, lambda m, bi, pi, c0, n: act(FB[:, m, c0:c0 + n], PS[pi][:, 0:n], AF.Copy, [("PS", pi)], fm_keys("FB", m, bi), scale=0.125))
            if stop_after == "r3a":
                raise Stop()
            load_w_group(4, 0)
            proj_fm(1, lambda m, bi, pi, c0, n: cp("dve", FC[:, m, c0:c0 + n], PS[pi][:, 0:n], [("PS", pi)], fm_keys("FC", m, bi)))
            if stop_after == "r3b":
                raise Stop()
            proj_tm(1, k_p, k_s, False)
            if stop_after == "r3c":
                raise Stop()
            load_w_group(5, 1)
            proj_tm(0, v_p, v_s, True)
            if stop_after == "r3d":
                raise Stop()
            proj_fm(1, lambda m, bi, pi, c0, n: act(FD[:, m, c0:c0 + n], PS[pi][:, 0:n], AF.Silu, [("PS", pi)], fm_keys("FD", m, bi)))
            dump("q", FB[:, :, :], [("FB", m, t_) for m in range(4) for t_ in range(NT)])
            dump("kt", FC[:, :, :], [("FC", m, t_) for m in range(4) for t_ in range(NT)])
            barrier()
            if stop_after == "inproj2":
                raise Stop()

            SBK = [0, 1, 4, 5]
            KTZ = WB[:, :].rearrange("p (s n) -> p s n", s=4)[:, 0:2, 0:T]
            mset("pool", KTZ[64:128, 0, :], 0.0, [("KTZz", 0)])
            mset("pool", KTZ[0:64, 1, :], 0.0, [("KTZz", 1)])
            LA = 3
            its = []
            octr = 0
            for h in range(8):
                for qc in range(4):
                    ob = 2 + octr % 2
                    octr += 1
                    nj = 4 * qc + 4
                    for j in range(nj):
                        its.append((h, qc, j, nj, ob))

            def stage_a(n):
                h, qc, j, nj, ob = its[n]
                c, hb = h // 2, 64 * (h % 2)
                qb0 = max(4 * qc, j)
                nn = (4 * qc + 4 - qb0) * 128
                col0 = qb0 * 128
                e = n % 4
                sbk = SBK[e]
                if j == 0 and qc == 0:
                    cp("pool", VA[:, h % 2, :, hb:hb + 64], VB[:, 0:16, h * 64:(h + 1) * 64], [("VB", t_) for t_ in range(16)] + [("VAones", h % 2)], [("VA", h % 2)])
                    cp("act", KTZ[hb:hb + 64, h % 2, :], FC[hb:hb + 64, c, 0:T], [("FC", c, t_) for t_ in range(16)] + [("KTZz", h % 2)], [("KTZ", h % 2)])
                qk = [("FB", c, t_) for t_ in range(qb0, 4 * qc + 4)]
                mm(PS[sbk][:, 0:nn], KTZ[:, h % 2, j * 128:(j + 1) * 128], FB[:, c, col0:col0 + nn], True, True,
                   [("KTZ", h % 2), ("KTZz", h % 2)] + qk, [("PS", sbk)])
                act(EPB[:, e, 0:nn], PS[sbk][:, 0:nn], AF.Exp, [("PS", sbk)], [("EPB", e)])
                mc = (qb0 - j + 3) * 128
                tt("dve" if n % 2 == 0 else "pool", EPB[:, 4 + e, 0:nn], EPB[:, e, 0:nn], MASK[:, mc:mc + nn], ALU.mult,
                   [("EPB", e), ("MASK",)], [("EPB", 4 + e)])

            def stage_b(n):
                h, qc, j, nj, ob = its[n]
                c, hb = h // 2, 64 * (h % 2)
                dn = 64 - hb
                qb0 = max(4 * qc, j)
                nn = (4 * qc + 4 - qb0) * 128
                qoff = (qb0 - 4 * qc) * 128
                e = n % 4
                mm(PS[ob][:, qoff:qoff + nn], VA[:, h % 2, j, :], EPB[:, 4 + e, 0:nn], j == 0, j == nj - 1,
                   [("VA", h % 2), ("VAones", h % 2), ("EPB", 4 + e)], [("PS", ob)])
                if j == nj - 1:
                    pp = qc % 2
                    rk_, tk_ = [("TMPF", 2 * pp, 0), ("TMPF", 2 * pp, 1)], [("TMPF", 2 * pp + 1, 0), ("TMPF", 2 * pp + 1, 1)]
                    recip(TMPF[hb:hb + 64, 2 * pp, :], PS[ob][dn:dn + 64, :], [("PS", ob)], rk_)
                    tt("dve", TMPF[hb:hb + 64, 2 * pp + 1, :], PS[ob][hb:hb + 64, :], TMPF[hb:hb + 64, 2 * pp, :], ALU.mult, [("PS", ob)] + rk_, tk_)
                    tt("pool", XN[hb:hb + 64, c, qc * 512:(qc + 1) * 512], TMPF[hb:hb + 64, 2 * pp + 1, :], FD[hb:hb + 64, c, qc * 512:(qc + 1) * 512], ALU.mult,
                       tk_ + [("FD", c, t_) for t_ in range(4 * qc, 4 * qc + 4)], [("XN", c, t_, hb) for t_ in range(4 * qc, 4 * qc + 4)])

            for n in range(len(its) + LA):
                if n < len(its):
                    stage_a(n)
                if n - LA >= 0:
                    stage_b(n - LA)
            barrier()
            if stop_after == "attn":
                if "ya" in dbg_out:
                    dma("sp", dbg_out["ya"][:, :, 0:T], XN[:, 0:4, 0:T], [], [("dbg", "ya")])
                raise Stop()
            TRB2 = [PSB, PS[1][:].bitcast(BF16)]
            TRK2 = [("PSB",), ("PS", 1)]
            mset("pool", TMPF[:, 0:2, :], 0.0, [("QB",)])
            for c in range(4):
                for hh in range(2):
                    cp("dve", QB[hh * 64:(hh + 1) * 64, c, :].rearrange("p (b j h) -> p b j h", b=NB, j=4)[:, :, :, 2 * c + hh],
                       FB[hh * 64:(hh + 1) * 64, c, T:TA].rearrange("p (b j) -> p b j", j=4), [("QB",), ("FB", c, 16)], [("QBd", c, hh)])
            QBK = [("QB",)] + [("QBd", c, hh) for c in range(4) for hh in range(2)]
            for c in range(4):
                mm(PS[1][0:64, :], FC[:, c, T:TA], QB[:, c, :], c == 0, c == 3, [("FC", c, 16)] + QBK, [("PS", 1)])
            act(EPB[0:64, 2, :], PS[1][0:64, :], AF.Exp, [("PS", 1)], [("EPB", 2)])
            tt("dve", EPB[0:64, 3, :], EPB[0:64, 2, :], MSK_N[:, :], ALU.mult, [("EPB", 2), ("MSK_N",)], [("EPB", 3)])
            for c in range(4):
                mm(PS[2 + c][:, :], VB[0:64, 16, c * 128:(c + 1) * 128], EPB[0:64, 3, :], True, False, [("VB", 16), ("EPB", 3)], [("PS", 2 + c)])
            mm(PS[6][:, :], ONESB[0:64, :], EPB[0:64, 3, :], True, False, [("ONESB",), ("EPB", 3)], [("PS", 6)])
            trc = [0]
            KT2 = WST[:, 0:3584].bitcast(BF16).rearrange("p (s c k) -> p s c k", s=2, c=4)

            def cache_dma(b, kv, src):
                for tl in range(3):
                    srcap = src[b].rearrange("(g x) c -> g x c", x=16)[32 * tl:32 * tl + 32, 0:4, :]
                    dma("pool", KST[:, kv, tl, :], srcap, [], [("KST", kv, tl)])
                dma("pool", KST[:, kv, 3:7, :], src[b, 1536:2048, :].rearrange("(t p) c -> p t c", p=128), [], [("KST", kv, 3)])

            def st_t(b):
                cache_dma(b, 0, ck)
                KK = [("KST", 0, tl) for tl in range(4)]
                for c in range(4):
                    tb, tk_ = TRB2[trc[0] % 2], TRK2[trc[0] % 2]
                    trc[0] += 1
                    for tl in range(7):
                        tr(tb[:, tl * 128:(tl + 1) * 128], KST[:, 0, tl, c * 128:(c + 1) * 128], IDB[:], KK + [("IDB",)], [tk_])
                    cp("act" if c % 2 == 0 else "dve", KT2[:, b % 2, c, :], tb[:, 0:896], [tk_], [("KT", b % 2, c)])

            def st_c(b):
                cache_dma(b, 1, cv)
                VK = [("KST", 1, tl) for tl in range(4)]
                for tl in range(7):
                    for c in range(4):
                        mm(PS[0][:, tl * 32:(tl + 1) * 32], KT2[:, b % 2, c, tl * 128:(tl + 1) * 128], QB[:, c, b * 32:(b + 1) * 32], c == 0, c == 3,
                           [("KT", b % 2, c)] + QBK, [("PS", 0)])
                act(EPB[:, 0, 0:224], PS[0][:, 0:224], AF.Exp, [("PS", 0)], [("EPB", 0)])
                tt("dve", EPB[:, 1, 0:224], EPB[:, 0, 0:224], MSK_S[:].rearrange("p t j h -> p (t j h)"), ALU.mult, [("EPB", 0), ("MSK_S",)], [("EPB", 1)])
                last_b = (b == NB - 1)
                for tl in range(7):
                    for c in range(4):
                        mm(PS[2 + c][:, b * 32:(b + 1) * 32], KST[:, 1, tl, c * 128:(c + 1) * 128], EPB[:, 1, tl * 32:(tl + 1) * 32], False,
                           last_b and tl == 6, VK + [("EPB", 1)], [("PS", 2 + c)])
                    mm(PS[6][:, b * 32:(b + 1) * 32], ONESB[:, :], EPB[:, 1, tl * 32:(tl + 1) * 32], False, last_b and tl == 6,
                       [("ONESB",), ("EPB", 1)], [("PS", 6)])

            st_t(0)
            for b in range(NB):
                if b + 1 < NB:
                    st_t(b + 1)
                st_c(b)
            for c in range(4):
                for hh in range(2):
                    rows = slice(hh * 64, (hh + 1) * 64)
                    hsel = lambda ap: ap.rearrange("p (b j h) -> p b j h", b=NB, j=4)[:, :, :, 2 * c + hh]
                    rd = TMPF[rows, 2, 0:64].rearrange("p (b j) -> p b j", j=4)
                    t1 = TMPF[rows, 3, 0:64].rearrange("p (b j) -> p b j", j=4)
                    recip(rd, hsel(PS[6][rows, :]), [("PS", 6)], [("TMPF", 2, 0)])
                    tt("dve", t1, hsel(PS[2 + c][rows, :]), rd, ALU.mult, [("PS", 2 + c), ("TMPF", 2, 0)], [("TMPF", 3, 0)])
                    tt("pool", XN[rows, c, T:TA].rearrange("p (b j) -> p b j", j=4), t1, FD[rows, c, T:TA].rearrange("p (b j) -> p b j", j=4), ALU.mult,
                       [("TMPF", 3, 0), ("FD", c, 16)], [("XN", c, 16, hh * 64)])
            dump("ya", XN[:, 0:4, :], [("XN", c, t_, hb) for c in range(4) for t_ in range(NT) for hb in (0, 64)])
            barrier()
            if stop_after == "sattn":
                raise Stop()

            dma("sp", GFB[:, :], bass.AP(g_fin.tensor, 0, [[0, 128], [1, D]]), [], [("GFB",)])
            WO = WB[:, :].rearrange("p (k n) -> p k n", k=8)
            for half in range(2):
                dma("sp", WST[:, :].rearrange("p (k n) -> p k n", k=4), w_out[half * 512:(half + 1) * 512, :].rearrange("(k p) n -> p k n", p=128),
                    [], [("WST",)])
                for kc in range(4):
                    if kc % 2 == 0:
                        act(WO[:, half * 4 + kc, :], WST[:, kc * 1024:(kc + 1) * 1024], AF.Copy, [("WST",), SQ("gssm"), SQ("gattn")], [("WO", half * 4 + kc)],
                            scale=SMALL[:, 20 + half * 4 + kc:21 + half * 4 + kc])
                    else:
                        ts("dve", WO[:, half * 4 + kc, :], WST[:, kc * 1024:(kc + 1) * 1024], SMALL[:, 20 + half * 4 + kc:21 + half * 4 + kc], None, ALU.mult, None,
                           [("WST",), SQ("gssm"), SQ("gattn")], [("WO", half * 4 + kc)])
            def op_a(t_):
                r = rows_of(t_)
                s = t_ % 2
                cs = slice(t_ * 128, t_ * 128 + r)
                dma("sp", XST[:r, s, :], x_src(t_), [], [("XST", s)])
                yak = [("XN", kc, t_, hb) for kc in range(4) for hb in (0, 64)]
                ysk = [("FA", kc, t_) for kc in range(4)]
                e0, e1 = 2 * s, 2 * s + 1
                tt("pool", EPB[:, e0, :].rearrange("p (k t) -> p k t", k=4)[:, :, 0:r], FA[:, :, cs], FA[:, :, cs], ALU.mult, ysk, [("EPB", e0)])
                tt("pool", EPB[:, e1, :].rearrange("p (k t) -> p k t", k=4)[:, :, 0:r], XN[:, 0:4, cs], XN[:, 0:4, cs], ALU.mult, yak, [("EPB", e1)])

            def op_mm(t_):
                r = rows_of(t_)
                s = t_ % 2
                cs = slice(t_ * 128, t_ * 128 + r)
                yak = [("XN", kc, t_, hb) for kc in range(4) for hb in (0, 64)]
                ysk = [("FA", kc, t_) for kc in range(4)]
                for br in range(2):
                    for kc in range(4):
                        mm(PS[4][:r, br:br + 1], EPB[:, 2 * s + br, kc * 128:kc * 128 + r], ONESB[:, 0:1], kc == 0, kc == 3,
                           [("EPB", 2 * s + br), ("ONESB",)], [("PS", 4)])
                for half in range(2):
                    for kc in range(4):
                        mm(PS[half][:r, :], FA[:, kc, cs], WO[:, kc, half * 512:(half + 1) * 512], kc == 0, kc == 3,
                           ysk + [("WO", kc)], [("PS", half)])
                    for kc in range(4):
                        mm(PS[2 + half][:r, :], XN[:, kc, cs], WO[:, 4 + kc, half * 512:(half + 1) * 512], kc == 0, kc == 3,
                           yak + [("WO", 4 + kc)], [("PS", 2 + half)])

            def op_b1(t_):
                r = rows_of(t_)
                s = t_ % 2
                rsk = SQ(("rs2", s))
                rs2 = SMALL[:r, 70 + 16 * s:72 + 16 * s]
                ts("dve", rs2, PS[4][:r, 0:2], 1.0 / 512, EPS, ALU.mult, ALU.add, [("PS", 4)], [rsk])
                act(rs2, rs2, AF.Sqrt, [rsk], [rsk])
                recip(rs2, rs2, [rsk], [rsk])
                for half in range(2):
                    hs = slice(half * 512, (half + 1) * 512)
                    stt("dve", OST[:r, s, hs], PS[half][:r, :], SMALL[:r, 70 + 16 * s:71 + 16 * s], XST[:r, s, hs], ALU.mult, ALU.add,
                        [("PS", half), rsk, ("XST", s)], [("OST", s, half)])
                    stt("dve", XST[:r, s, hs], PS[2 + half][:r, :], SMALL[:r, 71 + 16 * s:72 + 16 * s], OST[:r, s, hs], ALU.mult, ALU.add,
                        [("PS", 2 + half), rsk, ("OST", s, half)], [("XST", s)])

            def op_b2(t_):
                r = rows_of(t_)
                s = t_ % 2
                fk = SQ(("fs", s))
                fsc = SMALL[:r, 104 + s:105 + s]
                act(OST[:r, s, :], XST[:r, s, :], AF.Square, [("XST", s)], [("OST", s, 0), ("OST", s, 1), fk], accum_out=fsc)
                ts("dve", fsc, fsc, 1.0 / D, EPS, ALU.mult, ALU.add, [fk], [fk])
                act(fsc, fsc, AF.Sqrt, [fk], [fk])
                recip(fsc, fsc, [fk], [fk])
                stt("dve", OST[:r, s, :], XST[:r, s, :], fsc, GFB[:r, :], ALU.mult, ALU.mult, [("XST", s), fk, ("GFB",)], [("OST", s, 0), ("OST", s, 1)])
                dstap = y_p[t_ * 128:(t_ + 1) * 128, :] if t_ < 16 else y_s[:, :]
                dma("sp", dstap, OST[:r, s, :], [("OST", s, 0), ("OST", s, 1)], [("out", "y", t_)])

            op_a(0)
            op_mm(0)
            for t_ in range(NT):
                if t_ + 1 < NT:
                    op_a(t_ + 1)
                op_b1(t_)
                if t_ + 1 < NT:
                    op_mm(t_ + 1)
                op_b2(t_)
        except Stop:
            pass

        with (
            nc.semaphore("s_pe") as s_pe, nc.semaphore("s_act") as s_act, nc.semaphore("s_dve") as s_dve,
            nc.semaphore("s_pool") as s_pool, nc.semaphore("s_sp") as s_sp,
        ):
            dstack = ExitStack()
            with dstack:
                dsems = [dstack.enter_context(nc.semaphore("dma%d" % i)) for i in range(P.ndma)]
                with nc.Block() as block:
                    P.emit(nc, block, {"pe": s_pe, "act": s_act, "dve": s_dve, "pool": s_pool, "sp": s_sp}, dsems)
    return nc


_NC_CACHE = {}


def _shard_inputs(inp):
    maps = []
    f32 = lambda a: np.ascontiguousarray(a, dtype=np.float32)
    for c in range(NCORES):
        sl = slice(NB * c, NB * (c + 1))
        maps.append({
            "xp": f32(inp["x_prompt"][c]), "xsm": f32(inp["x_sample"][sl].reshape(NS, D)),
            "ck": f32(inp["cache_k"][0, sl].reshape(NB, 2048, 512)), "cv": f32(inp["cache_v"][0, sl].reshape(NB, 2048, 512)),
            "st_re": f32(inp["state_ssm_re"][0, sl].reshape(NB, 2048)), "st_im": f32(inp["state_ssm_im"][0, sl].reshape(NB, 2048)),
            "g_in": f32(inp["norm_in_g"][0]), "w_in": f32(inp["w_in"][0]),
            "a_re": f32(inp["ssm_A_re"][0].reshape(2048)), "a_im": f32(inp["ssm_A_im"][0].reshape(2048)),
            "log_dt": f32(inp["ssm_log_dt"][0]),
            "b_re": f32(inp["ssm_B_re"][0].reshape(2048, 16)), "b_im": f32(inp["ssm_B_im"][0].reshape(2048, 16)),
            "c_re": f32(inp["ssm_C_re"][0].reshape(512, 64)), "c_im": f32(inp["ssm_C_im"][0].reshape(512, 64)),
            "d_skip": f32(inp["ssm_D"][0]), "w_glu": f32(inp["w_glu"][0]), "b_glu": f32(inp["b_glu"][0]),
            "g_ssm": f32(inp["norm_ssm_g"][0]), "g_attn": f32(inp["norm_attn_g"][0]), "w_out": f32(inp["w_out"][0]),
            "g_fin": f32(inp["final_norm_g"]),
        })
    return maps


def kernel(**inputs):
    inp = {k: np.asarray(v) for k, v in inputs.items()}
    if "nc" not in _NC_CACHE:
        _NC_CACHE["nc"] = build_program()
    nc = _NC_CACHE["nc"]
    maps = _shard_inputs(inp)
    res = run_bass_kernel_spmd(nc, maps, core_ids=list(range(NCORES)))
    R = res.results
    cat = lambda name: np.stack([np.asarray(R[c][name], dtype=np.float32) for c in range(NCORES)], 0)
    y_prompt = cat("y_p").reshape(8, T, D)
    y_sample = cat("y_s").reshape(8 * NB, 4, D)
    k_prompt = cat("k_p").reshape(1, 8, T, 8, 64)
    v_prompt = cat("v_p").reshape(1, 8, T, 8, 64)
    sre_prompt = cat("sre_p").reshape(1, 8, 32, 64)
    sim_prompt = cat("sim_p").reshape(1, 8, 32, 64)
    k_sample = cat("k_s").reshape(1, 8 * NB, 4, 8, 64)
    v_sample = cat("v_s").reshape(1, 8 * NB, 4, 8, 64)
    sre_sample = cat("sre_s").reshape(1, 8 * NB, 32, 64)
    sim_sample = cat("sim_s").reshape(1, 8 * NB, 32, 64)
    return (y_prompt, y_sample, k_prompt, v_prompt, sre_prompt, sim_prompt, k_sample, v_sample, sre_sample, sim_sample)
```

```python
import math
import numpy as np
import concourse.bass as bass
import concourse.mybir as mybir
from concourse.bass_utils import run_bass_kernel_spmd

F32 = mybir.dt.float32
BF16 = mybir.dt.bfloat16
I32 = mybir.dt.int32
AF = mybir.ActivationFunctionType
ALU = mybir.AluOpType
AX = mybir.AxisListType

NCORES = 8
D = 1024
T = 2048
NS = 64
TA = T + NS
NB = 16
L = 8
J = T // L
EPS = 1e-6
PI = math.pi
DEBUG = {}


class Op:
    __slots__ = ("stream", "fn", "deps", "is_dma", "slot", "val", "need_inc", "name")


class Prog:
    STREAMS = ["pe", "act", "dve", "pool", "sp"]

    def __init__(self, ndma=40):
        self.ops = {s: [] for s in self.STREAMS}
        self.st = {}
        self.ndma = ndma
        self.dma_last = [None] * ndma
        self.dma_cnt = [0] * ndma
        self.rr = 0
        self.rr_sw = 0
        self.nhw = ndma - 8
        self.all_dma_out = []

    def add(self, stream, fn, reads=(), writes=(), dma=False, name="", extra=()):
        op = Op()
        op.stream, op.fn, op.is_dma, op.need_inc, op.name = stream, fn, dma, False, name
        op.slot, op.val = None, 0
        deps = {}
        for k in reads:
            s = self.st.get(k)
            if s is not None and s[0] is not None:
                deps[id(s[0])] = (s[0], True)
        for k in writes:
            s = self.st.get(k)
            if s is not None:
                if s[0] is not None and id(s[0]) not in deps:
                    deps[id(s[0])] = (s[0], False)
                for r in s[1].values():
                    if id(r) not in deps:
                        deps[id(r)] = (r, False)
        final = []
        for d, raw in deps.values():
            if d is op:
                continue
            if (not d.is_dma) and (not dma) and d.stream == stream and stream == 'pe' and not raw:
                continue
            final.append(d)
        for d in extra:
            if d is not None:
                final.append(d)
        if dma:
            if stream == "pool":
                slot = self.nhw + self.rr_sw
                self.rr_sw = (self.rr_sw + 1) % (self.ndma - self.nhw)
            else:
                slot = self.rr
                self.rr = (self.rr + 1) % self.nhw
            prev = self.dma_last[slot]
            if prev is not None:
                final.append(prev)
            self.dma_cnt[slot] += 1
            op.slot, op.val = slot, 16 * self.dma_cnt[slot]
            self.dma_last[slot] = op
        for d in final:
            d.need_inc = True
        op.deps = final
        for k in reads:
            s = self.st.setdefault(k, [None, {}])
            key = (stream, id(op)) if dma else stream
            s[1][key] = op
        for k in writes:
            self.st[k] = [op, {}]
        self.ops[stream].append(op)
        return op

    def emit(self, nc, block, sems, dma_sems):
        for s in self.STREAMS:
            c = 0
            for op in self.ops[s]:
                if not op.is_dma and op.need_inc:
                    c += 1
                    op.val = c
        engs = {"pe": block.tensor, "act": block.scalar, "dve": block.vector, "pool": block.gpsimd, "sp": block.sync}

        def run_stream(s):
            def body(eng):
                waited = {}
                for op in self.ops[s]:
                    for d in op.deps:
                        if d.is_dma:
                            key, sem, val = ("d", d.slot), dma_sems[d.slot], d.val
                        else:
                            key, sem, val = ("c", d.stream), sems[d.stream], d.val
                        if waited.get(key, 0) < val:
                            eng.wait_ge(sem, val)
                            waited[key] = val
                    ins = op.fn(eng)
                    if op.is_dma:
                        ins.then_inc(dma_sems[op.slot], 16)
                    elif op.need_inc:
                        ins.then_inc(sems[s], 1)
                if s == "sp":
                    for slot in range(self.ndma):
                        if self.dma_cnt[slot] and waited.get(("d", slot), 0) < 16 * self.dma_cnt[slot]:
                            eng.wait_ge(dma_sems[slot], 16 * self.dma_cnt[slot])
            return body

        for s in self.STREAMS:
            engs[s](run_stream(s))


def build_program(dbg=None, dbg_bf16=(), stop_after=None, small_cache=False):
    dbg = dbg or {}
    nc = bass.Bass("TRN2", target_bir_lowering=False)
    P = Prog()

    def din(name, shape, dt=F32):
        return nc.dram_tensor(name, list(shape), dt, kind="ExternalInput").ap()

    def dout(name, shape, dt=F32):
        return nc.dram_tensor(name, list(shape), dt, kind="ExternalOutput").ap()

    xp = din("xp", [T, D]); xsm = din("xsm", [NS, D])
    ck = din("ck", [NB, 8 if small_cache else 2048, 512]); cv = din("cv", [NB, 8 if small_cache else 2048, 512])
    st_re = din("st_re", [NB, 2048]); st_im = din("st_im", [NB, 2048])
    g_in = din("g_in", [D]); w_in = din("w_in", [D, 3072])
    a_re = din("a_re", [2048]); a_im = din("a_im", [2048]); log_dt = din("log_dt", [32])
    b_re = din("b_re", [2048, 16]); b_im = din("b_im", [2048, 16])
    c_re = din("c_re", [512, 64]); c_im = din("c_im", [512, 64])
    d_skip = din("d_skip", [512]); w_glu = din("w_glu", [512, 1024]); b_glu = din("b_glu", [1024])
    g_ssm = din("g_ssm", [512]); g_attn = din("g_attn", [512]); w_out = din("w_out", [1024, 1024]); g_fin = din("g_fin", [D])

    y_p = dout("y_p", [T, D]); y_s = dout("y_s", [NS, D])
    k_p = dout("k_p", [T, 512]); v_p = dout("v_p", [T, 512])
    sre_p = dout("sre_p", [2048]); sim_p = dout("sim_p", [2048])
    k_s = dout("k_s", [NS, 512]); v_s = dout("v_s", [NS, 512])
    sre_s = dout("sre_s", [NB, 2048]); sim_s = dout("sim_s", [NB, 2048])
    dbg_out = {k: dout("dbg_" + k, shp, BF16 if k in dbg_bf16 else F32) for k, shp in dbg.items()}

    from contextlib import ExitStack
    es = ExitStack()

    def sb(name, shape, dt):
        return es.enter_context(nc.sbuf_tensor(name, list(shape), dt))

    def ps(name, shape, dt):
        return es.enter_context(nc.psum_tensor(name, list(shape), dt))

    class Stop(Exception):
        pass

    with es:
        XN = sb("XN", [128, 8, TA], BF16)
        FA = sb("FA", [128, 4, TA], BF16)
        FB = sb("FB", [128, 4, TA], BF16)
        FC = sb("FC", [128, 4, TA], BF16)
        FD = sb("FD", [128, 4, TA], BF16)
        VB = sb("VB", [128, 17, 544], BF16)
        VA = sb("VA", [128, 2, 16, 128], BF16)
        WST = sb("WST", [128, 4096], F32)
        WB = sb("WB", [128, 8192], BF16)
        STG = sb("STG", [128, 5120], F32)
        MASK = sb("MASK", [128, 19 * 128], BF16)
        IDB = sb("IDB", [128, 128], BF16)
        ONESB = sb("ONESB", [128, 128], BF16)
        SMALL = sb("SMALL", [128, 128], F32)
        SM2 = sb("SM2", [128, 64], F32)
        EPB = sb("EPB", [128, 8, 512], BF16)
        TMPF = sb("TMPF", [128, 4, 512], F32)
        XC = sb("XC", [128, 2, J], BF16)
        LAM = sb("LAM", [128, 40, 16], F32)
        MSK_S = sb("MSK_S", [128, 7, 4, 8], BF16)
        MSK_N = sb("MSK_N", [64, 512], BF16)
        PS = [ps("PS%d" % i, [128, 512], F32) for i in range(7)]
        PSB = ps("PSB", [128, 1024], BF16)

        XST = STG[:, 0:2048].rearrange("p (s d) -> p s d", s=2)
        XSB = STG[:, 2048:3072].bitcast(BF16).rearrange("p (s d) -> p s d", s=2)
        OST = STG[:, 3072:5120].rearrange("p (s d) -> p s d", s=2)
        S5F = STG[:, 0:3072].rearrange("p (k j) -> p k j", k=12)
        PT = STG[:, 3072:5120].bitcast(BF16).rearrange("p (t a j) -> p t a j", t=L, a=2)
        BKT = FD[:].rearrange("p a b -> p (a b)")[:, 0:4 * L * 2 * 128].rearrange("p (f k a c) -> p f k a c", f=4, k=L, a=2)
        CK = VB[:].rearrange("p a b -> p (a b)")[:, 0:16 * (L + 1) * 64].rearrange("p (i v a c) -> p i v a c", i=16, v=L + 1, a=2)
        GFB = MASK[:, 0:2048].bitcast(F32)
        KST = XN[:, 4:8, :].rearrange("p a b -> p (a b)")[:, 0:7168].rearrange("p (s t c) -> p s t c", s=2, t=7)
        KT = WST[:, 0:1792].bitcast(BF16).rearrange("p (c k) -> p c k", c=4)
        QB = TMPF[:, 0:2, :].rearrange("p a b -> p (a b)").bitcast(BF16).rearrange("p (c n) -> p c n", c=4)
        NT = 17
        SQ = lambda k: ("SMALL", k)

        def dma(q, out, in_, reads, writes, **kw):
            return P.add(q, lambda e: e.dma_start(out=out, in_=in_, **kw), reads=reads, writes=writes, dma=True)

        def act(out, in_, func, reads, writes, **kw):
            return P.add("act", lambda e: e.activation(out=out, in_=in_, func=func, **kw), reads=reads, writes=writes)

        def tt(eng, out, in0, in1, op, reads, writes):
            return P.add(eng, lambda e: e.tensor_tensor(out, in0, in1, op), reads=reads, writes=writes)

        def ts(eng, out, in0, s1, s2, op0, op1, reads, writes):
            if op1 is None:
                return P.add(eng, lambda e: e.tensor_single_scalar(out, in0, s1, op0), reads=reads, writes=writes)
            return P.add(eng, lambda e: e.tensor_scalar(out, in0, s1, s2, op0, op1), reads=reads, writes=writes)

        def stt(eng, out, in0, scalar, in1, op0, op1, reads, writes):
            return P.add(eng, lambda e: e.scalar_tensor_tensor(out, in0, scalar, in1, op0, op1), reads=reads, writes=writes)

        def cp(eng, out, in_, reads, writes):
            if eng == "act":
                return P.add("act", lambda e: e.copy(out, in_), reads=reads, writes=writes)
            return P.add(eng, lambda e: e.tensor_copy(out, in_), reads=reads, writes=writes)

        pe_last = [None, None]

        def mm(out, lhsT, rhs, start, stop, reads, writes, **kw):
            tp = kw.get("tile_position")
            extra = []
            if tp is not None and pe_last[0] is not None and tp != pe_last[0]:
                extra = [pe_last[1]]
            op = P.add("pe", lambda e: e.matmul(out, lhsT, rhs, start=start, stop=stop, **kw), reads=reads, writes=writes, extra=extra)
            pe_last[0], pe_last[1] = tp, op
            return op

        def tr(out, in_, ident, reads, writes):
            op = P.add("pe", lambda e: e.transpose(out, in_, ident), reads=reads, writes=writes)
            pe_last[0], pe_last[1] = None, op
            return op

        def mset(eng, ap, val, writes, reads=()):
            return P.add(eng, lambda e: e.memset(ap, val), reads=reads, writes=writes)

        def recip(out, in_, reads, writes):
            return P.add("dve", lambda e: e.reciprocal(out, in_), reads=reads, writes=writes)

        def dump(name, src, reads):
            if name in dbg_out:
                dma("sp", dbg_out[name], src, reads, [("dbg", name)])

        bar_n = [0]

        def barrier():
            n = bar_n[0]
            bar_n[0] += 1
            P.add("pe", lambda e: e.transpose(PSB[:, 896:1024], IDB[:], IDB[:]), reads=[("IDB",)], writes=[("PSB",), ("bar", n, "pe")])
            P.add("act", lambda e: e.copy(SM2[:, 56:57], ONESB[:, 0:1]), reads=[("ONESB",)], writes=[("bar", n, "act"), ("bscr", "act")])
            mset("dve", SM2[:, 58:59], 0.0, [("bar", n, "dve"), ("bscr", "dve")])
            mset("pool", SM2[:, 59:60], 0.0, [("bar", n, "pool"), ("bscr", "pool")])
            rk = [("bar", n, e_) for e_ in ("pe", "act", "dve", "pool")]
            dm = [d for d in P.dma_last if d is not None]
            P.add("pe", lambda e: e.transpose(PSB[:, 896:1024], IDB[:], IDB[:]), reads=rk + [("IDB",)], writes=[("PSB",)], extra=dm)
            P.add("act", lambda e: e.copy(SM2[:, 56:57], ONESB[:, 0:1]), reads=rk + [("ONESB",)], writes=[("bscr", "act")], extra=dm)
            mset("dve", SM2[:, 58:59], 0.0, [("bscr", "dve")], reads=rk)
            P.ops["dve"][-1].deps.extend(dm)
            mset("pool", SM2[:, 59:60], 0.0, [("bscr", "pool")], reads=rk)
            P.ops["pool"][-1].deps.extend(dm)
            P.add("sp", lambda e: e.nop(), reads=rk, extra=dm)
            for d in dm:
                d.need_inc = True

        C1, C2, PILO = 6.28125, 2 * PI - 6.28125, 3.1415925

        def sincos(dst_sin, dst_cos, src, rk, wk_s, wk_c, tmp, tk, ki, kk):
            ts("dve", ki, src, 1.0 / (2 * PI), None, ALU.mult, None, rk, [kk])
            stt("dve", tmp, ki, -C1, src, ALU.mult, ALU.add, [kk] + rk, [tk])
            stt("dve", tmp, ki, -C2, tmp, ALU.mult, ALU.add, [kk, tk], [tk])
            ts("dve", tmp, tmp, -PILO, PILO, ALU.max, ALU.min, [tk], [tk])
            act(dst_sin, tmp, AF.Sin, [tk], wk_s)
            ts("dve", ki, src, 1.0 / (2 * PI), 0.25, ALU.mult, ALU.add, rk, [kk])
            stt("dve", tmp, ki, -C1, src, ALU.mult, ALU.add, [kk] + rk, [tk])
            stt("dve", tmp, ki, -C2, tmp, ALU.mult, ALU.add, [kk, tk], [tk])
            ts("dve", tmp, tmp, PI / 2, PILO, ALU.add, ALU.min, [tk], [tk])
            act(dst_cos, tmp, AF.Sin, [tk], wk_c)

        def cmul(eng, ore, oim, are, aim, bre, bim, t1, t2, rk, wre, wim, tk1, tk2):
            tt(eng, t1, are, bre, ALU.mult, rk, [tk1])
            tt(eng, t2, aim, bim, ALU.mult, rk, [tk2])
            tt(eng, ore, t1, t2, ALU.subtract, [tk1, tk2], wre)
            tt(eng, t1, are, bim, ALU.mult, rk, [tk1])
            tt(eng, t2, aim, bre, ALU.mult, rk, [tk2])
            tt(eng, oim, t1, t2, ALU.add, [tk1, tk2], wim)

        def mult_mask(di, ti, t1, t2, out, kd, ki_, k1, k2, kout, eng="dve"):
            ts(eng, ti, di, 3, None, ALU.bitwise_and, None, [kd], [ki_])
            ts(eng, t1, ti, 0, None, ALU.is_equal, None, [ki_], [k1])
            ts(eng, t2, di, 512, None, ALU.is_le, None, [kd], [k2])
            tt(eng, t1, t1, t2, ALU.mult, [k1, k2], [k1])
            ts(eng, t2, di, 128, None, ALU.is_le, None, [kd], [k2])
            tt(eng, t1, t1, t2, ALU.add, [k1, k2], [k1])
            ts(eng, ti, di, 15, None, ALU.bitwise_and, None, [kd], [ki_])
            ts(eng, t2, ti, 0, None, ALU.is_equal, None, [ki_], [k2])
            tt(eng, t1, t1, t2, ALU.add, [k1, k2], [k1])
            ts(eng, t2, di, 0, None, ALU.is_ge, None, [kd], [k2])
            tt(eng, out, t1, t2, ALU.mult, [k1, k2], kout)

        try:
            mset("pool", ONESB[:], 1.0, [("ONESB",)])
            IIv = FD[:].rearrange("p a b -> p (a b)").bitcast(I32)
            MFv = [F_[:].rearrange("p a b -> p (a b)").bitcast(F32) for F_ in (FA, FB, FC)]
            P.add("pool", lambda e: e.iota(IIv[:, 0:128], [[1, 128]], base=0, channel_multiplier=-1), writes=[("II",)])
            ts("dve", IDB[:], IIv[:, 0:128], 0, None, ALU.is_equal, None, [("II",)], [("IDB",)])
            P.add("pool", lambda e: e.iota(IIv[:, 0:2432], [[1, 2432]], base=-384, channel_multiplier=-1), reads=[("IDB",)], writes=[("II",)])
            TIv = FA[:].rearrange("p a b -> p (a b)").bitcast(I32)
            mult_mask(IIv[:, 0:2432], TIv[:, 0:2432], MFv[1][:, 0:2432], MFv[2][:, 0:2432], MASK[:], ("II",), ("MF", 0), ("MF", 1), ("MF", 2), [("MASK",)])
            dump("mask", MASK[:], [("MASK",)])
            P.add("pool", lambda e: e.iota(IIv[:, 64:65], [[0, 1]], base=0, channel_multiplier=1), reads=[("MASK",)], writes=[("IIp",)])
            ts("dve", IIv[:, 65:66], IIv[:, 64:65], 3, None, ALU.bitwise_and, None, [("IIp",)], [("IIp3",)])
            ts("dve", IIv[:, 65:66], IIv[:, 65:66], 3, None, ALU.mult, None, [("IIp3",)], [("IIp3",)])
            P.add("pool", lambda e: e.iota(IIv[:, 0:12], [[-512, 3], [1, 4]], base=2048, channel_multiplier=-4), reads=[("MASK",)], writes=[("II",)])
            P.add("pool", lambda e: e.iota(IIv[:, 12:28], [[-128, 4], [1, 4]], base=512, channel_multiplier=-1), reads=[("MASK",)], writes=[("II",)])
            ts("dve", IIv[:, 0:12], IIv[:, 0:12], IIv[:, 65:66], None, ALU.add, None, [("II",), ("IIp3",)], [("II",)])
            mult_mask(IIv[:, 0:28], TIv[:, 0:28], MFv[1][:, 0:28], MFv[2][:, 0:28], MFv[1][:, 32:60], ("II",), ("MF", 0), ("MF", 1), ("MF", 2), [("MF", 3)])
            cp("dve", MSK_S[:], MFv[1][:, 32:60].rearrange("p (t j) -> p t j", t=7).unsqueeze(3).to_broadcast([128, 7, 4, 8]), [("MF", 3)], [("MSK_S",)])
            P.add("pool", lambda e: e.iota(IIv[0:64, 0:512], [[1, 512]], base=0, channel_multiplier=0), reads=[("MF", 3), ("MSK_S",)], writes=[("II",)])
            P.add("pool", lambda e: e.iota(IIv[0:64, 512:1024], [[0, 512]], base=0, channel_multiplier=1), reads=[("MF", 3), ("MSK_S",)], writes=[("II2",)])
            ci, pi_ = IIv[0:64, 0:512], IIv[0:64, 512:1024]
            cb, cj, pb_, pj = (IIv[0:64, 1024 + 512 * k_:1536 + 512 * k_] for k_ in range(4))
            ts("dve", cb, ci, 5, None, ALU.arith_shift_right, None, [("II",)], [("I", 0)])
            ts("dve", cj, ci, 3, 3, ALU.arith_shift_right, ALU.bitwise_and, [("II",)], [("I", 1)])
            ts("dve", pb_, pi_, 2, None, ALU.arith_shift_right, None, [("II2",)], [("I", 2)])
            ts("dve", pj, pi_, 3, None, ALU.bitwise_and, None, [("II2",)], [("I", 3)])
            m0, m1 = MFv[1][0:64, 0:512], MFv[2][0:64, 0:512]
            tt("dve", m0, cb, pb_, ALU.is_equal, [("I", 0), ("I", 2), ("MF", 3), ("MSK_S",)], [("MF", 1)])
            tt("dve", m1, pj, cj, ALU.is_le, [("I", 1), ("I", 3)], [("MF", 2)])
            tt("dve", m0, m0, m1, ALU.mult, [("MF", 1), ("MF", 2)], [("MF", 1)])
            tt("dve", m1, pj, cj, ALU.is_equal, [("I", 1), ("I", 3)], [("MF", 2)])
            ts("dve", m1, m1, 2.0, 1.0, ALU.mult, ALU.add, [("MF", 2)], [("MF", 2)])
            tt("dve", MSK_N[:], m0, m1, ALU.mult, [("MF", 1), ("MF", 2)], [("MSK_N",)])
            dump("msk_s", MSK_S[:].rearrange("p t j h -> p (t j h)"), [("MSK_S",)])
            dump("msk_n", MSK_N[:], [("MSK_N",)])
            dma("sp", SMALL[:, 0:8], g_in.rearrange("(k p) -> p k", p=128), [], [SQ("gin")], allow_slow_non_contiguous=True)
            dma("sp", SMALL[:, 8:12], d_skip.rearrange("(k p) -> p k", p=128), [], [SQ("dsk")], allow_slow_non_contiguous=True)
            dma("sp", SMALL[:, 12:20], b_glu.rearrange("(k p) -> p k", p=128), [], [SQ("bglu")], allow_slow_non_contiguous=True)
            dma("sp", SMALL[:, 20:24], g_ssm.rearrange("(k p) -> p k", p=128), [], [SQ("gssm")], allow_slow_non_contiguous=True)
            dma("sp", SMALL[:, 24:28], g_attn.rearrange("(k p) -> p k", p=128), [], [SQ("gattn")], allow_slow_non_contiguous=True)
            barrier()
            if stop_after == "p0":
                raise Stop()

            TRB = [PSB, PS[6][:].bitcast(BF16)]
            TRK = [("PSB",), ("PS", 6)]
            rows_of = lambda t_: 128 if t_ < 16 else NS
            x_src = lambda t_: xp[t_ * 128:(t_ + 1) * 128, :] if t_ < 16 else xsm[:, :]
            def p2_gen():
                for t_ in range(NT):
                    r = rows_of(t_)
                    s = t_ % 2
                    dma("sp", XST[:r, s, :], x_src(t_), [], [("XST", s)])
                    act(OST[:r, s, :], XST[:r, s, :], AF.Square, [("XST", s)], [("OST", s), SQ(("ss", t_))],
                        accum_out=SMALL[:r, 32 + t_:33 + t_])
                    ts("dve", SMALL[:r, 49 + t_:50 + t_], SMALL[:r, 32 + t_:33 + t_], 1.0 / D, EPS, ALU.mult, ALU.add,
                       [SQ(("ss", t_))], [SQ(("rs", t_))])
                    act(SMALL[:r, 49 + t_:50 + t_], SMALL[:r, 49 + t_:50 + t_], AF.Sqrt, [SQ(("rs", t_))], [SQ(("rs", t_))])
                    recip(SMALL[:r, 49 + t_:50 + t_], SMALL[:r, 49 + t_:50 + t_], [SQ(("rs", t_))], [SQ(("rs", t_))])
                    act(XSB[:r, s, :], XST[:r, s, :], AF.Copy, [("XST", s), SQ(("rs", t_))], [("XSB", s)],
                        scale=SMALL[:r, 49 + t_:50 + t_])
                    for kc in range(8):
                        tr(TRB[s][:, kc * 128:kc * 128 + r], XSB[:r, s, kc * 128:(kc + 1) * 128], IDB[:r, :r], [("XSB", s), ("IDB",)], [TRK[s]])
                    cp("dve" if t_ % 2 == 0 else "act", XN[:, :, t_ * 128:t_ * 128 + r],
                       TRB[s][:, :].rearrange("p (k t) -> p k t", k=8)[:, :, 0:r],
                       [TRK[s]], [("XN", kc, t_) for kc in range(8)])
                    yield
            p2 = p2_gen()

            def p2_step():
                try:
                    next(p2)
                except StopIteration:
                    pass

            LQ = lambda k: ("LAM", k)
            mset("pool", LAM[:], 0.0, [LQ(k) for k in range(40)] + [LQ((2, 0)), LQ((2, 1))])
            dma("sp", LAM[:, 0, :], a_re.rearrange("(i p) -> p i", p=128), [], [LQ(0)], allow_slow_non_contiguous=True)
            dma("sp", LAM[:, 1, :], a_im.rearrange("(i p) -> p i", p=128), [], [LQ(1)], allow_slow_non_contiguous=True)
            for gl in range(2):
                dma("sp", LAM[gl * 64:(gl + 1) * 64, 2, :], bass.AP(log_dt.tensor, gl, [[0, 64], [2, 16]]), [], [LQ((2, gl))],
                    allow_slow_non_contiguous=True)
            act(LAM[:, 2, :], LAM[:, 2, :], AF.Exp, [LQ((2, 0)), LQ((2, 1))], [LQ(2)])
            tt("dve", LAM[:, 3, :], LAM[:, 0, :], LAM[:, 2, :], ALU.mult, [LQ(0), LQ(2)], [LQ(3)])
            act(LAM[:, 4, :], LAM[:, 3, :], AF.Exp, [LQ(3)], [LQ(4)])
            act(LAM[:, 14, :], LAM[:, 3, :], AF.Exp, [LQ(3)], [LQ(14)], scale=float(L))
            tt("dve", LAM[:, 5, :], LAM[:, 1, :], LAM[:, 2, :], ALU.mult, [LQ(1), LQ(2)], [LQ(5)])
            ts("dve", LAM[:, 15, :], LAM[:, 5, :], float(L), None, ALU.mult, None, [LQ(5)], [LQ(15)])
            sincos(LAM[:, 8, :], LAM[:, 7, :], LAM[:, 5, :], [LQ(5)], [LQ(8)], [LQ(7)], LAM[:, 6, :], LQ(6),
                   LAM[:, 39, :].bitcast(I32), LQ(39))
            tt("dve", LAM[:, 9, :], LAM[:, 4, :], LAM[:, 7, :], ALU.mult, [LQ(4), LQ(7)], [LQ(9)])
            tt("dve", LAM[:, 10, :], LAM[:, 4, :], LAM[:, 8, :], ALU.mult, [LQ(4), LQ(8)], [LQ(10)])
            tt("dve", LAM[:, 11, :], LAM[:, 0, :], LAM[:, 0, :], ALU.mult, [LQ(0)], [LQ(11)])
            tt("dve", LAM[:, 6, :], LAM[:, 1, :], LAM[:, 1, :], ALU.mult, [LQ(1)], [LQ(6)])
            tt("dve", LAM[:, 11, :], LAM[:, 11, :], LAM[:, 6, :], ALU.add, [LQ(11), LQ(6)], [LQ(11)])
            recip(LAM[:, 11, :], LAM[:, 11, :], [LQ(11)], [LQ(11)])
            ts("dve", LAM[:, 6, :], LAM[:, 9, :], -1.0, None, ALU.add, None, [LQ(9)], [LQ(6)])
            tt("dve", LAM[:, 12, :], LAM[:, 6, :], LAM[:, 0, :], ALU.mult, [LQ(6), LQ(0)], [LQ(12)])
            tt("dve", LAM[:, 38, :], LAM[:, 10, :], LAM[:, 1, :], ALU.mult, [LQ(10), LQ(1)], [LQ(38)])
            tt("dve", LAM[:, 12, :], LAM[:, 12, :], LAM[:, 38, :], ALU.add, [LQ(12), LQ(38)], [LQ(12)])
            tt("dve", LAM[:, 12, :], LAM[:, 12, :], LAM[:, 11, :], ALU.mult, [LQ(12), LQ(11)], [LQ(12)])
            tt("dve", LAM[:, 13, :], LAM[:, 10, :], LAM[:, 0, :], ALU.mult, [LQ(10), LQ(0)], [LQ(13)])
            tt("dve", LAM[:, 38, :], LAM[:, 6, :], LAM[:, 1, :], ALU.mult, [LQ(6), LQ(1)], [LQ(38)])
            tt("dve", LAM[:, 13, :], LAM[:, 13, :], LAM[:, 38, :], ALU.subtract, [LQ(13), LQ(38)], [LQ(13)])
            tt("dve", LAM[:, 13, :], LAM[:, 13, :], LAM[:, 11, :], ALU.mult, [LQ(13), LQ(11)], [LQ(13)])
            mset("dve", LAM[:, 16, :], 1.0, [LQ(16)])
            mset("dve", LAM[:, 17, :], 0.0, [LQ(17)])
            for k in range(1, L + 1):
                cmul("dve", LAM[:, 16 + 2 * k, :], LAM[:, 17 + 2 * k, :], LAM[:, 14 + 2 * k, :], LAM[:, 15 + 2 * k, :],
                     LAM[:, 9, :], LAM[:, 10, :], LAM[:, 38, :], LAM[:, 39, :],
                     [LQ(14 + 2 * k), LQ(15 + 2 * k), LQ(9), LQ(10)], [LQ(16 + 2 * k)], [LQ(17 + 2 * k)], LQ(38), LQ(39))
                p2_step()
            dump("lam", LAM[:, :, :], [LQ(k) for k in range(40)])
            if stop_after == "lam":
                raise Stop()
            BNf = FC[:].rearrange("p a b -> p (a b)").bitcast(F32)
            BPf = FB[:].rearrange("p a b -> p (a b)")
            EPF1 = EPB[:].rearrange("p a b -> p (a b)").bitcast(F32)
            Braw = EPF1[:, 0:512].rearrange("p (a i c) -> p a i c", a=2, i=16)
            Btmp = EPF1[:, 512:1024].rearrange("p (a i c) -> p a i c", a=2, i=16)
            Bk = BNf[:, 0:L * 512].rearrange("p (k a i c) -> p k a i c", k=L, a=2, i=16)
            Bpad = BPf[:, 0:L * 1024].rearrange("p (k a f r c) -> p k a f r c", k=L, a=2, f=4, r=4)
            dma("sp", Braw[:, 0], b_re.rearrange("(i p) c -> p i c", p=128), [], [("Braw", 0)])
            dma("sp", Braw[:, 1], b_im.rearrange("(i p) c -> p i c", p=128), [], [("Braw", 1)])
            bc = lambda row: LAM[:, row, :].unsqueeze(2).to_broadcast([128, 16, 16])
            cmul("dve", Bk[:, 0, 0], Bk[:, 0, 1], Braw[:, 0], Braw[:, 1], bc(12), bc(13), Btmp[:, 0], Btmp[:, 1],
                 [("Braw", 0), ("Braw", 1), LQ(12), LQ(13)], [("Bk", 0, 0)], [("Bk", 0, 1)], ("Btmp", 0), ("Btmp", 1))
            for k in range(1, L):
                cmul("dve", Bk[:, k, 0], Bk[:, k, 1], Bk[:, k - 1, 0], Bk[:, k - 1, 1], bc(9), bc(10), Btmp[:, 0], Btmp[:, 1],
                     [("Bk", k - 1, 0), ("Bk", k - 1, 1), LQ(9), LQ(10)], [("Bk", k, 0)], [("Bk", k, 1)], ("Btmp", 0), ("Btmp", 1))
                p2_step()
            mset("pool", BPf[:, 0:L * 1024], 0.0, [("Bpad",)])
            for gl in range(2):
                for k in range(L):
                    cp("pool", Bpad[gl * 64:(gl + 1) * 64, k, :, :, :, gl * 16:(gl + 1) * 16].rearrange("p a f r c -> p a (f r) c"),
                       Bk[gl * 64:(gl + 1) * 64, k], [("Bk", k, 0), ("Bk", k, 1), ("Bpad",)], [("Bpad", gl, k)])
            gi_ = 0
            for f in range(4):
                for kk in range(0, L, 4):
                    tb, tk_ = TRB[gi_ % 2], TRK[gi_ % 2]
                    gi_ += 1
                    for k in range(kk, kk + 4):
                        for ri in range(2):
                            col = (k - kk) * 2 + ri
                            tr(tb[:, col * 128:(col + 1) * 128], Bpad[:, k, ri, f].rearrange("p r c -> p (r c)"), IDB[:],
                               [("Bpad", 0, k), ("Bpad", 1, k), ("IDB",)], [tk_])
                    cp("act", BKT[:, f, kk:kk + 4, :, :].rearrange("p k a c -> p (k a c)"), tb[:, :], [tk_],
                       [("BKT", f, k) for k in range(kk, kk + 4)])
            if stop_after == "B":
                dump("bkt", BKT.rearrange("p f k a c -> p (f k a c)"), [("BKT", f, k) for f in range(4) for k in range(L)])
                raise Stop()
            CN = WST[:, 0:1024].rearrange("p (a t c) -> p a t c", a=2, t=4)
            CNB = WB[:, 0:1024].rearrange("p (a t c) -> p a t c", a=2, t=4)
            CT = TMPF[:, 0:2, :].rearrange("p a (t c) -> p a t c", t=4)
            mset("pool", WST[:, 0:1024], 0.0, [("CN",)])
            for ri, csrc in enumerate((c_re, c_im)):
                cv4 = csrc.rearrange("(t r l c) n -> t r l c n", t=4, r=4, l=2)
                for r in range(4):
                    for gl in range(2):
                        p0 = r * 32 + gl * 16
                        dma("sp", CN[p0:p0 + 16, ri, :, gl * 64:(gl + 1) * 64], cv4[:, r, gl].rearrange("t c n -> c t n"),
                            [("CN",)], [("CNd", ri, r, gl)])
            cp("dve", CNB[:], CN[:], [("CN",)] + [("CNd", ri, r, gl) for ri in range(2) for r in range(4) for gl in range(2)], [("CNB",)])
            for ri in range(2):
                for rt in range(4):
                    col = ri * 4 + rt
                    tr(PSB[:, col * 128:(col + 1) * 128], CNB[:, ri, rt, :], IDB[:], [("CNB",), ("IDB",)], [("PSB",)])
            cp("dve", CT.rearrange("p a t c -> p (a t c)"), PSB[:, :], [("PSB",)], [("CT",)])
            CTK = [("CT",)]
            CTv = lambda ri: CT[:, ri].rearrange("p t (r c) -> p (t r) c", r=4)
            bc32 = lambda row: LAM[:, row, :].unsqueeze(2).to_broadcast([128, 16, 32])
            T1 = TMPF[:, 2, :].rearrange("p (i c) -> p i c", i=16)
            T2 = TMPF[:, 3, :].rearrange("p (i c) -> p i c", i=16)
            for var in range(L + 1):
                lr, li = 16 + 2 * var, 17 + 2 * var
                tt("dve", T1, CTv(0), bc32(lr), ALU.mult, CTK + [LQ(lr)], [("T1",)])
                tt("dve", T2, CTv(1), bc32(li), ALU.mult, CTK + [LQ(li)], [("T2",)])
                tt("dve", CK[:, :, var, 0, :], T1, T2, ALU.subtract, [("T1",), ("T2",)], [("CK", var, 0)])
                tt("dve", T1, CTv(0), bc32(li), ALU.mult, CTK + [LQ(li)], [("T1",)])
                tt("dve", T2, CTv(1), bc32(lr), ALU.mult, CTK + [LQ(lr)], [("T2",)])
                stt("dve", CK[:, :, var, 1, :], T1, -1.0, T2, ALU.mult, ALU.subtract, [("T1",), ("T2",)], [("CK", var, 1)])
                p2_step()
            dump("bkt", BKT.rearrange("p f k a c -> p (f k a c)"), [("BKT", f, k) for f in range(4) for k in range(L)])
            dump("ck", CK.rearrange("p i v a c -> p (i v a c)"), [("CK", v, a) for v in range(L + 1) for a in range(2)])
            for _ in p2:
                pass
            barrier()
            if stop_after == "params":
                raise Stop()

            dump("xn", XN[:, :, :], [("XN", kc, t_) for kc in range(8) for t_ in range(NT)])

            TBK = [(0, 512), (512, 512), (1024, 512), (1536, 512), (2048, NS)]
            tiles_of_block = lambda bi: list(range(4 * bi, 4 * bi + 4)) if bi < 4 else [16]
            psrr = [0]

            def next_ps(n=3):
                i = psrr[0] % n
                psrr[0] += 1
                return i
            WBv = WB[:, :].rearrange("p (s k n) -> p s k n", s=2, k=8)

            def load_w_group(gi, slot):
                dma("sp", WST[:, :].rearrange("p (k n) -> p k n", k=8),
                    w_in[:, gi * 512:(gi + 1) * 512].rearrange("(k p) n -> p k n", p=128), [], [("WST",)])
                for kc in range(8):
                    if kc % 2 == 0:
                        act(WBv[:, slot, kc, :], WST[:, kc * 512:(kc + 1) * 512], AF.Copy, [("WST",), SQ("gin")], [("WB", slot, kc)],
                            scale=SMALL[:, kc:kc + 1])
                    else:
                        ts("dve", WBv[:, slot, kc, :], WST[:, kc * 512:(kc + 1) * 512], SMALL[:, kc:kc + 1], None, ALU.mult, None,
                           [("WST",), SQ("gin")], [("WB", slot, kc)])

            def fm_keys(name, m, bi):
                return [(name, m, t_) for t_ in tiles_of_block(bi)]

            def proj_fm(slot, evac):
                for m in range(4):
                    for bi, (c0, n) in enumerate(TBK):
                        pi = next_ps()
                        for kc in range(8):
                            mm(PS[pi][:, 0:n], WBv[:, slot, kc, m * 128:(m + 1) * 128], XN[:, kc, c0:c0 + n], kc == 0, kc == 7,
                               [("WB", slot, kc)] + [("XN", kc, t_) for t_ in tiles_of_block(bi)], [("PS", pi)])
                        evac(m, bi, pi, c0, n)

            load_w_group(0, 0)
            load_w_group(1, 1)
            proj_fm(0, lambda m, bi, pi, c0, n: cp("dve", FA[:, m, c0:c0 + n], PS[pi][:, 0:n], [("PS", pi)], fm_keys("FA", m, bi)))
            proj_fm(1, lambda m, bi, pi, c0, n: act(FB[:, m, c0:c0 + n], PS[pi][:, 0:n], AF.Silu, [("PS", pi)], fm_keys("FB", m, bi)))
            dump("xs", FA[:, :, :], [("FA", m, t_) for m in range(4) for t_ in range(NT)])
            dump("zs", FB[:, :, :], [("FB", m, t_) for m in range(4) for t_ in range(NT)])
            barrier()
            if stop_after == "inproj1":
                raise Stop()

            YB = [PS[3], PS[4], PS[5], PS[6]]
            ybank = lambda t: (YB[t // 2], (t % 2) * J)
            xsv = lambda f: FA[:, f, 0:T].rearrange("p (j s) -> p j s", s=L)
            S5K = lambda k: ("S5F", k)
            FAK = lambda f: [("FA", f, t_) for t_ in range(16)]

            def gelu_to(yv, ykeys, dst, dkeys, s, n):
                x2 = TMPF[:, 1, s * 256:s * 256 + n]
                u = TMPF[:, 2, s * 256:s * 256 + n]
                k2, ku = ("TMPF", 1, s), ("TMPF", 2, s)
                act(x2, yv, AF.Square, ykeys, [k2])
                ts("pool", x2, x2, 0.044715, 1.0, ALU.mult, ALU.add, [k2], [k2])
                tt("pool", u, x2, yv, ALU.mult, [k2] + ykeys, [ku])
                act(u, u, AF.Sigmoid, [ku], [ku], scale=1.5957691216)
                tt("dve", dst, u, yv, ALU.mult, [ku] + ykeys, dkeys)

            JI = TMPF[:, 3, 0:J].bitcast(I32)
            P.add("pool", lambda e: e.iota(JI, [[1, J]], base=1, channel_multiplier=0), writes=[("JI",)])
            cp("dve", S5F[:, 11, :], JI, [("JI",)], [S5K(11)])
            XSD = WB[:, :].rearrange("p (f s j) -> p f s j", f=4, s=L)
            for f in range(4):
                cp("act" if f % 2 == 0 else "dve", XSD[:, f], xsv(f).rearrange("p j s -> p s j"), FAK(f), [("XSD", f)])
            PTB = [PT, WST[:, 0:2048].bitcast(BF16).rearrange("p (t a j) -> p t a j", t=L, a=2)]
            XCB = [XC, WST[:, 2048:2304].bitcast(BF16).rearrange("p (a j) -> p a j", a=2)]
            SBF = [S5F[:, 4:6, :], WST[:, 2304:2816].rearrange("p (a j) -> p a j", a=2)]
            PIi = TMPF[:, 3, 300:301].bitcast(I32)
            PI2 = TMPF[:, 3, 304:305].bitcast(I32)
            P.add("pool", lambda e: e.iota(PIi, [[0, 1]], base=0, channel_multiplier=1), writes=[("PIi",)])
            ts("dve", PI2, PIi, 5, None, ALU.arith_shift_right, None, [("PIi",)], [("PI2",)])
            for r_ in range(4):
                ts("dve", SMALL[:, 108 + r_:109 + r_], PI2, r_, None, ALU.is_equal, None, [("PI2",)], [SQ(("rm", r_))])
            XSZ = WST[:, 2816:3840].bitcast(BF16).rearrange("p (s j) -> p s j", s=L)

            def a_step(i):
                f, r = i // 4, i % 4
                pb = i % 2
                ts("dve", XSZ.rearrange("p s j -> p (s j)"), XSD[:, f].rearrange("p s j -> p (s j)"), SMALL[:, 108 + r:109 + r], None, ALU.mult, None,
                   [("XSD", f), SQ(("rm", r))], [("XSZ",)])
                for t in range(L):
                    pi = next_ps()
                    for ri in range(2):
                        for s in range(t + 1):
                            mm(PS[pi][:, ri * J:(ri + 1) * J], BKT[:, f, t - s, ri, :], XSZ[:, s, :],
                               s == 0, s == t, [("BKT", f, t - s), ("XSZ",)], [("PS", pi)])
                    cp("act", PTB[pb][:, t].rearrange("p a j -> p (a j)"), PS[pi][:, 0:2 * J], [("PS", pi)], [("PT", pb, t)])
                    if t == L - 1:
                        cp("act", SBF[pb].rearrange("p a j -> p (a j)"), PS[pi][:, 0:2 * J], [("PS", pi)], [("SBF", pb, 0), ("SBF", pb, 1)])

            EPBF = EPB[:].rearrange("p a b -> p (a b)").bitcast(F32).rearrange("p (k j) -> p k j", j=J)
            VAI = VA[:].rearrange("p a b c -> p (a b c)").bitcast(I32)[:, 0:J]
            TABK = lambda pb, w: ("TAB", pb, w)

            def sincos_ops(i):
                pb = i % 2
                cosT, sinT = EPBF[:, 2 * pb, :], EPBF[:, 2 * pb + 1, :]
                arg, tmp, ki = EPBF[:, 4, :], EPBF[:, 5, :], VAI
                ka, kt, kk = ("SC", "arg"), ("SC", "tmp"), ("SC", "ki")
                return [
                    lambda: ts("dve", arg, S5F[:, 11, :], LAM[:, 15, i:i + 1], None, ALU.mult, None, [S5K(11), LQ(15)], [ka]),
                    lambda: ts("dve", ki, arg, 1.0 / (2 * PI), None, ALU.mult, None, [ka], [kk]),
                    lambda: stt("dve", tmp, ki, -C1, arg, ALU.mult, ALU.add, [kk, ka], [kt]),
                    lambda: stt("dve", tmp, ki, -C2, tmp, ALU.mult, ALU.add, [kk, kt], [kt]),
                    lambda: ts("dve", tmp, tmp, -PILO, PILO, ALU.max, ALU.min, [kt], [kt]),
                    lambda: act(sinT, tmp, AF.Sin, [kt], [TABK(pb, 1)]),
                    lambda: ts("dve", ki, arg, 1.0 / (2 * PI), 0.25, ALU.mult, ALU.add, [ka], [kk]),
                    lambda: stt("dve", tmp, ki, -C1, arg, ALU.mult, ALU.add, [kk, ka], [kt]),
                    lambda: stt("dve", tmp, ki, -C2, tmp, ALU.mult, ALU.add, [kk, kt], [kt]),
                    lambda: ts("dve", tmp, tmp, PI / 2, PILO, ALU.add, ALU.min, [kt], [kt]),
                    lambda: act(cosT, tmp, AF.Sin, [kt], [TABK(pb, 0)]),
                ]

            def rot_ops(i):
                pb = i % 2
                cosT, sinT = EPBF[:, 2 * pb, :], EPBF[:, 2 * pb + 1, :]
                kc_, ks_ = TABK(pb, 0), TABK(pb, 1)
                Sre, Sim = SBF[pb][:, 0, :], SBF[pb][:, 1, :]
                kre, kim = ("SBF", pb, 0), ("SBF", pb, 1)
                a, b_, cre, cim, wre, wim = (S5F[:, k, :] for k in (6, 7, 8, 9, 10, 3))
                r8 = LAM[:, 14, i:i + 1].to_broadcast([128, J])
                return [
                    lambda: tt("dve", a, cosT, Sre, ALU.mult, [kc_, kre], [S5K(6)]),
                    lambda: tt("dve", b_, sinT, Sim, ALU.mult, [ks_, kim], [S5K(7)]),
                    lambda: tt("dve", cre, a, b_, ALU.add, [S5K(6), S5K(7)], [S5K(8)]),
                    lambda: tt("dve", a, cosT, Sim, ALU.mult, [kc_, kim], [S5K(6)]),
                    lambda: tt("dve", b_, sinT, Sre, ALU.mult, [ks_, kre], [S5K(7)]),
                    lambda: tt("dve", cim, a, b_, ALU.subtract, [S5K(6), S5K(7)], [S5K(9)]),
                    lambda: P.add("dve", lambda e: e.tensor_tensor_scan(wre, r8, cre, 0.0, ALU.mult, ALU.add), reads=[S5K(8), LQ(14)], writes=[S5K(10)]),
                    lambda: P.add("dve", lambda e: e.tensor_tensor_scan(wim, r8, cim, 0.0, ALU.mult, ALU.add), reads=[S5K(9), LQ(14)], writes=[S5K(3)]),
                    lambda: tt("dve", a, cosT, wre, ALU.mult, [kc_, S5K(10)], [S5K(6)]),
                    lambda: tt("dve", b_, sinT, wim, ALU.mult, [ks_, S5K(3)], [S5K(7)]),
                    lambda: tt("dve", cre, a, b_, ALU.subtract, [S5K(6), S5K(7)], [S5K(8)]),
                    lambda: tt("dve", a, sinT, wre, ALU.mult, [ks_, S5K(10)], [S5K(6)]),
                    lambda: tt("dve", b_, cosT, wim, ALU.mult, [kc_, S5K(3)], [S5K(7)]),
                    lambda: tt("dve", cim, a, b_, ALU.add, [S5K(6), S5K(7)], [S5K(9)]),
                    lambda: cp("act", SM2[:, i:i + 1], cre[:, J - 1:J], [S5K(8)], [("SM2", i)]),
                    lambda: cp("act", SM2[:, 16 + i:17 + i], cim[:, J - 1:J], [S5K(9)], [("SM2", 16 + i)]),
                    lambda: mset("pool", XCB[pb][:, :, 0:1], 0.0, [("XC", pb, "z")]),
                    lambda: cp("act", XCB[pb][:, 0, 1:J], cre[:, 0:J - 1], [S5K(8)], [("XC", pb, 0)]),
                    lambda: cp("act", XCB[pb][:, 1, 1:J], cim[:, 0:J - 1], [S5K(9)], [("XC", pb, 1)]),
                ]

            def chain(i):
                la = rot_ops(i)
                lb = sincos_ops(i + 1) if i + 1 < 16 else []
                for k in range(max(len(la), len(lb))):
                    if k < len(la):
                        la[k]()
                    if k < len(lb):
                        lb[k]()

            def d_step(i):
                f, r = i // 4, i % 4
                pb = i % 2
                for t in range(L):
                    bank, co = ybank(t)
                    for ri in range(2):
                        mm(bank[32 * r:32 * r + 32, co:co + J], CK[:, i, 0, ri, :], PTB[pb][:, t, ri, :], ri == 0, False,
                           [("CK", 0, ri), ("PT", pb, t)], [("YB", t // 2, r)], tile_position=(0, 32 * r))
                    for ri in range(2):
                        mm(bank[32 * r:32 * r + 32, co:co + J], CK[:, i, t + 1, ri, :], XCB[pb][:, ri, :], False, ri == 1,
                           [("CK", t + 1, ri), ("XC", pb, 0), ("XC", pb, 1), ("XC", pb, "z")], [("YB", t // 2, r)], tile_position=(0, 32 * r))
                if r == 3:
                    for t in range(L):
                        bank, co = ybank(t)
                        s = t % 2
                        yk = ("TMPF", 0, s)
                        yv = TMPF[:, 0, s * 256:s * 256 + J]
                        stt("dve", yv, xsv(f)[:, :, t], SMALL[:, 8 + f:9 + f], bank[:, co:co + J], ALU.mult, ALU.add,
                            FAK(f) + [SQ("dsk")] + [("YB", t // 2, r_) for r_ in range(4)], [yk])
                        gelu_to(yv, [yk], FC[:, f, 0:T].rearrange("p (j s) -> p j s", s=L)[:, :, t], [("FC", f, t_) for t_ in range(16)], s, J)

            for th in sincos_ops(0):
                th()
            a_step(0)
            for i in range(16):
                if i + 1 < 16:
                    a_step(i + 1)
                chain(i)
                d_step(i)
            dma("sp", sre_p.rearrange("(i p) -> p i", p=128), SM2[:, 0:16], [("SM2", i) for i in range(16)], [("out", "sre_p")],
                allow_slow_non_contiguous=True)
            dma("sp", sim_p.rearrange("(i p) -> p i", p=128), SM2[:, 16:32], [("SM2", 16 + i) for i in range(16)], [("out", "sim_p")],
                allow_slow_non_contiguous=True)
            barrier()
            if stop_after == "s5p":
                if "gel" in dbg_out:
                    dma("sp", dbg_out["gel"][:, :, 0:T], FC[:, :, 0:T], [], [("dbg", "gel")])
                raise Stop()
            SBU = STG[:, 3072:5120].rearrange("p (r a f n) -> p r a f n", r=4, a=2, f=4)
            SA = S5F[:, 0:2, :]
            SBs = S5F[:, 2:4, :]
            SXS = S5F[:, 8:12, :].rearrange("p a j -> p (a j)").bitcast(BF16).rearrange("p (a i n) -> p a i n", a=2, i=16)
            for r in range(4):
                for ri in range(2):
                    for f in range(4):
                        bk = 3 + r
                        q_ = ri * 4 + f
                        mm(PS[bk][:, q_ * 64:q_ * 64 + 64], BKT[32 * r:32 * r + 32, f, 0, ri, :], FA[32 * r:32 * r + 32, f, T:TA],
                           True, True, [("BKT", f, 0), ("FA", f, 16)], [("PS", bk)], tile_position=(32 * r, 0))
            for bk in range(4):
                cp("act" if bk % 2 == 0 else "dve", STG[:, 3072 + bk * 512:3072 + (bk + 1) * 512], PS[3 + bk][:, :], [("PS", 3 + bk)], [("SBU", bk)])
            if stop_after == "r1a":
                raise Stop()
            TK = lambda row: [("TMPF", row, 0), ("TMPF", row, 1)]
            NAT = TMPF[:, 0, :].rearrange("p (q c) -> p q c", q=4)
            HILO = TMPF[:, 1, :].bitcast(BF16).rearrange("p (a q c) -> p a q c", a=2, q=4)
            HIF = TMPF[:, 2, :]

            def split_hilo(src_flat, src_keys):
                cp("dve", HILO[:, 0].rearrange("p q c -> p (q c)"), src_flat, src_keys, [("HILO", 0)])
                cp("dve", HIF, HILO[:, 0].rearrange("p q c -> p (q c)"), [("HILO", 0)], TK(2))
                tt("dve", HILO[:, 1].rearrange("p q c -> p (q c)"), src_flat, HIF, ALU.subtract, src_keys + TK(2), [("HILO", 1)])
                for a_ in range(2):
                    for q_ in range(4):
                        tr(PSB[:, (a_ * 4 + q_) * 128:(a_ * 4 + q_ + 1) * 128], HILO[:, a_, q_, :], IDB[:], [("HILO", a_), ("IDB",)], [("PSB",)])

            def merge_hilo(dst_flat, dst_keys):
                cp("dve", dst_flat, PSB[:, 0:512], [("PSB",)], dst_keys)
                tt("dve", dst_flat, dst_flat, PSB[:, 512:1024], ALU.add, [("PSB",)] + dst_keys, dst_keys)

            for ri, src in enumerate((st_re, st_im)):
                for u in range(2):
                    dma("sp", NAT[:, ri * 2 + u, :], src[8 * u:8 * u + 8].rearrange("b (i p) -> (b i) p", p=128), [], [("NAT", ri * 2 + u)])
            split_hilo(TMPF[:, 0, :], [("NAT", q_) for q_ in range(4)])
            SAK = [("SA", 0), ("SA", 1)]
            merge_hilo(SA.rearrange("p a n -> p (a n)"), SAK)
            if stop_after == "r1b":
                raise Stop()
            lamb = lambda row: LAM[:, row, :].rearrange("p (f r) -> p f r", f=4).unsqueeze(1).to_broadcast([128, 16, 4, 4])
            v3 = lambda ap: ap.rearrange("p (b f r) -> p b f r", b=16, f=4)
            SBUK = [("SBU", bk) for bk in range(4)]
            cur, nxt = SA, SBs
            curk, nxtk = "SA", "SB"
            for j in range(4):
                ck_keys = [(curk, 0), (curk, 1)]
                cmul("dve", v3(S5F[:, 6, :]), v3(S5F[:, 7, :]), v3(cur[:, 0, :]), v3(cur[:, 1, :]), lamb(9), lamb(10),
                     v3(S5F[:, 4, :]), v3(S5F[:, 5, :]), ck_keys + [LQ(9), LQ(10)], [S5K(6)], [S5K(7)], S5K(4), S5K(5))
                for ri in range(2):
                    buj = SBU[:, :, ri].rearrange("p r f (b j) -> p b f r j", j=4)[:, :, :, :, j]
                    tt("dve", v3(nxt[:, ri, :]), v3(S5F[:, 6 + ri, :]), buj, ALU.add, [S5K(6 + ri)] + SBUK, [(nxtk, ri)])
                    cp("act", SXS[:, ri].rearrange("p (f r) (b j) -> p b f r j", j=4, f=4)[:, :, :, :, j], v3(nxt[:, ri, :]), [(nxtk, ri)], [("SXS", j)])
                cur, nxt, curk, nxtk = nxt, cur, nxtk, curk
            if stop_after == "r1c":
                raise Stop()
            split_hilo(cur.rearrange("p a n -> p (a n)"), [(curk, 0), (curk, 1)])
            merge_hilo(TMPF[:, 0, :], TK(0))
            for ri, dst in enumerate((sre_s, sim_s)):
                for u in range(2):
                    dma("sp", dst[8 * u:8 * u + 8].rearrange("b (i p) -> (b i) p", p=128), NAT[:, ri * 2 + u, :], TK(0), [("out", "ss", ri, u)])
            if stop_after == "r1d":
                raise Stop()
            SXSK = [("SXS", j) for j in range(4)]
            for i in [4 * f_ + r_ for r_ in range(4) for f_ in range(4)]:
                f, r = i // 4, i % 4
                for ri in range(2):
                    mm(PS[0][32 * r:32 * r + 32, f * 64:(f + 1) * 64], CK[:, i, 0, ri, :], SXS[:, ri, i, :], ri == 0, ri == 1,
                       [("CK", 0, ri)] + SXSK, [("PS", 0)], tile_position=(0, 32 * r))
            for f in range(4):
                s = f % 2
                yk = ("TMPF", 0, s)
                yv = TMPF[:, 0, s * 256:s * 256 + NS]
                stt("dve", yv, FA[:, f, T:TA], SMALL[:, 8 + f:9 + f], PS[0][:, f * 64:(f + 1) * 64], ALU.mult, ALU.add,
                    [("FA", f, 16), SQ("dsk"), ("PS", 0)], [yk])
                gelu_to(yv, [yk], FC[:, f, T:TA], [("FC", f, 16)], s, NS)
            if stop_after == "s5s":
                dump("gel", FC[:, :, :], [("FC", f, t_) for f in range(4) for t_ in range(NT)])
                raise Stop()

            dma("sp", WST[:, :].rearrange("p (k n) -> p k n", k=4), w_glu.rearrange("(k p) n -> p k n", p=128), [], [("WST",)])
            WG = WB[:, 0:4096].rearrange("p (k n) -> p k n", k=4)
            for kc in range(4):
                cp("act" if kc % 2 == 0 else "dve", WG[:, kc, :], WST[:, kc * 1024:(kc + 1) * 1024], [("WST",)], [("WB", 0, 2 * kc), ("WB", 0, 2 * kc + 1)])
            cnt = 0
            for m in range(4):
                for bi, (c0, n) in enumerate(TBK):
                    pa = next_ps()
                    for kc in range(4):
                        mm(PS[pa][:, 0:n], WG[:, kc, m * 128:(m + 1) * 128], FC[:, kc, c0:c0 + n], kc == 0, kc == 3,
                           [("WB", 0, 2 * kc), ("WB", 0, 2 * kc + 1)] + fm_keys("FC", kc, bi), [("PS", pa)])
                    pb = next_ps()
                    for kc in range(4):
                        mm(PS[pb][:, 0:n], WG[:, kc, 512 + m * 128:512 + (m + 1) * 128], FC[:, kc, c0:c0 + n], kc == 0, kc == 3,
                           [("WB", 0, 2 * kc), ("WB", 0, 2 * kc + 1)] + fm_keys("FC", kc, bi), [("PS", pb)])
                    pp = cnt % 2
                    cnt += 1
                    sgk = [("TMPF", 2 + pp, 0), ("TMPF", 2 + pp, 1)]
                    t1k = [("TMPF", pp, 0), ("TMPF", pp, 1)]
                    act(TMPF[:, 2 + pp, 0:n], PS[pb][:, 0:n], AF.Sigmoid, [("PS", pb), SQ("bglu")], sgk, bias=SMALL[:, 16 + m:17 + m])
                    stt("dve", TMPF[:, pp, 0:n], PS[pa][:, 0:n], SMALL[:, 12 + m:13 + m], TMPF[:, 2 + pp, 0:n], ALU.add, ALU.mult,
                        [("PS", pa), SQ("bglu")] + sgk, t1k)
                    tt("pool", FA[:, m, c0:c0 + n], TMPF[:, pp, 0:n], FB[:, m, c0:c0 + n], ALU.mult, t1k + fm_keys("FB", m, bi), fm_keys("FA", m, bi))
            dump("ys", FA[:, :, :], [("FA", m, t_) for m in range(4) for t_ in range(NT)])
            barrier()
            if stop_after == "glu":
                raise Stop()

            mset("pool", VA[:, 0, :, 64:128], 1.0, [("VAones", 0)])
            mset("pool", VA[:, 1, :, 0:64], 1.0, [("VAones", 1)])
            OSTH = lambda s2, h: OST[:, s2, h * 512:(h + 1) * 512]

            def proj_tm(slot, dst_p, dst_s, to_vb):
                for t_ in range(NT):
                    r = rows_of(t_)
                    pi = next_ps()
                    for kc in range(8):
                        mm(PS[pi][:r, :], XN[:, kc, t_ * 128:t_ * 128 + r], WBv[:, slot, kc, :], kc == 0, kc == 7,
                           [("WB", slot, kc), ("XN", kc, t_)], [("PS", pi)])
                    s2, h = (t_ // 2) % 2, t_ % 2
                    cp("act", OSTH(s2, h)[:r], PS[pi][:r, :], [("PS", pi)], [("OST", s2, h)])
                    if to_vb:
                        cp("pool", VB[:r, t_, 0:512], OSTH(s2, h)[:r], [("OST", s2, h)], [("VB", t_)])
                    dstap = dst_p[t_ * 128:(t_ + 1) * 128, :] if t_ < 16 else dst_s[:, :]
                    dma("pool", dstap, OSTH(s2, h)[:r], [("OST", s2, h)], [("out", "kv", id(dst_p), t_)])

            load_w_group(2, 0)
            load_w_group(3, 1)
            proj_fm(0, lambda m, bi, pi, c0, n: act(FB[:, m, c0:c0 + n], PS[pi][:, 0:n], AF.Copy, [("PS", pi)], fm_keys("FB", m, bi), scale=0.125))
            if stop_after == "r3a":
                raise Stop()
            load_w_group(4, 0)
            proj_fm(1, lambda m, bi, pi, c0, n: cp("dve", FC[:, m, c0:c0 + n], PS[pi][:, 0:n], [("PS", pi)], fm_keys("FC", m, bi)))
            if stop_after == "r3b":
                raise Stop()
            proj_tm(1, k_p, k_s, False)
            if stop_after == "r3c":
                raise Stop()
            load_w_group(5, 1)
            proj_tm(0, v_p, v_s, True)
            if stop_after == "r3d":
                raise Stop()
            proj_fm(1, lambda m, bi, pi, c0, n: act(FD[:, m, c0:c0 + n], PS[pi][:, 0:n], AF.Silu, [("PS", pi)], fm_keys("FD", m, bi)))
            dump("q", FB[:, :, :], [("FB", m, t_) for m in range(4) for t_ in range(NT)])
            dump("kt", FC[:, :, :], [("FC", m, t_) for m in range(4) for t_ in range(NT)])
            barrier()
            if stop_after == "inproj2":
                raise Stop()

            SBK = [0, 1, 4, 5]
            KTZ = WB[:, :].rearrange("p (s n) -> p s n", s=4)[:, 0:2, 0:T]
            mset("pool", KTZ[64:128, 0, :], 0.0, [("KTZz", 0)])
            mset("pool", KTZ[0:64, 1, :], 0.0, [("KTZz", 1)])
            LA = 3
            its = []
            octr = 0
            for h in range(8):
                for qc in range(4):
                    ob = 2 + octr % 2
                    octr += 1
                    nj = 4 * qc + 4
                    for j in range(nj):
                        its.append((h, qc, j, nj, ob))

            def stage_a(n):
                h, qc, j, nj, ob = its[n]
                c, hb = h // 2, 64 * (h % 2)
                qb0 = max(4 * qc, j)
                nn = (4 * qc + 4 - qb0) * 128
                col0 = qb0 * 128
                e = n % 4
                sbk = SBK[e]
                if j == 0 and qc == 0:
                    cp("pool", VA[:, h % 2, :, hb:hb + 64], VB[:, 0:16, h * 64:(h + 1) * 64], [("VB", t_) for t_ in range(16)] + [("VAones", h % 2)], [("VA", h % 2)])
                    cp("act", KTZ[hb:hb + 64, h % 2, :], FC[hb:hb + 64, c, 0:T], [("FC", c, t_) for t_ in range(16)] + [("KTZz", h % 2)], [("KTZ", h % 2)])
                qk = [("FB", c, t_) for t_ in range(qb0, 4 * qc + 4)]
                mm(PS[sbk][:, 0:nn], KTZ[:, h % 2, j * 128:(j + 1) * 128], FB[:, c, col0:col0 + nn], True, True,
                   [("KTZ", h % 2), ("KTZz", h % 2)] + qk, [("PS", sbk)])
                act(EPB[:, e, 0:nn], PS[sbk][:, 0:nn], AF.Exp, [("PS", sbk)], [("EPB", e)])
                mc = (qb0 - j + 3) * 128
                tt("dve" if n % 2 == 0 else "pool", EPB[:, 4 + e, 0:nn], EPB[:, e, 0:nn], MASK[:, mc:mc + nn], ALU.mult,
                   [("EPB", e), ("MASK",)], [("EPB", 4 + e)])

            def stage_b(n):
                h, qc, j, nj, ob = its[n]
                c, hb = h // 2, 64 * (h % 2)
                dn = 64 - hb
                qb0 = max(4 * qc, j)
                nn = (4 * qc + 4 - qb0) * 128
                qoff = (qb0 - 4 * qc) * 128
                e = n % 4
                mm(PS[ob][:, qoff:qoff + nn], VA[:, h % 2, j, :], EPB[:, 4 + e, 0:nn], j == 0, j == nj - 1,
                   [("VA", h % 2), ("VAones", h % 2), ("EPB", 4 + e)], [("PS", ob)])
                if j == nj - 1:
                    pp = qc % 2
                    rk_, tk_ = [("TMPF", 2 * pp, 0), ("TMPF", 2 * pp, 1)], [("TMPF", 2 * pp + 1, 0), ("TMPF", 2 * pp + 1, 1)]
                    recip(TMPF[hb:hb + 64, 2 * pp, :], PS[ob][dn:dn + 64, :], [("PS", ob)], rk_)
                    tt("dve", TMPF[hb:hb + 64, 2 * pp + 1, :], PS[ob][hb:hb + 64, :], TMPF[hb:hb + 64, 2 * pp, :], ALU.mult, [("PS", ob)] + rk_, tk_)
                    tt("pool", XN[hb:hb + 64, c, qc * 512:(qc + 1) * 512], TMPF[hb:hb + 64, 2 * pp + 1, :], FD[hb:hb + 64, c, qc * 512:(qc + 1) * 512], ALU.mult,
                       tk_ + [("FD", c, t_) for t_ in range(4 * qc, 4 * qc + 4)], [("XN", c, t_, hb) for t_ in range(4 * qc, 4 * qc + 4)])

            for n in range(len(its) + LA):
                if n < len(its):
                    stage_a(n)
                if n - LA >= 0:
                    stage_b(n - LA)
            barrier()
            if stop_after == "attn":
                if "ya" in dbg_out:
                    dma("sp", dbg_out["ya"][:, :, 0:T], XN[:, 0:4, 0:T], [], [("dbg", "ya")])
                raise Stop()
            TRB2 = [PSB, PS[1][:].bitcast(BF16)]
            TRK2 = [("PSB",), ("PS", 1)]
            mset("pool", TMPF[:, 0:2, :], 0.0, [("QB",)])
            for c in range(4):
                for hh in range(2):
                    cp("dve", QB[hh * 64:(hh + 1) * 64, c, :].rearrange("p (b j h) -> p b j h", b=NB, j=4)[:, :, :, 2 * c + hh],
                       FB[hh * 64:(hh + 1) * 64, c, T:TA].rearrange("p (b j) -> p b j", j=4), [("QB",), ("FB", c, 16)], [("QBd", c, hh)])
            QBK = [("QB",)] + [("QBd", c, hh) for c in range(4) for hh in range(2)]
            for c in range(4):
                mm(PS[1][0:64, :], FC[:, c, T:TA], QB[:, c, :], c == 0, c == 3, [("FC", c, 16)] + QBK, [("PS", 1)])
            act(EPB[0:64, 2, :], PS[1][0:64, :], AF.Exp, [("PS", 1)], [("EPB", 2)])
            tt("dve", EPB[0:64, 3, :], EPB[0:64, 2, :], MSK_N[:, :], ALU.mult, [("EPB", 2), ("MSK_N",)], [("EPB", 3)])
            for c in range(4):
                mm(PS[2 + c][:, :], VB[0:64, 16, c * 128:(c + 1) * 128], EPB[0:64, 3, :], True, False, [("VB", 16), ("EPB", 3)], [("PS", 2 + c)])
            mm(PS[6][:, :], ONESB[0:64, :], EPB[0:64, 3, :], True, False, [("ONESB",), ("EPB", 3)], [("PS", 6)])
            trc = [0]
            KT2 = WST[:, 0:3584].bitcast(BF16).rearrange("p (s c k) -> p s c k", s=2, c=4)

            def cache_dma(b, kv, src):
                for tl in range(3):
                    srcap = src[b].rearrange("(g x) c -> g x c", x=16)[32 * tl:32 * tl + 32, 0:4, :]
                    dma("pool", KST[:, kv, tl, :], srcap, [], [("KST", kv, tl)])
                dma("pool", KST[:, kv, 3:7, :], src[b, 1536:2048, :].rearrange("(t p) c -> p t c", p=128), [], [("KST", kv, 3)])

            def st_t(b):
                cache_dma(b, 0, ck)
                KK = [("KST", 0, tl) for tl in range(4)]
                for c in range(4):
                    tb, tk_ = TRB2[trc[0] % 2], TRK2[trc[0] % 2]
                    trc[0] += 1
                    for tl in range(7):
                        tr(tb[:, tl * 128:(tl + 1) * 128], KST[:, 0, tl, c * 128:(c + 1) * 128], IDB[:], KK + [("IDB",)], [tk_])
                    cp("act" if c % 2 == 0 else "dve", KT2[:, b % 2, c, :], tb[:, 0:896], [tk_], [("KT", b % 2, c)])

            def st_c(b):
                cache_dma(b, 1, cv)
                VK = [("KST", 1, tl) for tl in range(4)]
                for tl in range(7):
                    for c in range(4):
                        mm(PS[0][:, tl * 32:(tl + 1) * 32], KT2[:, b % 2, c, tl * 128:(tl + 1) * 128], QB[:, c, b * 32:(b + 1) * 32], c == 0, c == 3,
                           [("KT", b % 2, c)] + QBK, [("PS", 0)])
                act(EPB[:, 0, 0:224], PS[0][:, 0:224], AF.Exp, [("PS", 0)], [("EPB", 0)])
                tt("dve", EPB[:, 1, 0:224], EPB[:, 0, 0:224], MSK_S[:].rearrange("p t j h -> p (t j h)"), ALU.mult, [("EPB", 0), ("MSK_S",)], [("EPB", 1)])
                last_b = (b == NB - 1)
                for tl in range(7):
                    for c in range(4):
                        mm(PS[2 + c][:, b * 32:(b + 1) * 32], KST[:, 1, tl, c * 128:(c + 1) * 128], EPB[:, 1, tl * 32:(tl + 1) * 32], False,
                           last_b and tl == 6, VK + [("EPB", 1)], [("PS", 2 + c)])
                    mm(PS[6][:, b * 32:(b + 1) * 32], ONESB[:, :], EPB[:, 1, tl * 32:(tl + 1) * 32], False, last_b and tl == 6,
                       [("ONESB",), ("EPB", 1)], [("PS", 6)])

            st_t(0)
            for b in range(NB):
                if b + 1 < NB:
                    st_t(b + 1)
                st_c(b)
            for c in range(4):
                for hh in range(2):
                    rows = slice(hh * 64, (hh + 1) * 64)
                    hsel = lambda ap: ap.rearrange("p (b j h) -> p b j h", b=NB, j=4)[:, :, :, 2 * c + hh]
                    rd = TMPF[rows, 2, 0:64].rearrange("p (b j) -> p b j", j=4)
                    t1 = TMPF[rows, 3, 0:64].rearrange("p (b j) -> p b j", j=4)
                    recip(rd, hsel(PS[6][rows, :]), [("PS", 6)], [("TMPF", 2, 0)])
                    tt("dve", t1, hsel(PS[2 + c][rows, :]), rd, ALU.mult, [("PS", 2 + c), ("TMPF", 2, 0)], [("TMPF", 3, 0)])
                    tt("pool", XN[rows, c, T:TA].rearrange("p (b j) -> p b j", j=4), t1, FD[rows, c, T:TA].rearrange("p (b j) -> p b j", j=4), ALU.mult,
                       [("TMPF", 3, 0), ("FD", c, 16)], [("XN", c, 16, hh * 64)])
            dump("ya", XN[:, 0:4, :], [("XN", c, t_, hb) for c in range(4) for t_ in range(NT) for hb in (0, 64)])
            barrier()
            if stop_after == "sattn":
                raise Stop()

            dma("sp", GFB[:, :], bass.AP(g_fin.tensor, 0, [[0, 128], [1, D]]), [], [("GFB",)])
            WO = WB[:, :].rearrange("p (k n) -> p k n", k=8)
            for half in range(2):
                dma("sp", WST[:, :].rearrange("p (k n) -> p k n", k=4), w_out[half * 512:(half + 1) * 512, :].rearrange("(k p) n -> p k n", p=128),
                    [], [("WST",)])
                for kc in range(4):
                    if kc % 2 == 0:
                        act(WO[:, half * 4 + kc, :], WST[:, kc * 1024:(kc + 1) * 1024], AF.Copy, [("WST",), SQ("gssm"), SQ("gattn")], [("WO", half * 4 + kc)],
                            scale=SMALL[:, 20 + half * 4 + kc:21 + half * 4 + kc])
                    else:
                        ts("dve", WO[:, half * 4 + kc, :], WST[:, kc * 1024:(kc + 1) * 1024], SMALL[:, 20 + half * 4 + kc:21 + half * 4 + kc], None, ALU.mult, None,
                           [("WST",), SQ("gssm"), SQ("gattn")], [("WO", half * 4 + kc)])
            def op_a(t_):
                r = rows_of(t_)
                s = t_ % 2
                cs = slice(t_ * 128, t_ * 128 + r)
                dma("sp", XST[:r, s, :], x_src(t_), [], [("XST", s)])
                yak = [("XN", kc, t_, hb) for kc in range(4) for hb in (0, 64)]
                ysk = [("FA", kc, t_) for kc in range(4)]
                e0, e1 = 2 * s, 2 * s + 1
                tt("pool", EPB[:, e0, :].rearrange("p (k t) -> p k t", k=4)[:, :, 0:r], FA[:, :, cs], FA[:, :, cs], ALU.mult, ysk, [("EPB", e0)])
                tt("pool", EPB[:, e1, :].rearrange("p (k t) -> p k t", k=4)[:, :, 0:r], XN[:, 0:4, cs], XN[:, 0:4, cs], ALU.mult, yak, [("EPB", e1)])

            def op_mm(t_):
                r = rows_of(t_)
                s = t_ % 2
                cs = slice(t_ * 128, t_ * 128 + r)
                yak = [("XN", kc, t_, hb) for kc in range(4) for hb in (0, 64)]
                ysk = [("FA", kc, t_) for kc in range(4)]
                for br in range(2):
                    for kc in range(4):
                        mm(PS[4][:r, br:br + 1], EPB[:, 2 * s + br, kc * 128:kc * 128 + r], ONESB[:, 0:1], kc == 0, kc == 3,
                           [("EPB", 2 * s + br), ("ONESB",)], [("PS", 4)])
                for half in range(2):
                    for kc in range(4):
                        mm(PS[half][:r, :], FA[:, kc, cs], WO[:, kc, half * 512:(half + 1) * 512], kc == 0, kc == 3,
                           ysk + [("WO", kc)], [("PS", half)])
                    for kc in range(4):
                        mm(PS[2 + half][:r, :], XN[:, kc, cs], WO[:, 4 + kc, half * 512:(half + 1) * 512], kc == 0, kc == 3,
                           yak + [("WO", 4 + kc)], [("PS", 2 + half)])

            def op_b1(t_):
                r = rows_of(t_)
                s = t_ % 2
                rsk = SQ(("rs2", s))
                rs2 = SMALL[:r, 70 + 16 * s:72 + 16 * s]
                ts("dve", rs2, PS[4][:r, 0:2], 1.0 / 512, EPS, ALU.mult, ALU.add, [("PS", 4)], [rsk])
                act(rs2, rs2, AF.Sqrt, [rsk], [rsk])
                recip(rs2, rs2, [rsk], [rsk])
                for half in range(2):
                    hs = slice(half * 512, (half + 1) * 512)
                    stt("dve", OST[:r, s, hs], PS[half][:r, :], SMALL[:r, 70 + 16 * s:71 + 16 * s], XST[:r, s, hs], ALU.mult, ALU.add,
                        [("PS", half), rsk, ("XST", s)], [("OST", s, half)])
                    stt("dve", XST[:r, s, hs], PS[2 + half][:r, :], SMALL[:r, 71 + 16 * s:72 + 16 * s], OST[:r, s, hs], ALU.mult, ALU.add,
                        [("PS", 2 + half), rsk, ("OST", s, half)], [("XST", s)])

            def op_b2(t_):
                r = rows_of(t_)
                s = t_ % 2
                fk = SQ(("fs", s))
                fsc = SMALL[:r, 104 + s:105 + s]
                act(OST[:r, s, :], XST[:r, s, :], AF.Square, [("XST", s)], [("OST", s, 0), ("OST", s, 1), fk], accum_out=fsc)
                ts("dve", fsc, fsc, 1.0 / D, EPS, ALU.mult, ALU.add, [fk], [fk])
                act(fsc, fsc, AF.Sqrt, [fk], [fk])
                recip(fsc, fsc, [fk], [fk])
                stt("dve", OST[:r, s, :], XST[:r, s, :], fsc, GFB[:r, :], ALU.mult, ALU.mult, [("XST", s), fk, ("GFB",)], [("OST", s, 0), ("OST", s, 1)])
                dstap = y_p[t_ * 128:(t_ + 1) * 128, :] if t_ < 16 else y_s[:, :]
                dma("sp", dstap, OST[:r, s, :], [("OST", s, 0), ("OST", s, 1)], [("out", "y", t_)])

            op_a(0)
            op_mm(0)
            for t_ in range(NT):
                if t_ + 1 < NT:
                    op_a(t_ + 1)
                op_b1(t_)
                if t_ + 1 < NT:
                    op_mm(t_ + 1)
                op_b2(t_)
        except Stop:
            pass

        with (
            nc.semaphore("s_pe") as s_pe, nc.semaphore("s_act") as s_act, nc.semaphore("s_dve") as s_dve,
            nc.semaphore("s_pool") as s_pool, nc.semaphore("s_sp") as s_sp,
        ):
            dstack = ExitStack()
            with dstack:
                dsems = [dstack.enter_context(nc.semaphore("dma%d" % i)) for i in range(P.ndma)]
                with nc.Block() as block:
                    P.emit(nc, block, {"pe": s_pe, "act": s_act, "dve": s_dve, "pool": s_pool, "sp": s_sp}, dsems)
    return nc


_NC_CACHE = {}


def _shard_inputs(inp):
    maps = []
    f32 = lambda a: np.ascontiguousarray(a, dtype=np.float32)
    for c in range(NCORES):
        sl = slice(NB * c, NB * (c + 1))
        maps.append({
            "xp": f32(inp["x_prompt"][c]), "xsm": f32(inp["x_sample"][sl].reshape(NS, D)),
            "ck": f32(inp["cache_k"][0, sl].reshape(NB, 2048, 512)), "cv": f32(inp["cache_v"][0, sl].reshape(NB, 2048, 512)),
            "st_re": f32(inp["state_ssm_re"][0, sl].reshape(NB, 2048)), "st_im": f32(inp["state_ssm_im"][0, sl].reshape(NB, 2048)),
            "g_in": f32(inp["norm_in_g"][0]), "w_in": f32(inp["w_in"][0]),
            "a_re": f32(inp["ssm_A_re"][0].reshape(2048)), "a_im": f32(inp["ssm_A_im"][0].reshape(2048)),
            "log_dt": f32(inp["ssm_log_dt"][0]),
            "b_re": f32(inp["ssm_B_re"][0].reshape(2048, 16)), "b_im": f32(inp["ssm_B_im"][0].reshape(2048, 16)),
            "c_re": f32(inp["ssm_C_re"][0].reshape(512, 64)), "c_im": f32(inp["ssm_C_im"][0].reshape(512, 64)),
            "d_skip": f32(inp["ssm_D"][0]), "w_glu": f32(inp["w_glu"][0]), "b_glu": f32(inp["b_glu"][0]),
            "g_ssm": f32(inp["norm_ssm_g"][0]), "g_attn": f32(inp["norm_attn_g"][0]), "w_out": f32(inp["w_out"][0]),
            "g_fin": f32(inp["final_norm_g"]),
        })
    return maps


def kernel(**inputs):
    inp = {k: np.asarray(v) for k, v in inputs.items()}
    if "nc" not in _NC_CACHE:
        _NC_CACHE["nc"] = build_program()
    nc = _NC_CACHE["nc"]
    maps = _shard_inputs(inp)
    res = run_bass_kernel_spmd(nc, maps, core_ids=list(range(NCORES)))
    R = res.results
    cat = lambda name: np.stack([np.asarray(R[c][name], dtype=np.float32) for c in range(NCORES)], 0)
    y_prompt = cat("y_p").reshape(8, T, D)
    y_sample = cat("y_s").reshape(8 * NB, 4, D)
    k_prompt = cat("k_p").reshape(1, 8, T, 8, 64)
    v_prompt = cat("v_p").reshape(1, 8, T, 8, 64)
    sre_prompt = cat("sre_p").reshape(1, 8, 32, 64)
    sim_prompt = cat("sim_p").reshape(1, 8, 32, 64)
    k_sample = cat("k_s").reshape(1, 8 * NB, 4, 8, 64)
    v_sample = cat("v_s").reshape(1, 8 * NB, 4, 8, 64)
    sre_sample = cat("sre_s").reshape(1, 8 * NB, 32, 64)
    sim_sample = cat("sim_s").reshape(1, 8 * NB, 32, 64)
    return (y_prompt, y_sample, k_prompt, v_prompt, sre_prompt, sim_prompt, k_sample, v_sample, sre_sample, sim_sample)
```

```python
import math
import numpy as np
import concourse.bass as bass
import concourse.mybir as mybir
from concourse.bass_utils import run_bass_kernel_spmd

F32 = mybir.dt.float32
BF16 = mybir.dt.bfloat16
I32 = mybir.dt.int32
AF = mybir.ActivationFunctionType
ALU = mybir.AluOpType
AX = mybir.AxisListType

NCORES = 8
D = 1024
T = 2048
NS = 64
TA = T + NS
NB = 16
L = 8
J = T // L
EPS = 1e-6
PI = math.pi
DEBUG = {}


class Op:
    __slots__ = ("stream", "fn", "deps", "is_dma", "slot", "val", "need_inc", "name")


class Prog:
    STREAMS = ["pe", "act", "dve", "pool", "sp"]

    def __init__(self, ndma=40):
        self.ops = {s: [] for s in self.STREAMS}
        self.st = {}
        self.ndma = ndma
        self.dma_last = [None] * ndma
        self.dma_cnt = [0] * ndma
        self.rr = 0
        self.rr_sw = 0
        self.nhw = ndma - 8
        self.all_dma_out = []

    def add(self, stream, fn, reads=(), writes=(), dma=False, name="", extra=()):
        op = Op()
        op.stream, op.fn, op.is_dma, op.need_inc, op.name = stream, fn, dma, False, name
        op.slot, op.val = None, 0
        deps = {}
        for k in reads:
            s = self.st.get(k)
            if s is not None and s[0] is not None:
                deps[id(s[0])] = (s[0], True)
        for k in writes:
            s = self.st.get(k)
            if s is not None:
                if s[0] is not None and id(s[0]) not in deps:
                    deps[id(s[0])] = (s[0], False)
                for r in s[1].values():
                    if id(r) not in deps:
                        deps[id(r)] = (r, False)
        final = []
        for d, raw in deps.values():
            if d is op:
                continue
            if (not d.is_dma) and (not dma) and d.stream == stream and stream == 'pe' and not raw:
                continue
            final.append(d)
        for d in extra:
            if d is not None:
                final.append(d)
        if dma:
            if stream == "pool":
                slot = self.nhw + self.rr_sw
                self.rr_sw = (self.rr_sw + 1) % (self.ndma - self.nhw)
            else:
                slot = self.rr
                self.rr = (self.rr + 1) % self.nhw
            prev = self.dma_last[slot]
            if prev is not None:
                final.append(prev)
            self.dma_cnt[slot] += 1
            op.slot, op.val = slot, 16 * self.dma_cnt[slot]
            self.dma_last[slot] = op
        for d in final:
            d.need_inc = True
        op.deps = final
        for k in reads:
            s = self.st.setdefault(k, [None, {}])
            key = (stream, id(op)) if dma else stream
            s[1][key] = op
        for k in writes:
            self.st[k] = [op, {}]
        self.ops[stream].append(op)
        return op

    def emit(self, nc, block, sems, dma_sems):
        for s in self.STREAMS:
            c = 0
            for op in self.ops[s]:
                if not op.is_dma and op.need_inc:
                    c += 1
                    op.val = c
        engs = {"pe": block.tensor, "act": block.scalar, "dve": block.vector, "pool": block.gpsimd, "sp": block.sync}

        def run_stream(s):
            def body(eng):
                waited = {}
                for op in self.ops[s]:
                    for d in op.deps:
                        if d.is_dma:
                            key, sem, val = ("d", d.slot), dma_sems[d.slot], d.val
                        else:
                            key, sem, val = ("c", d.stream), sems[d.stream], d.val
                        if waited.get(key, 0) < val:
                            eng.wait_ge(sem, val)
                            waited[key] = val
                    ins = op.fn(eng)
                    if op.is_dma:
                        ins.then_inc(dma_sems[op.slot], 16)
                    elif op.need_inc:
                        ins.then_inc(sems[s], 1)
                if s == "sp":
                    for slot in range(self.ndma):
                        if self.dma_cnt[slot] and waited.get(("d", slot), 0) < 16 * self.dma_cnt[slot]:
                            eng.wait_ge(dma_sems[slot], 16 * self.dma_cnt[slot])
            return body

        for s in self.STREAMS:
            engs[s](run_stream(s))


def build_program(dbg=None, dbg_bf16=(), stop_after=None, small_cache=False):
    dbg = dbg or {}
    nc = bass.Bass("TRN2", target_bir_lowering=False)
    P = Prog()

    def din(name, shape, dt=F32):
        return nc.dram_tensor(name, list(shape), dt, kind="ExternalInput").ap()

    def dout(name, shape, dt=F32):
        return nc.dram_tensor(name, list(shape), dt, kind="ExternalOutput").ap()

    xp = din("xp", [T, D]); xsm = din("xsm", [NS, D])
    ck = din("ck", [NB, 8 if small_cache else 2048, 512]); cv = din("cv", [NB, 8 if small_cache else 2048, 512])
    st_re = din("st_re", [NB, 2048]); st_im = din("st_im", [NB, 2048])
    g_in = din("g_in", [D]); w_in = din("w_in", [D, 3072])
    a_re = din("a_re", [2048]); a_im = din("a_im", [2048]); log_dt = din("log_dt", [32])
    b_re = din("b_re", [2048, 16]); b_im = din("b_im", [2048, 16])
    c_re = din("c_re", [512, 64]); c_im = din("c_im", [512, 64])
    d_skip = din("d_skip", [512]); w_glu = din("w_glu", [512, 1024]); b_glu = din("b_glu", [1024])
    g_ssm = din("g_ssm", [512]); g_attn = din("g_attn", [512]); w_out = din("w_out", [1024, 1024]); g_fin = din("g_fin", [D])

    y_p = dout("y_p", [T, D]); y_s = dout("y_s", [NS, D])
    k_p = dout("k_p", [T, 512]); v_p = dout("v_p", [T, 512])
    sre_p = dout("sre_p", [2048]); sim_p = dout("sim_p", [2048])
    k_s = dout("k_s", [NS, 512]); v_s = dout("v_s", [NS, 512])
    sre_s = dout("sre_s", [NB, 2048]); sim_s = dout("sim_s", [NB, 2048])
    dbg_out = {k: dout("dbg_" + k, shp, BF16 if k in dbg_bf16 else F32) for k, shp in dbg.items()}

    from contextlib import ExitStack
    es = ExitStack()

    def sb(name, shape, dt):
        return es.enter_context(nc.sbuf_tensor(name, list(shape), dt))

    def ps(name, shape, dt):
        return es.enter_context(nc.psum_tensor(name, list(shape), dt))

    class Stop(Exception):
        pass

    with es:
        XN = sb("XN", [128, 8, TA], BF16)
        FA = sb("FA", [128, 4, TA], BF16)
        FB = sb("FB", [128, 4, TA], BF16)
        FC = sb("FC", [128, 4, TA], BF16)
        FD = sb("FD", [128, 4, TA], BF16)
        VB = sb("VB", [128, 17, 544], BF16)
        VA = sb("VA", [128, 2, 16, 128], BF16)
        WST = sb("WST", [128, 4096], F32)
        WB = sb("WB", [128, 8192], BF16)
        STG = sb("STG", [128, 5120], F32)
        MASK = sb("MASK", [128, 19 * 128], BF16)
        IDB = sb("IDB", [128, 128], BF16)
        ONESB = sb("ONESB", [128, 128], BF16)
        SMALL = sb("SMALL", [128, 128], F32)
        SM2 = sb("SM2", [128, 64], F32)
        EPB = sb("EPB", [128, 8, 512], BF16)
        TMPF = sb("TMPF", [128, 4, 512], F32)
        XC = sb("XC", [128, 2, J], BF16)
        LAM = sb("LAM", [128, 40, 16], F32)
        MSK_S = sb("MSK_S", [128, 7, 4, 8], BF16)
        MSK_N = sb("MSK_N", [64, 512], BF16)
        PS = [ps("PS%d" % i, [128, 512], F32) for i in range(7)]
        PSB = ps("PSB", [128, 1024], BF16)

        XST = STG[:, 0:2048].rearrange("p (s d) -> p s d", s=2)
        XSB = STG[:, 2048:3072].bitcast(BF16).rearrange("p (s d) -> p s d", s=2)
        OST = STG[:, 3072:5120].rearrange("p (s d) -> p s d", s=2)
        S5F = STG[:, 0:3072].rearrange("p (k j) -> p k j", k=12)
        PT = STG[:, 3072:5120].bitcast(BF16).rearrange("p (t a j) -> p t a j", t=L, a=2)
        BKT = FD[:].rearrange("p a b -> p (a b)")[:, 0:4 * L * 2 * 128].rearrange("p (f k a c) -> p f k a c", f=4, k=L, a=2)
        CK = VB[:].rearrange("p a b -> p (a b)")[:, 0:16 * (L + 1) * 64].rearrange("p (i v a c) -> p i v a c", i=16, v=L + 1, a=2)
        GFB = MASK[:, 0:2048].bitcast(F32)
        KST = XN[:, 4:8, :].rearrange("p a b -> p (a b)")[:, 0:7168].rearrange("p (s t c) -> p s t c", s=2, t=7)
        KT = WST[:, 0:1792].bitcast(BF16).rearrange("p (c k) -> p c k", c=4)
        QB = TMPF[:, 0:2, :].rearrange("p a b -> p (a b)").bitcast(BF16).rearrange("p (c n) -> p c n", c=4)
        NT = 17
        SQ = lambda k: ("SMALL", k)

        def dma(q, out, in_, reads, writes, **kw):
            return P.add(q, lambda e: e.dma_start(out=out, in_=in_, **kw), reads=reads, writes=writes, dma=True)

        def act(out, in_, func, reads, writes, **kw):
            return P.add("act", lambda e: e.activation(out=out, in_=in_, func=func, **kw), reads=reads, writes=writes)

        def tt(eng, out, in0, in1, op, reads, writes):
            return P.add(eng, lambda e: e.tensor_tensor(out, in0, in1, op), reads=reads, writes=writes)

        def ts(eng, out, in0, s1, s2, op0, op1, reads, writes):
            if op1 is None:
                return P.add(eng, lambda e: e.tensor_single_scalar(out, in0, s1, op0), reads=reads, writes=writes)
            return P.add(eng, lambda e: e.tensor_scalar(out, in0, s1, s2, op0, op1), reads=reads, writes=writes)

        def stt(eng, out, in0, scalar, in1, op0, op1, reads, writes):
            return P.add(eng, lambda e: e.scalar_tensor_tensor(out, in0, scalar, in1, op0, op1), reads=reads, writes=writes)

        def cp(eng, out, in_, reads, writes):
            if eng == "act":
                return P.add("act", lambda e: e.copy(out, in_), reads=reads, writes=writes)
            return P.add(eng, lambda e: e.tensor_copy(out, in_), reads=reads, writes=writes)

        pe_last = [None, None]

        def mm(out, lhsT, rhs, start, stop, reads, writes, **kw):
            tp = kw.get("tile_position")
            extra = []
            if tp is not None and pe_last[0] is not None and tp != pe_last[0]:
                extra = [pe_last[1]]
            op = P.add("pe", lambda e: e.matmul(out, lhsT, rhs, start=start, stop=stop, **kw), reads=reads, writes=writes, extra=extra)
            pe_last[0], pe_last[1] = tp, op
            return op

        def tr(out, in_, ident, reads, writes):
            op = P.add("pe", lambda e: e.transpose(out, in_, ident), reads=reads, writes=writes)
            pe_last[0], pe_last[1] = None, op
            return op

        def mset(eng, ap, val, writes, reads=()):
            return P.add(eng, lambda e: e.memset(ap, val), reads=reads, writes=writes)

        def recip(out, in_, reads, writes):
            return P.add("dve", lambda e: e.reciprocal(out, in_), reads=reads, writes=writes)

        def dump(name, src, reads):
            if name in dbg_out:
                dma("sp", dbg_out[name], src, reads, [("dbg", name)])

        bar_n = [0]

        def barrier():
            n = bar_n[0]
            bar_n[0] += 1
            P.add("pe", lambda e: e.transpose(PSB[:, 896:1024], IDB[:], IDB[:]), reads=[("IDB",)], writes=[("PSB",), ("bar", n, "pe")])
            P.add("act", lambda e: e.copy(SM2[:, 56:57], ONESB[:, 0:1]), reads=[("ONESB",)], writes=[("bar", n, "act"), ("bscr", "act")])
            mset("dve", SM2[:, 58:59], 0.0, [("bar", n, "dve"), ("bscr", "dve")])
            mset("pool", SM2[:, 59:60], 0.0, [("bar", n, "pool"), ("bscr", "pool")])
            rk = [("bar", n, e_) for e_ in ("pe", "act", "dve", "pool")]
            dm = [d for d in P.dma_last if d is not None]
            P.add("pe", lambda e: e.transpose(PSB[:, 896:1024], IDB[:], IDB[:]), reads=rk + [("IDB",)], writes=[("PSB",)], extra=dm)
            P.add("act", lambda e: e.copy(SM2[:, 56:57], ONESB[:, 0:1]), reads=rk + [("ONESB",)], writes=[("bscr", "act")], extra=dm)
            mset("dve", SM2[:, 58:59], 0.0, [("bscr", "dve")], reads=rk)
            P.ops["dve"][-1].deps.extend(dm)
            mset("pool", SM2[:, 59:60], 0.0, [("bscr", "pool")], reads=rk)
            P.ops["pool"][-1].deps.extend(dm)
            P.add("sp", lambda e: e.nop(), reads=rk, extra=dm)
            for d in dm:
                d.need_inc = True

        C1, C2, PILO = 6.28125, 2 * PI - 6.28125, 3.1415925

        def sincos(dst_sin, dst_cos, src, rk, wk_s, wk_c, tmp, tk, ki, kk):
            ts("dve", ki, src, 1.0 / (2 * PI), None, ALU.mult, None, rk, [kk])
            stt("dve", tmp, ki, -C1, src, ALU.mult, ALU.add, [kk] + rk, [tk])
            stt("dve", tmp, ki, -C2, tmp, ALU.mult, ALU.add, [kk, tk], [tk])
            ts("dve", tmp, tmp, -PILO, PILO, ALU.max, ALU.min, [tk], [tk])
            act(dst_sin, tmp, AF.Sin, [tk], wk_s)
            ts("dve", ki, src, 1.0 / (2 * PI), 0.25, ALU.mult, ALU.add, rk, [kk])
            stt("dve", tmp, ki, -C1, src, ALU.mult, ALU.add, [kk] + rk, [tk])
            stt("dve", tmp, ki, -C2, tmp, ALU.mult, ALU.add, [kk, tk], [tk])
            ts("dve", tmp, tmp, PI / 2, PILO, ALU.add, ALU.min, [tk], [tk])
            act(dst_cos, tmp, AF.Sin, [tk], wk_c)

        def cmul(eng, ore, oim, are, aim, bre, bim, t1, t2, rk, wre, wim, tk1, tk2):
            tt(eng, t1, are, bre, ALU.mult, rk, [tk1])
            tt(eng, t2, aim, bim, ALU.mult, rk, [tk2])
            tt(eng, ore, t1, t2, ALU.subtract, [tk1, tk2], wre)
            tt(eng, t1, are, bim, ALU.mult, rk, [tk1])
            tt(eng, t2, aim, bre, ALU.mult, rk, [tk2])
            tt(eng, oim, t1, t2, ALU.add, [tk1, tk2], wim)

        def mult_mask(di, ti, t1, t2, out, kd, ki_, k1, k2, kout, eng="dve"):
            ts(eng, ti, di, 3, None, ALU.bitwise_and, None, [kd], [ki_])
            ts(eng, t1, ti, 0, None, ALU.is_equal, None, [ki_], [k1])
            ts(eng, t2, di, 512, None, ALU.is_le, None, [kd], [k2])
            tt(eng, t1, t1, t2, ALU.mult, [k1, k2], [k1])
            ts(eng, t2, di, 128, None, ALU.is_le, None, [kd], [k2])
            tt(eng, t1, t1, t2, ALU.add, [k1, k2], [k1])
            ts(eng, ti, di, 15, None, ALU.bitwise_and, None, [kd], [ki_])
            ts(eng, t2, ti, 0, None, ALU.is_equal, None, [ki_], [k2])
            tt(eng, t1, t1, t2, ALU.add, [k1, k2], [k1])
            ts(eng, t2, di, 0, None, ALU.is_ge, None, [kd], [k2])
            tt(eng, out, t1, t2, ALU.mult, [k1, k2], kout)

        try:
            mset("pool", ONESB[:], 1.0, [("ONESB",)])
            IIv = FD[:].rearrange("p a b -> p (a b)").bitcast(I32)
            MFv = [F_[:].rearrange("p a b -> p (a b)").bitcast(F32) for F_ in (FA, FB, FC)]
            P.add("pool", lambda e: e.iota(IIv[:, 0:128], [[1, 128]], base=0, channel_multiplier=-1), writes=[("II",)])
            ts("dve", IDB[:], IIv[:, 0:128], 0, None, ALU.is_equal, None, [("II",)], [("IDB",)])
            P.add("pool", lambda e: e.iota(IIv[:, 0:2432], [[1, 2432]], base=-384, channel_multiplier=-1), reads=[("IDB",)], writes=[("II",)])
            TIv = FA[:].rearrange("p a b -> p (a b)").bitcast(I32)
            mult_mask(IIv[:, 0:2432], TIv[:, 0:2432], MFv[1][:, 0:2432], MFv[2][:, 0:2432], MASK[:], ("II",), ("MF", 0), ("MF", 1), ("MF", 2), [("MASK",)])
            dump("mask", MASK[:], [("MASK",)])
            P.add("pool", lambda e: e.iota(IIv[:, 64:65], [[0, 1]], base=0, channel_multiplier=1), reads=[("MASK",)], writes=[("IIp",)])
            ts("dve", IIv[:, 65:66], IIv[:, 64:65], 3, None, ALU.bitwise_and, None, [("IIp",)], [("IIp3",)])
            ts("dve", IIv[:, 65:66], IIv[:, 65:66], 3, None, ALU.mult, None, [("IIp3",)], [("IIp3",)])
            P.add("pool", lambda e: e.iota(IIv[:, 0:12], [[-512, 3], [1, 4]], base=2048, channel_multiplier=-4), reads=[("MASK",)], writes=[("II",)])
            P.add("pool", lambda e: e.iota(IIv[:, 12:28], [[-128, 4], [1, 4]], base=512, channel_multiplier=-1), reads=[("MASK",)], writes=[("II",)])
            ts("dve", IIv[:, 0:12], IIv[:, 0:12], IIv[:, 65:66], None, ALU.add, None, [("II",), ("IIp3",)], [("II",)])
            mult_mask(IIv[:, 0:28], TIv[:, 0:28], MFv[1][:, 0:28], MFv[2][:, 0:28], MFv[1][:, 32:60], ("II",), ("MF", 0), ("MF", 1), ("MF", 2), [("MF", 3)])
            cp("dve", MSK_S[:], MFv[1][:, 32:60].rearrange("p (t j) -> p t j", t=7).unsqueeze(3).to_broadcast([128, 7, 4, 8]), [("MF", 3)], [("MSK_S",)])
            P.add("pool", lambda e: e.iota(IIv[0:64, 0:512], [[1, 512]], base=0, channel_multiplier=0), reads=[("MF", 3), ("MSK_S",)], writes=[("II",)])
            P.add("pool", lambda e: e.iota(IIv[0:64, 512:1024], [[0, 512]], base=0, channel_multiplier=1), reads=[("MF", 3), ("MSK_S",)], writes=[("II2",)])
            ci, pi_ = IIv[0:64, 0:512], IIv[0:64, 512:1024]
            cb, cj, pb_, pj = (IIv[0:64, 1024 + 512 * k_:1536 + 512 * k_] for k_ in range(4))
            ts("dve", cb, ci, 5, None, ALU.arith_shift_right, None, [("II",)], [("I", 0)])
            ts("dve", cj, ci, 3, 3, ALU.arith_shift_right, ALU.bitwise_and, [("II",)], [("I", 1)])
            ts("dve", pb_, pi_, 2, None, ALU.arith_shift_right, None, [("II2",)], [("I", 2)])
            ts("dve", pj, pi_, 3, None, ALU.bitwise_and, None, [("II2",)], [("I", 3)])
            m0, m1 = MFv[1][0:64, 0:512], MFv[2][0:64, 0:512]
            tt("dve", m0, cb, pb_, ALU.is_equal, [("I", 0), ("I", 2), ("MF", 3), ("MSK_S",)], [("MF", 1)])
            tt("dve", m1, pj, cj, ALU.is_le, [("I", 1), ("I", 3)], [("MF", 2)])
            tt("dve", m0, m0, m1, ALU.mult, [("MF", 1), ("MF", 2)], [("MF", 1)])
            tt("dve", m1, pj, cj, ALU.is_equal, [("I", 1), ("I", 3)], [("MF", 2)])
            ts("dve", m1, m1, 2.0, 1.0, ALU.mult, ALU.add, [("MF", 2)], [("MF", 2)])
            tt("dve", MSK_N[:], m0, m1, ALU.mult, [("MF", 1), ("MF", 2)], [("MSK_N",)])
            dump("msk_s", MSK_S[:].rearrange("p t j h -> p (t j h)"), [("MSK_S",)])
            dump("msk_n", MSK_N[:], [("MSK_N",)])
            dma("sp", SMALL[:, 0:8], g_in.rearrange("(k p) -> p k", p=128), [], [SQ("gin")], allow_slow_non_contiguous=True)
            dma("sp", SMALL[:, 8:12], d_skip.rearrange("(k p) -> p k", p=128), [], [SQ("dsk")], allow_slow_non_contiguous=True)
            dma("sp", SMALL[:, 12:20], b_glu.rearrange("(k p) -> p k", p=128), [], [SQ("bglu")], allow_slow_non_contiguous=True)
            dma("sp", SMALL[:, 20:24], g_ssm.rearrange("(k p) -> p k", p=128), [], [SQ("gssm")], allow_slow_non_contiguous=True)
            dma("sp", SMALL[:, 24:28], g_attn.rearrange("(k p) -> p k", p=128), [], [SQ("gattn")], allow_slow_non_contiguous=True)
            barrier()
            if stop_after == "p0":
                raise Stop()

            TRB = [PSB, PS[6][:].bitcast(BF16)]
            TRK = [("PSB",), ("PS", 6)]
            rows_of = lambda t_: 128 if t_ < 16 else NS
            x_src = lambda t_: xp[t_ * 128:(t_ + 1) * 128, :] if t_ < 16 else xsm[:, :]
            def p2_gen():
                for t_ in range(NT):
                    r = rows_of(t_)
                    s = t_ % 2
                    dma("sp", XST[:r, s, :], x_src(t_), [], [("XST", s)])
                    act(OST[:r, s, :], XST[:r, s, :], AF.Square, [("XST", s)], [("OST", s), SQ(("ss", t_))],
                        accum_out=SMALL[:r, 32 + t_:33 + t_])
                    ts("dve", SMALL[:r, 49 + t_:50 + t_], SMALL[:r, 32 + t_:33 + t_], 1.0 / D, EPS, ALU.mult, ALU.add,
                       [SQ(("ss", t_))], [SQ(("rs", t_))])
                    act(SMALL[:r, 49 + t_:50 + t_], SMALL[:r, 49 + t_:50 + t_], AF.Sqrt, [SQ(("rs", t_))], [SQ(("rs", t_))])
                    recip(SMALL[:r, 49 + t_:50 + t_], SMALL[:r, 49 + t_:50 + t_], [SQ(("rs", t_))], [SQ(("rs", t_))])
                    act(XSB[:r, s, :], XST[:r, s, :], AF.Copy, [("XST", s), SQ(("rs", t_))], [("XSB", s)],
                        scale=SMALL[:r, 49 + t_:50 + t_])
                    for kc in range(8):
                        tr(TRB[s][:, kc * 128:kc * 128 + r], XSB[:r, s, kc * 128:(kc + 1) * 128], IDB[:r, :r], [("XSB", s), ("IDB",)], [TRK[s]])
                    cp("dve" if t_ % 2 == 0 else "act", XN[:, :, t_ * 128:t_ * 128 + r],
                       TRB[s][:, :].rearrange("p (k t) -> p k t", k=8)[:, :, 0:r],
                       [TRK[s]], [("XN", kc, t_) for kc in range(8)])
                    yield
            p2 = p2_gen()

            def p2_step():
                try:
                    next(p2)
                except StopIteration:
                    pass

            LQ = lambda k: ("LAM", k)
            mset("pool", LAM[:], 0.0, [LQ(k) for k in range(40)] + [LQ((2, 0)), LQ((2, 1))])
            dma("sp", LAM[:, 0, :], a_re.rearrange("(i p) -> p i", p=128), [], [LQ(0)], allow_slow_non_contiguous=True)
            dma("sp", LAM[:, 1, :], a_im.rearrange("(i p) -> p i", p=128), [], [LQ(1)], allow_slow_non_contiguous=True)
            for gl in range(2):
                dma("sp", LAM[gl * 64:(gl + 1) * 64, 2, :], bass.AP(log_dt.tensor, gl, [[0, 64], [2, 16]]), [], [LQ((2, gl))],
                    allow_slow_non_contiguous=True)
            act(LAM[:, 2, :], LAM[:, 2, :], AF.Exp, [LQ((2, 0)), LQ((2, 1))], [LQ(2)])
            tt("dve", LAM[:, 3, :], LAM[:, 0, :], LAM[:, 2, :], ALU.mult, [LQ(0), LQ(2)], [LQ(3)])
            act(LAM[:, 4, :], LAM[:, 3, :], AF.Exp, [LQ(3)], [LQ(4)])
            act(LAM[:, 14, :], LAM[:, 3, :], AF.Exp, [LQ(3)], [LQ(14)], scale=float(L))
            tt("dve", LAM[:, 5, :], LAM[:, 1, :], LAM[:, 2, :], ALU.mult, [LQ(1), LQ(2)], [LQ(5)])
            ts("dve", LAM[:, 15, :], LAM[:, 5, :], float(L), None, ALU.mult, None, [LQ(5)], [LQ(15)])
            sincos(LAM[:, 8, :], LAM[:, 7, :], LAM[:, 5, :], [LQ(5)], [LQ(8)], [LQ(7)], LAM[:, 6, :], LQ(6),
                   LAM[:, 39, :].bitcast(I32), LQ(39))
            tt("dve", LAM[:, 9, :], LAM[:, 4, :], LAM[:, 7, :], ALU.mult, [LQ(4), LQ(7)], [LQ(9)])
            tt("dve", LAM[:, 10, :], LAM[:, 4, :], LAM[:, 8, :], ALU.mult, [LQ(4), LQ(8)], [LQ(10)])
            tt("dve", LAM[:, 11, :], LAM[:, 0, :], LAM[:, 0, :], ALU.mult, [LQ(0)], [LQ(11)])
            tt("dve", LAM[:, 6, :], LAM[:, 1, :], LAM[:, 1, :], ALU.mult, [LQ(1)], [LQ(6)])
            tt("dve", LAM[:, 11, :], LAM[:, 11, :], LAM[:, 6, :], ALU.add, [LQ(11), LQ(6)], [LQ(11)])
            recip(LAM[:, 11, :], LAM[:, 11, :], [LQ(11)], [LQ(11)])
            ts("dve", LAM[:, 6, :], LAM[:, 9, :], -1.0, None, ALU.add, None, [LQ(9)], [LQ(6)])
            tt("dve", LAM[:, 12, :], LAM[:, 6, :], LAM[:, 0, :], ALU.mult, [LQ(6), LQ(0)], [LQ(12)])
            tt("dve", LAM[:, 38, :], LAM[:, 10, :], LAM[:, 1, :], ALU.mult, [LQ(10), LQ(1)], [LQ(38)])
            tt("dve", LAM[:, 12, :], LAM[:, 12, :], LAM[:, 38, :], ALU.add, [LQ(12), LQ(38)], [LQ(12)])
            tt("dve", LAM[:, 12, :], LAM[:, 12, :], LAM[:, 11, :], ALU.mult, [LQ(12), LQ(11)], [LQ(12)])
            tt("dve", LAM[:, 13, :], LAM[:, 10, :], LAM[:, 0, :], ALU.mult, [LQ(10), LQ(0)], [LQ(13)])
            tt("dve", LAM[:, 38, :], LAM[:, 6, :], LAM[:, 1, :], ALU.mult, [LQ(6), LQ(1)], [LQ(38)])
            tt("dve", LAM[:, 13, :], LAM[:, 13, :], LAM[:, 38, :], ALU.subtract, [LQ(13), LQ(38)], [LQ(13)])
            tt("dve", LAM[:, 13, :], LAM[:, 13, :], LAM[:, 11, :], ALU.mult, [LQ(13), LQ(11)], [LQ(13)])
            mset("dve", LAM[:, 16, :], 1.0, [LQ(16)])
            mset("dve", LAM[:, 17, :], 0.0, [LQ(17)])
            for k in range(1, L + 1):
                cmul("dve", LAM[:, 16 + 2 * k, :], LAM[:, 17 + 2 * k, :], LAM[:, 14 + 2 * k, :], LAM[:, 15 + 2 * k, :],
                     LAM[:, 9, :], LAM[:, 10, :], LAM[:, 38, :], LAM[:, 39, :],
                     [LQ(14 + 2 * k), LQ(15 + 2 * k), LQ(9), LQ(10)], [LQ(16 + 2 * k)], [LQ(17 + 2 * k)], LQ(38), LQ(39))
                p2_step()
            dump("lam", LAM[:, :, :], [LQ(k) for k in range(40)])
            if stop_after == "lam":
                raise Stop()
            BNf = FC[:].rearrange("p a b -> p (a b)").bitcast(F32)
            BPf = FB[:].rearrange("p a b -> p (a b)")
            EPF1 = EPB[:].rearrange("p a b -> p (a b)").bitcast(F32)
            Braw = EPF1[:, 0:512].rearrange("p (a i c) -> p a i c", a=2, i=16)
            Btmp = EPF1[:, 512:1024].rearrange("p (a i c) -> p a i c", a=2, i=16)
            Bk = BNf[:, 0:L * 512].rearrange("p (k a i c) -> p k a i c", k=L, a=2, i=16)
            Bpad = BPf[:, 0:L * 1024].rearrange("p (k a f r c) -> p k a f r c", k=L, a=2, f=4, r=4)
            dma("sp", Braw[:, 0], b_re.rearrange("(i p) c -> p i c", p=128), [], [("Braw", 0)])
            dma("sp", Braw[:, 1], b_im.rearrange("(i p) c -> p i c", p=128), [], [("Braw", 1)])
            bc = lambda row: LAM[:, row, :].unsqueeze(2).to_broadcast([128, 16, 16])
            cmul("dve", Bk[:, 0, 0], Bk[:, 0, 1], Braw[:, 0], Braw[:, 1], bc(12), bc(13), Btmp[:, 0], Btmp[:, 1],
                 [("Braw", 0), ("Braw", 1), LQ(12), LQ(13)], [("Bk", 0, 0)], [("Bk", 0, 1)], ("Btmp", 0), ("Btmp", 1))
            for k in range(1, L):
                cmul("dve", Bk[:, k, 0], Bk[:, k, 1], Bk[:, k - 1, 0], Bk[:, k - 1, 1], bc(9), bc(10), Btmp[:, 0], Btmp[:, 1],
                     [("Bk", k - 1, 0), ("Bk", k - 1, 1), LQ(9), LQ(10)], [("Bk", k, 0)], [("Bk", k, 1)], ("Btmp", 0), ("Btmp", 1))
                p2_step()
            mset("pool", BPf[:, 0:L * 1024], 0.0, [("Bpad",)])
            for gl in range(2):
                for k in range(L):
                    cp("pool", Bpad[gl * 64:(gl + 1) * 64, k, :, :, :, gl * 16:(gl + 1) * 16].rearrange("p a f r c -> p a (f r) c"),
                       Bk[gl * 64:(gl + 1) * 64, k], [("Bk", k, 0), ("Bk", k, 1), ("Bpad",)], [("Bpad", gl, k)])
            gi_ = 0
            for f in range(4):
                for kk in range(0, L, 4):
                    tb, tk_ = TRB[gi_ % 2], TRK[gi_ % 2]
                    gi_ += 1
                    for k in range(kk, kk + 4):
                        for ri in range(2):
                            col = (k - kk) * 2 + ri
                            tr(tb[:, col * 128:(col + 1) * 128], Bpad[:, k, ri, f].rearrange("p r c -> p (r c)"), IDB[:],
                               [("Bpad", 0, k), ("Bpad", 1, k), ("IDB",)], [tk_])
                    cp("act", BKT[:, f, kk:kk + 4, :, :].rearrange("p k a c -> p (k a c)"), tb[:, :], [tk_],
                       [("BKT", f, k) for k in range(kk, kk + 4)])
            if stop_after == "B":
                dump("bkt", BKT.rearrange("p f k a c -> p (f k a c)"), [("BKT", f, k) for f in range(4) for k in range(L)])
                raise Stop()
            CN = WST[:, 0:1024].rearrange("p (a t c) -> p a t c", a=2, t=4)
            CNB = WB[:, 0:1024].rearrange("p (a t c) -> p a t c", a=2, t=4)
            CT = TMPF[:, 0:2, :].rearrange("p a (t c) -> p a t c", t=4)
            mset("pool", WST[:, 0:1024], 0.0, [("CN",)])
            for ri, csrc in enumerate((c_re, c_im)):
                cv4 = csrc.rearrange("(t r l c) n -> t r l c n", t=4, r=4, l=2)
                for r in range(4):
                    for gl in range(2):
                        p0 = r * 32 + gl * 16
                        dma("sp", CN[p0:p0 + 16, ri, :, gl * 64:(gl + 1) * 64], cv4[:, r, gl].rearrange("t c n -> c t n"),
                            [("CN",)], [("CNd", ri, r, gl)])
            cp("dve", CNB[:], CN[:], [("CN",)] + [("CNd", ri, r, gl) for ri in range(2) for r in range(4) for gl in range(2)], [("CNB",)])
            for ri in range(2):
                for rt in range(4):
                    col = ri * 4 + rt
                    tr(PSB[:, col * 128:(col + 1) * 128], CNB[:, ri, rt, :], IDB[:], [("CNB",), ("IDB",)], [("PSB",)])
            cp("dve", CT.rearrange("p a t c -> p (a t c)"), PSB[:, :], [("PSB",)], [("CT",)])
            CTK = [("CT",)]
            CTv = lambda ri: CT[:, ri].rearrange("p t (r c) -> p (t r) c", r=4)
            bc32 = lambda row: LAM[:, row, :].unsqueeze(2).to_broadcast([128, 16, 32])
            T1 = TMPF[:, 2, :].rearrange("p (i c) -> p i c", i=16)
            T2 = TMPF[:, 3, :].rearrange("p (i c) -> p i c", i=16)
            for var in range(L + 1):
                lr, li = 16 + 2 * var, 17 + 2 * var
                tt("dve", T1, CTv(0), bc32(lr), ALU.mult, CTK + [LQ(lr)], [("T1",)])
                tt("dve", T2, CTv(1), bc32(li), ALU.mult, CTK + [LQ(li)], [("T2",)])
                tt("dve", CK[:, :, var, 0, :], T1, T2, ALU.subtract, [("T1",), ("T2",)], [("CK", var, 0)])
                tt("dve", T1, CTv(0), bc32(li), ALU.mult, CTK + [LQ(li)], [("T1",)])
                tt("dve", T2, CTv(1), bc32(lr), ALU.mult, CTK + [LQ(lr)], [("T2",)])
                stt("dve", CK[:, :, var, 1, :], T1, -1.0, T2, ALU.mult, ALU.subtract, [("T1",), ("T2",)], [("CK", var, 1)])
                p2_step()
            dump("bkt", BKT.rearrange("p f k a c -> p (f k a c)"), [("BKT", f, k) for f in range(4) for k in range(L)])
            dump("ck", CK.rearrange("p i v a c -> p (i v a c)"), [("CK", v, a) for v in range(L + 1) for a in range(2)])
            for _ in p2:
                pass
            barrier()
            if stop_after == "params":
                raise Stop()

            dump("xn", XN[:, :, :], [("XN", kc, t_) for kc in range(8) for t_ in range(NT)])

            TBK = [(0, 512), (512, 512), (1024, 512), (1536, 512), (2048, NS)]
            tiles_of_block = lambda bi: list(range(4 * bi, 4 * bi + 4)) if bi < 4 else [16]
            psrr = [0]

            def next_ps(n=3):
                i = psrr[0] % n
                psrr[0] += 1
                return i
            WBv = WB[:, :].rearrange("p (s k n) -> p s k n", s=2, k=8)

            def load_w_group(gi, slot):
                dma("sp", WST[:, :].rearrange("p (k n) -> p k n", k=8),
                    w_in[:, gi * 512:(gi + 1) * 512].rearrange("(k p) n -> p k n", p=128), [], [("WST",)])
                for kc in range(8):
                    if kc % 2 == 0:
                        act(WBv[:, slot, kc, :], WST[:, kc * 512:(kc + 1) * 512], AF.Copy, [("WST",), SQ("gin")], [("WB", slot, kc)],
                            scale=SMALL[:, kc:kc + 1])
                    else:
                        ts("dve", WBv[:, slot, kc, :], WST[:, kc * 512:(kc + 1) * 512], SMALL[:, kc:kc + 1], None, ALU.mult, None,
                           [("WST",), SQ("gin")], [("WB", slot, kc)])

            def fm_keys(name, m, bi):
                return [(name, m, t_) for t_ in tiles_of_block(bi)]

            def proj_fm(slot, evac):
                for m in range(4):
                    for bi, (c0, n) in enumerate(TBK):
                        pi = next_ps()
                        for kc in range(8):
                            mm(PS[pi][:, 0:n], WBv[:, slot, kc, m * 128:(m + 1) * 128], XN[:, kc, c0:c0 + n], kc == 0, kc == 7,
                               [("WB", slot, kc)] + [("XN", kc, t_) for t_ in tiles_of_block(bi)], [("PS", pi)])
                        evac(m, bi, pi, c0, n)

            load_w_group(0, 0)
            load_w_group(1, 1)
            proj_fm(0, lambda m, bi, pi, c0, n: cp("dve", FA[:, m, c0:c0 + n], PS[pi][:, 0:n], [("PS", pi)], fm_keys("FA", m, bi)))
            proj_fm(1, lambda m, bi, pi, c0, n: act(FB[:, m, c0:c0 + n], PS[pi][:, 0:n], AF.Silu, [("PS", pi)], fm_keys("FB", m, bi)))
            dump("xs", FA[:, :, :], [("FA", m, t_) for m in range(4) for t_ in range(NT)])
            dump("zs", FB[:, :, :], [("FB", m, t_) for m in range(4) for t_ in range(NT)])
            barrier()
            if stop_after == "inproj1":
                raise Stop()

            YB = [PS[3], PS[4], PS[5], PS[6]]
            ybank = lambda t: (YB[t // 2], (t % 2) * J)
            xsv = lambda f: FA[:, f, 0:T].rearrange("p (j s) -> p j s", s=L)
            S5K = lambda k: ("S5F", k)
            FAK = lambda f: [("FA", f, t_) for t_ in range(16)]

            def gelu_to(yv, ykeys, dst, dkeys, s, n):
                x2 = TMPF[:, 1, s * 256:s * 256 + n]
                u = TMPF[:, 2, s * 256:s * 256 + n]
                k2, ku = ("TMPF", 1, s), ("TMPF", 2, s)
                act(x2, yv, AF.Square, ykeys, [k2])
                ts("pool", x2, x2, 0.044715, 1.0, ALU.mult, ALU.add, [k2], [k2])
                tt("pool", u, x2, yv, ALU.mult, [k2] + ykeys, [ku])
                act(u, u, AF.Sigmoid, [ku], [ku], scale=1.5957691216)
                tt("dve", dst, u, yv, ALU.mult, [ku] + ykeys, dkeys)

            JI = TMPF[:, 3, 0:J].bitcast(I32)
            P.add("pool", lambda e: e.iota(JI, [[1, J]], base=1, channel_multiplier=0), writes=[("JI",)])
            cp("dve", S5F[:, 11, :], JI, [("JI",)], [S5K(11)])
            XSD = WB[:, :].rearrange("p (f s j) -> p f s j", f=4, s=L)
            for f in range(4):
                cp("act" if f % 2 == 0 else "dve", XSD[:, f], xsv(f).rearrange("p j s -> p s j"), FAK(f), [("XSD", f)])
            PTB = [PT, WST[:, 0:2048].bitcast(BF16).rearrange("p (t a j) -> p t a j", t=L, a=2)]
            XCB = [XC, WST[:, 2048:2304].bitcast(BF16).rearrange("p (a j) -> p a j", a=2)]
            SBF = [S5F[:, 4:6, :], WST[:, 2304:2816].rearrange("p (a j) -> p a j", a=2)]
            PIi = TMPF[:, 3, 300:301].bitcast(I32)
            PI2 = TMPF[:, 3, 304:305].bitcast(I32)
            P.add("pool", lambda e: e.iota(PIi, [[0, 1]], base=0, channel_multiplier=1), writes=[("PIi",)])
            ts("dve", PI2, PIi, 5, None, ALU.arith_shift_right, None, [("PIi",)], [("PI2",)])
            for r_ in range(4):
                ts("dve", SMALL[:, 108 + r_:109 + r_], PI2, r_, None, ALU.is_equal, None, [("PI2",)], [SQ(("rm", r_))])
            XSZ = WST[:, 2816:3840].bitcast(BF16).rearrange("p (s j) -> p s j", s=L)

            def a_step(i):
                f, r = i // 4, i % 4
                pb = i % 2
                ts("dve", XSZ.rearrange("p s j -> p (s j)"), XSD[:, f].rearrange("p s j -> p (s j)"), SMALL[:, 108 + r:109 + r], None, ALU.mult, None,
                   [("XSD", f), SQ(("rm", r))], [("XSZ",)])
                for t in range(L):
                    pi = next_ps()
                    for ri in range(2):
                        for s in range(t + 1):
                            mm(PS[pi][:, ri * J:(ri + 1) * J], BKT[:, f, t - s, ri, :], XSZ[:, s, :],
                               s == 0, s == t, [("BKT", f, t - s), ("XSZ",)], [("PS", pi)])
                    cp("act", PTB[pb][:, t].rearrange("p a j -> p (a j)"), PS[pi][:, 0:2 * J], [("PS", pi)], [("PT", pb, t)])
                    if t == L - 1:
                        cp("act", SBF[pb].rearrange("p a j -> p (a j)"), PS[pi][:, 0:2 * J], [("PS", pi)], [("SBF", pb, 0), ("SBF", pb, 1)])

            EPBF = EPB[:].rearrange("p a b -> p (a b)").bitcast(F32).rearrange("p (k j) -> p k j", j=J)
            VAI = VA[:].rearrange("p a b c -> p (a b c)").bitcast(I32)[:, 0:J]
            TABK = lambda pb, w: ("TAB", pb, w)

            def sincos_ops(i):
                pb = i % 2
                cosT, sinT = EPBF[:, 2 * pb, :], EPBF[:, 2 * pb + 1, :]
                arg, tmp, ki = EPBF[:, 4, :], EPBF[:, 5, :], VAI
                ka, kt, kk = ("SC", "arg"), ("SC", "tmp"), ("SC", "ki")
                return [
                    lambda: ts("dve", arg, S5F[:, 11, :], LAM[:, 15, i:i + 1], None, ALU.mult, None, [S5K(11), LQ(15)], [ka]),
                    lambda: ts("dve", ki, arg, 1.0 / (2 * PI), None, ALU.mult, None, [ka], [kk]),
                    lambda: stt("dve", tmp, ki, -C1, arg, ALU.mult, ALU.add, [kk, ka], [kt]),
                    lambda: stt("dve", tmp, ki, -C2, tmp, ALU.mult, ALU.add, [kk, kt], [kt]),
                    lambda: ts("dve", tmp, tmp, -PILO, PILO, ALU.max, ALU.min, [kt], [kt]),
                    lambda: act(sinT, tmp, AF.Sin, [kt], [TABK(pb, 1)]),
                    lambda: ts("dve", ki, arg, 1.0 / (2 * PI), 0.25, ALU.mult, ALU.add, [ka], [kk]),
                    lambda: stt("dve", tmp, ki, -C1, arg, ALU.mult, ALU.add, [kk, ka], [kt]),
                    lambda: stt("dve", tmp, ki, -C2, tmp, ALU.mult, ALU.add, [kk, kt], [kt]),
                    lambda: ts("dve", tmp, tmp, PI / 2, PILO, ALU.add, ALU.min, [kt], [kt]),
                    lambda: act(cosT, tmp, AF.Sin, [kt], [TABK(pb, 0)]),
                ]

            def rot_ops(i):
                pb = i % 2
                cosT, sinT = EPBF[:, 2 * pb, :], EPBF[:, 2 * pb + 1, :]
                kc_, ks_ = TABK(pb, 0), TABK(pb, 1)
                Sre, Sim = SBF[pb][:, 0, :], SBF[pb][:, 1, :]
                kre, kim = ("SBF", pb, 0), ("SBF", pb, 1)
                a, b_, cre, cim, wre, wim = (S5F[:, k, :] for k in (6, 7, 8, 9, 10, 3))
                r8 = LAM[:, 14, i:i + 1].to_broadcast([128, J])
                return [
                    lambda: tt("dve", a, cosT, Sre, ALU.mult, [kc_, kre], [S5K(6)]),
                    lambda: tt("dve", b_, sinT, Sim, ALU.mult, [ks_, kim], [S5K(7)]),
                    lambda: tt("dve", cre, a, b_, ALU.add, [S5K(6), S5K(7)], [S5K(8)]),
                    lambda: tt("dve", a, cosT, Sim, ALU.mult, [kc_, kim], [S5K(6)]),
                    lambda: tt("dve", b_, sinT, Sre, ALU.mult, [ks_, kre], [S5K(7)]),
                    lambda: tt("dve", cim, a, b_, ALU.subtract, [S5K(6), S5K(7)], [S5K(9)]),
                    lambda: P.add("dve", lambda e: e.tensor_tensor_scan(wre, r8, cre, 0.0, ALU.mult, ALU.add), reads=[S5K(8), LQ(14)], writes=[S5K(10)]),
                    lambda: P.add("dve", lambda e: e.tensor_tensor_scan(wim, r8, cim, 0.0, ALU.mult, ALU.add), reads=[S5K(9), LQ(14)], writes=[S5K(3)]),
                    lambda: tt("dve", a, cosT, wre, ALU.mult, [kc_, S5K(10)], [S5K(6)]),
                    lambda: tt("dve", b_, sinT, wim, ALU.mult, [ks_, S5K(3)], [S5K(7)]),
                    lambda: tt("dve", cre, a, b_, ALU.subtract, [S5K(6), S5K(7)], [S5K(8)]),
                    lambda: tt("dve", a, sinT, wre, ALU.mult, [ks_, S5K(10)], [S5K(6)]),
                    lambda: tt("dve", b_, cosT, wim, ALU.mult, [kc_, S5K(3)], [S5K(7)]),
                    lambda: tt("dve", cim, a, b_, ALU.add, [S5K(6), S5K(7)], [S5K(9)]),
                    lambda: cp("act", SM2[:, i:i + 1], cre[:, J - 1:J], [S5K(8)], [("SM2", i)]),
                    lambda: cp("act", SM2[:, 16 + i:17 + i], cim[:, J - 1:J], [S5K(9)], [("SM2", 16 + i)]),
                    lambda: mset("pool", XCB[pb][:, :, 0:1], 0.0, [("XC", pb, "z")]),
                    lambda: cp("act", XCB[pb][:, 0, 1:J], cre[:, 0:J - 1], [S5K(8)], [("XC", pb, 0)]),
                    lambda: cp("act", XCB[pb][:, 1, 1:J], cim[:, 0:J - 1], [S5K(9)], [("XC", pb, 1)]),
                ]

            def chain(i):
                la = rot_ops(i)
                lb = sincos_ops(i + 1) if i + 1 < 16 else []
                for k in range(max(len(la), len(lb))):
                    if k < len(la):
                        la[k]()
                    if k < len(lb):
                        lb[k]()

            def d_step(i):
                f, r = i // 4, i % 4
                pb = i % 2
                for t in range(L):
                    bank, co = ybank(t)
                    for ri in range(2):
                        mm(bank[32 * r:32 * r + 32, co:co + J], CK[:, i, 0, ri, :], PTB[pb][:, t, ri, :], ri == 0, False,
                           [("CK", 0, ri), ("PT", pb, t)], [("YB", t // 2, r)], tile_position=(0, 32 * r))
                    for ri in range(2):
                        mm(bank[32 * r:32 * r + 32, co:co + J], CK[:, i, t + 1, ri, :], XCB[pb][:, ri, :], False, ri == 1,
                           [("CK", t + 1, ri), ("XC", pb, 0), ("XC", pb, 1), ("XC", pb, "z")], [("YB", t // 2, r)], tile_position=(0, 32 * r))
                if r == 3:
                    for t in range(L):
                        bank, co = ybank(t)
                        s = t % 2
                        yk = ("TMPF", 0, s)
                        yv = TMPF[:, 0, s * 256:s * 256 + J]
                        stt("dve", yv, xsv(f)[:, :, t], SMALL[:, 8 + f:9 + f], bank[:, co:co + J], ALU.mult, ALU.add,
                            FAK(f) + [SQ("dsk")] + [("YB", t // 2, r_) for r_ in range(4)], [yk])
                        gelu_to(yv, [yk], FC[:, f, 0:T].rearrange("p (j s) -> p j s", s=L)[:, :, t], [("FC", f, t_) for t_ in range(16)], s, J)

            for th in sincos_ops(0):
                th()
            a_step(0)
            for i in range(16):
                if i + 1 < 16:
                    a_step(i + 1)
                chain(i)
                d_step(i)
            dma("sp", sre_p.rearrange("(i p) -> p i", p=128), SM2[:, 0:16], [("SM2", i) for i in range(16)], [("out", "sre_p")],
                allow_slow_non_contiguous=True)
            dma("sp", sim_p.rearrange("(i p) -> p i", p=128), SM2[:, 16:32], [("SM2", 16 + i) for i in range(16)], [("out", "sim_p")],
                allow_slow_non_contiguous=True)
            barrier()
            if stop_after == "s5p":
                if "gel" in dbg_out:
                    dma("sp", dbg_out["gel"][:, :, 0:T], FC[:, :, 0:T], [], [("dbg", "gel")])
                raise Stop()
            SBU = STG[:, 3072:5120].rearrange("p (r a f n) -> p r a f n", r=4, a=2, f=4)
            SA = S5F[:, 0:2, :]
            SBs = S5F[:, 2:4, :]
            SXS = S5F[:, 8:12, :].rearrange("p a j -> p (a j)").bitcast(BF16).rearrange("p (a i n) -> p a i n", a=2, i=16)
            for r in range(4):
                for ri in range(2):
                    for f in range(4):
                        bk = 3 + r
                        q_ = ri * 4 + f
                        mm(PS[bk][:, q_ * 64:q_ * 64 + 64], BKT[32 * r:32 * r + 32, f, 0, ri, :], FA[32 * r:32 * r + 32, f, T:TA],
                           True, True, [("BKT", f, 0), ("FA", f, 16)], [("PS", bk)], tile_position=(32 * r, 0))
            for bk in range(4):
                cp("act" if bk % 2 == 0 else "dve", STG[:, 3072 + bk * 512:3072 + (bk + 1) * 512], PS[3 + bk][:, :], [("PS", 3 + bk)], [("SBU", bk)])
            if stop_after == "r1a":
                raise Stop()
            TK = lambda row: [("TMPF", row, 0), ("TMPF", row, 1)]
            NAT = TMPF[:, 0, :].rearrange("p (q c) -> p q c", q=4)
            HILO = TMPF[:, 1, :].bitcast(BF16).rearrange("p (a q c) -> p a q c", a=2, q=4)
            HIF = TMPF[:, 2, :]

            def split_hilo(src_flat, src_keys):
                cp("dve", HILO[:, 0].rearrange("p q c -> p (q c)"), src_flat, src_keys, [("HILO", 0)])
                cp("dve", HIF, HILO[:, 0].rearrange("p q c -> p (q c)"), [("HILO", 0)], TK(2))
                tt("dve", HILO[:, 1].rearrange("p q c -> p (q c)"), src_flat, HIF, ALU.subtract, src_keys + TK(2), [("HILO", 1)])
                for a_ in range(2):
                    for q_ in range(4):
                        tr(PSB[:, (a_ * 4 + q_) * 128:(a_ * 4 + q_ + 1) * 128], HILO[:, a_, q_, :], IDB[:], [("HILO", a_), ("IDB",)], [("PSB",)])

            def merge_hilo(dst_flat, dst_keys):
                cp("dve", dst_flat, PSB[:, 0:512], [("PSB",)], dst_keys)
                tt("dve", dst_flat, dst_flat, PSB[:, 512:1024], ALU.add, [("PSB",)] + dst_keys, dst_keys)

            for ri, src in enumerate((st_re, st_im)):
                for u in range(2):
                    dma("sp", NAT[:, ri * 2 + u, :], src[8 * u:8 * u + 8].rearrange("b (i p) -> (b i) p", p=128), [], [("NAT", ri * 2 + u)])
            split_hilo(TMPF[:, 0, :], [("NAT", q_) for q_ in range(4)])
            SAK = [("SA", 0), ("SA", 1)]
            merge_hilo(SA.rearrange("p a n -> p (a n)"), SAK)
            if stop_after == "r1b":
                raise Stop()
            lamb = lambda row: LAM[:, row, :].rearrange("p (f r) -> p f r", f=4).unsqueeze(1).to_broadcast([128, 16, 4, 4])
            v3 = lambda ap: ap.rearrange("p (b f r) -> p b f r", b=16, f=4)
            SBUK = [("SBU", bk) for bk in range(4)]
            cur, nxt = SA, SBs
            curk, nxtk = "SA", "SB"
            for j in range(4):
                ck_keys = [(curk, 0), (curk, 1)]
                cmul("dve", v3(S5F[:, 6, :]), v3(S5F[:, 7, :]), v3(cur[:, 0, :]), v3(cur[:, 1, :]), lamb(9), lamb(10),
                     v3(S5F[:, 4, :]), v3(S5F[:, 5, :]), ck_keys + [LQ(9), LQ(10)], [S5K(6)], [S5K(7)], S5K(4), S5K(5))
                for ri in range(2):
                    buj = SBU[:, :, ri].rearrange("p r f (b j) -> p b f r j", j=4)[:, :, :, :, j]
                    tt("dve", v3(nxt[:, ri, :]), v3(S5F[:, 6 + ri, :]), buj, ALU.add, [S5K(6 + ri)] + SBUK, [(nxtk, ri)])
                    cp("act", SXS[:, ri].rearrange("p (f r) (b j) -> p b f r j", j=4, f=4)[:, :, :, :, j], v3(nxt[:, ri, :]), [(nxtk, ri)], [("SXS", j)])
                cur, nxt, curk, nxtk = nxt, cur, nxtk, curk
            if stop_after == "r1c":
                raise Stop()
            split_hilo(cur.rearrange("p a n -> p (a n)"), [(curk, 0), (curk, 1)])
            merge_hilo(TMPF[:, 0, :], TK(0))
            for ri, dst in enumerate((sre_s, sim_s)):
                for u in range(2):
                    dma("sp", dst[8 * u:8 * u + 8].rearrange("b (i p) -> (b i) p", p=128), NAT[:, ri * 2 + u, :], TK(0), [("out", "ss", ri, u)])
            if stop_after == "r1d":
                raise Stop()
            SXSK = [("SXS", j) for j in range(4)]
            for i in [4 * f_ + r_ for r_ in range(4) for f_ in range(4)]:
                f, r = i // 4, i % 4
                for ri in range(2):
                    mm(PS[0][32 * r:32 * r + 32, f * 64:(f + 1) * 64], CK[:, i, 0, ri, :], SXS[:, ri, i, :], ri == 0, ri == 1,
                       [("CK", 0, ri)] + SXSK, [("PS", 0)], tile_position=(0, 32 * r))
            for f in range(4):
                s = f % 2
                yk = ("TMPF", 0, s)
                yv = TMPF[:, 0, s * 256:s * 256 + NS]
                stt("dve", yv, FA[:, f, T:TA], SMALL[:, 8 + f:9 + f], PS[0][:, f * 64:(f + 1) * 64], ALU.mult, ALU.add,
                    [("FA", f, 16), SQ("dsk"), ("PS", 0)], [yk])
                gelu_to(yv, [yk], FC[:, f, T:TA], [("FC", f, 16)], s, NS)
            if stop_after == "s5s":
                dump("gel", FC[:, :, :], [("FC", f, t_) for f in range(4) for t_ in range(NT)])
                raise Stop()

            dma("sp", WST[:, :].rearrange("p (k n) -> p k n", k=4), w_glu.rearrange("(k p) n -> p k n", p=128), [], [("WST",)])
            WG = WB[:, 0:4096].rearrange("p (k n) -> p k n", k=4)
            for kc in range(4):
                cp("act" if kc % 2 == 0 else "dve", WG[:, kc, :], WST[:, kc * 1024:(kc + 1) * 1024], [("WST",)], [("WB", 0, 2 * kc), ("WB", 0, 2 * kc + 1)])
            cnt = 0
            for m in range(4):
                for bi, (c0, n) in enumerate(TBK):
                    pa = next_ps()
                    for kc in range(4):
                        mm(PS[pa][:, 0:n], WG[:, kc, m * 128:(m + 1) * 128], FC[:, kc, c0:c0 + n], kc == 0, kc == 3,
                           [("WB", 0, 2 * kc), ("WB", 0, 2 * kc + 1)] + fm_keys("FC", kc, bi), [("PS", pa)])
                    pb = next_ps()
                    for kc in range(4):
                        mm(PS[pb][:, 0:n], WG[:, kc, 512 + m * 128:512 + (m + 1) * 128], FC[:, kc, c0:c0 + n], kc == 0, kc == 3,
                           [("WB", 0, 2 * kc), ("WB", 0, 2 * kc + 1)] + fm_keys("FC", kc, bi), [("PS", pb)])
                    pp = cnt % 2
                    cnt += 1
                    sgk = [("TMPF", 2 + pp, 0), ("TMPF", 2 + pp, 1)]
                    t1k = [("TMPF", pp, 0), ("TMPF", pp, 1)]
                    act(TMPF[:, 2 + pp, 0:n], PS[pb][:, 0:n], AF.Sigmoid, [("PS", pb), SQ("bglu")], sgk, bias=SMALL[:, 16 + m:17 + m])
                    stt("dve", TMPF[:, pp, 0:n], PS[pa][:, 0:n], SMALL[:, 12 + m:13 + m], TMPF[:, 2 + pp, 0:n], ALU.add, ALU.mult,
                        [("PS", pa), SQ("bglu")] + sgk, t1k)
                    tt("pool", FA[:, m, c0:c0 + n], TMPF[:, pp, 0:n], FB[:, m, c0:c0 + n], ALU.mult, t1k + fm_keys("FB", m, bi), fm_keys("FA", m, bi))
            dump("ys", FA[:, :, :], [("FA", m, t_) for m in range(4) for t_ in range(NT)])
            barrier()
            if stop_after == "glu":
                raise Stop()

            mset("pool", VA[:, 0, :, 64:128], 1.0, [("VAones", 0)])
            mset("pool", VA[:, 1, :, 0:64], 1.0, [("VAones", 1)])
            OSTH = lambda s2, h: OST[:, s2, h * 512:(h + 1) * 512]

            def proj_tm(slot, dst_p, dst_s, to_vb):
                for t_ in range(NT):
                    r = rows_of(t_)
                    pi = next_ps()
                    for kc in range(8):
                        mm(PS[pi][:r, :], XN[:, kc, t_ * 128:t_ * 128 + r], WBv[:, slot, kc, :], kc == 0, kc == 7,
                           [("WB", slot, kc), ("XN", kc, t_)], [("PS", pi)])
                    s2, h = (t_ // 2) % 2, t_ % 2
                    cp("act", OSTH(s2, h)[:r], PS[pi][:r, :], [("PS", pi)], [("OST", s2, h)])
                    if to_vb:
                        cp("pool", VB[:r, t_, 0:512], OSTH(s2, h)[:r], [("OST", s2, h)], [("VB", t_)])
                    dstap = dst_p[t_ * 128:(t_ + 1) * 128, :] if t_ < 16 else dst_s[:, :]
                    dma("pool", dstap, OSTH(s2, h)[:r], [("OST", s2, h)], [("out", "kv", id(dst_p), t_)])

            load_w_group(2, 0)
            load_w_group(3, 1)
            proj_fm(0, lambda m, bi, pi, c0, n: act(FB[:, m, c0:c0 + n], PS[pi][:, 0:n], AF.Copy, [("PS", pi)], fm_keys("FB", m, bi), scale=0.125))
            if stop_after == "r3a":
                raise Stop()
            load_w_group(4, 0)
            proj_fm(1, lambda m, bi, pi, c0, n: cp("dve", FC[:, m, c0:c0 + n], PS[pi][:, 0:n], [("PS", pi)], fm_keys("FC", m, bi)))
            if stop_after == "r3b":
                raise Stop()
            proj_tm(1, k_p, k_s, False)
            if stop_after == "r3c":
                raise Stop()
            load_w_group(5, 1)
            proj_tm(0, v_p, v_s, True)
            if stop_after == "r3d":
                raise Stop()
            proj_fm(1, lambda m, bi, pi, c0, n: act(FD[:, m, c0:c0 + n], PS[pi][:, 0:n], AF.Silu, [("PS", pi)], fm_keys("FD", m, bi)))
            dump("q", FB[:, :, :], [("FB", m, t_) for m in range(4) for t_ in range(NT)])
            dump("kt", FC[:, :, :], [("FC", m, t_) for m in range(4) for t_ in range(NT)])
            barrier()
            if stop_after == "inproj2":
                raise Stop()

            SBK = [0, 1, 4, 5]
            KTZ = WB[:, :].rearrange("p (s n) -> p s n", s=4)[:, 0:2, 0:T]
            mset("pool", KTZ[64:128, 0, :], 0.0, [("KTZz", 0)])
            mset("pool", KTZ[0:64, 1, :], 0.0, [("KTZz", 1)])
            LA = 3
            its = []
            octr = 0
            for h in range(8):
                for qc in range(4):
                    ob = (2, 3, 6)[octr % 3]
                    octr += 1
                    nj = 4 * qc + 4
                    for j in range(nj):
                        its.append((h, qc, j, nj, ob))

            def stage_a(n):
                h, qc, j, nj, ob = its[n]
                c, hb = h // 2, 64 * (h % 2)
                qb0 = max(4 * qc, j)
                nn = (4 * qc + 4 - qb0) * 128
                col0 = qb0 * 128
                e = n % 4
                sbk = SBK[e]
                if j == 0 and qc == 0:
                    cp("pool", VA[:, h % 2, :, hb:hb + 64], VB[:, 0:16, h * 64:(h + 1) * 64], [("VB", t_) for t_ in range(16)] + [("VAones", h % 2)], [("VA", h % 2)])
                    cp("act", KTZ[hb:hb + 64, h % 2, :], FC[hb:hb + 64, c, 0:T], [("FC", c, t_) for t_ in range(16)] + [("KTZz", h % 2)], [("KTZ", h % 2)])
                qk = [("FB", c, t_) for t_ in range(qb0, 4 * qc + 4)]
                mm(PS[sbk][:, 0:nn], KTZ[:, h % 2, j * 128:(j + 1) * 128], FB[:, c, col0:col0 + nn], True, True,
                   [("KTZ", h % 2), ("KTZz", h % 2)] + qk, [("PS", sbk)])
                act(EPB[:, e, 0:nn], PS[sbk][:, 0:nn], AF.Exp, [("PS", sbk)], [("EPB", e)])
                mc = (qb0 - j + 3) * 128
                tt("dve" if n % 2 == 0 else "pool", EPB[:, 4 + e, 0:nn], EPB[:, e, 0:nn], MASK[:, mc:mc + nn], ALU.mult,
                   [("EPB", e), ("MASK",)], [("EPB", 4 + e)])

            def stage_b(n):
                h, qc, j, nj, ob = its[n]
                c, hb = h // 2, 64 * (h % 2)
                dn = 64 - hb
                qb0 = max(4 * qc, j)
                nn = (4 * qc + 4 - qb0) * 128
                qoff = (qb0 - 4 * qc) * 128
                e = n % 4
                mm(PS[ob][:, qoff:qoff + nn], VA[:, h % 2, j, :], EPB[:, 4 + e, 0:nn], j == 0, j == nj - 1,
                   [("VA", h % 2), ("VAones", h % 2), ("EPB", 4 + e)], [("PS", ob)])
                if j == nj - 1:
                    pp = qc % 2
                    rk_, tk_ = [("TMPF", 2 * pp, 0), ("TMPF", 2 * pp, 1)], [("TMPF", 2 * pp + 1, 0), ("TMPF", 2 * pp + 1, 1)]
                    recip(TMPF[hb:hb + 64, 2 * pp, :], PS[ob][dn:dn + 64, :], [("PS", ob)], rk_)
                    tt("dve", TMPF[hb:hb + 64, 2 * pp + 1, :], PS[ob][hb:hb + 64, :], TMPF[hb:hb + 64, 2 * pp, :], ALU.mult, [("PS", ob)] + rk_, tk_)
                    tt("pool", XN[hb:hb + 64, c, qc * 512:(qc + 1) * 512], TMPF[hb:hb + 64, 2 * pp + 1, :], FD[hb:hb + 64, c, qc * 512:(qc + 1) * 512], ALU.mult,
                       tk_ + [("FD", c, t_) for t_ in range(4 * qc, 4 * qc + 4)], [("XN", c, t_, hb) for t_ in range(4 * qc, 4 * qc + 4)])

            for n in range(len(its) + LA):
                if n < len(its):
                    stage_a(n)
                if n - LA >= 0:
                    stage_b(n - LA)
            barrier()
            if stop_after == "attn":
                if "ya" in dbg_out:
                    dma("sp", dbg_out["ya"][:, :, 0:T], XN[:, 0:4, 0:T], [], [("dbg", "ya")])
                raise Stop()
            TRB2 = [PSB, PS[1][:].bitcast(BF16)]
            TRK2 = [("PSB",), ("PS", 1)]
            mset("pool", TMPF[:, 0:2, :], 0.0, [("QB",)])
            for c in range(4):
                for hh in range(2):
                    cp("dve", QB[hh * 64:(hh + 1) * 64, c, :].rearrange("p (b j h) -> p b j h", b=NB, j=4)[:, :, :, 2 * c + hh],
                       FB[hh * 64:(hh + 1) * 64, c, T:TA].rearrange("p (b j) -> p b j", j=4), [("QB",), ("FB", c, 16)], [("QBd", c, hh)])
            QBK = [("QB",)] + [("QBd", c, hh) for c in range(4) for hh in range(2)]
            for c in range(4):
                mm(PS[1][0:64, :], FC[:, c, T:TA], QB[:, c, :], c == 0, c == 3, [("FC", c, 16)] + QBK, [("PS", 1)])
            act(EPB[0:64, 2, :], PS[1][0:64, :], AF.Exp, [("PS", 1)], [("EPB", 2)])
            tt("dve", EPB[0:64, 3, :], EPB[0:64, 2, :], MSK_N[:, :], ALU.mult, [("EPB", 2), ("MSK_N",)], [("EPB", 3)])
            for c in range(4):
                mm(PS[2 + c][:, :], VB[0:64, 16, c * 128:(c + 1) * 128], EPB[0:64, 3, :], True, False, [("VB", 16), ("EPB", 3)], [("PS", 2 + c)])
            mm(PS[6][:, :], ONESB[0:64, :], EPB[0:64, 3, :], True, False, [("ONESB",), ("EPB", 3)], [("PS", 6)])
            trc = [0]
            KT2 = WST[:, 0:3584].bitcast(BF16).rearrange("p (s c k) -> p s c k", s=2, c=4)

            def cache_dma(b, kv, src):
                for tl in range(3):
                    srcap = src[b].rearrange("(g x) c -> g x c", x=16)[32 * tl:32 * tl + 32, 0:4, :]
                    dma("pool", KST[:, kv, tl, :], srcap, [], [("KST", kv, tl)])
                dma("pool", KST[:, kv, 3:7, :], src[b, 1536:2048, :].rearrange("(t p) c -> p t c", p=128), [], [("KST", kv, 3)])

            def st_t(b):
                cache_dma(b, 0, ck)
                KK = [("KST", 0, tl) for tl in range(4)]
                for c in range(4):
                    tb, tk_ = TRB2[trc[0] % 2], TRK2[trc[0] % 2]
                    trc[0] += 1
                    for tl in range(7):
                        tr(tb[:, tl * 128:(tl + 1) * 128], KST[:, 0, tl, c * 128:(c + 1) * 128], IDB[:], KK + [("IDB",)], [tk_])
                    cp("act" if c % 2 == 0 else "dve", KT2[:, b % 2, c, :], tb[:, 0:896], [tk_], [("KT", b % 2, c)])

            def st_c(b):
                cache_dma(b, 1, cv)
                VK = [("KST", 1, tl) for tl in range(4)]
                for tl in range(7):
                    for c in range(4):
                        mm(PS[0][:, tl * 32:(tl + 1) * 32], KT2[:, b % 2, c, tl * 128:(tl + 1) * 128], QB[:, c, b * 32:(b + 1) * 32], c == 0, c == 3,
                           [("KT", b % 2, c)] + QBK, [("PS", 0)])
                act(EPB[:, 0, 0:224], PS[0][:, 0:224], AF.Exp, [("PS", 0)], [("EPB", 0)])
                tt("dve", EPB[:, 1, 0:224], EPB[:, 0, 0:224], MSK_S[:].rearrange("p t j h -> p (t j h)"), ALU.mult, [("EPB", 0), ("MSK_S",)], [("EPB", 1)])
                last_b = (b == NB - 1)
                for tl in range(7):
                    for c in range(4):
                        mm(PS[2 + c][:, b * 32:(b + 1) * 32], KST[:, 1, tl, c * 128:(c + 1) * 128], EPB[:, 1, tl * 32:(tl + 1) * 32], False,
                           last_b and tl == 6, VK + [("EPB", 1)], [("PS", 2 + c)])
                    mm(PS[6][:, b * 32:(b + 1) * 32], ONESB[:, :], EPB[:, 1, tl * 32:(tl + 1) * 32], False, last_b and tl == 6,
                       [("ONESB",), ("EPB", 1)], [("PS", 6)])

            st_t(0)
            for b in range(NB):
                if b + 1 < NB:
                    st_t(b + 1)
                st_c(b)
            for c in range(4):
                for hh in range(2):
                    rows = slice(hh * 64, (hh + 1) * 64)
                    hsel = lambda ap: ap.rearrange("p (b j h) -> p b j h", b=NB, j=4)[:, :, :, 2 * c + hh]
                    rd = TMPF[rows, 2, 0:64].rearrange("p (b j) -> p b j", j=4)
                    t1 = TMPF[rows, 3, 0:64].rearrange("p (b j) -> p b j", j=4)
                    recip(rd, hsel(PS[6][rows, :]), [("PS", 6)], [("TMPF", 2, 0)])
                    tt("dve", t1, hsel(PS[2 + c][rows, :]), rd, ALU.mult, [("PS", 2 + c), ("TMPF", 2, 0)], [("TMPF", 3, 0)])
                    tt("pool", XN[rows, c, T:TA].rearrange("p (b j) -> p b j", j=4), t1, FD[rows, c, T:TA].rearrange("p (b j) -> p b j", j=4), ALU.mult,
                       [("TMPF", 3, 0), ("FD", c, 16)], [("XN", c, 16, hh * 64)])
            dump("ya", XN[:, 0:4, :], [("XN", c, t_, hb) for c in range(4) for t_ in range(NT) for hb in (0, 64)])
            barrier()
            if stop_after == "sattn":
                raise Stop()

            dma("sp", GFB[:, :], bass.AP(g_fin.tensor, 0, [[0, 128], [1, D]]), [], [("GFB",)])
            WO = WB[:, :].rearrange("p (k n) -> p k n", k=8)
            for half in range(2):
                dma("sp", WST[:, :].rearrange("p (k n) -> p k n", k=4), w_out[half * 512:(half + 1) * 512, :].rearrange("(k p) n -> p k n", p=128),
                    [], [("WST",)])
                for kc in range(4):
                    if kc % 2 == 0:
                        act(WO[:, half * 4 + kc, :], WST[:, kc * 1024:(kc + 1) * 1024], AF.Copy, [("WST",), SQ("gssm"), SQ("gattn")], [("WO", half * 4 + kc)],
                            scale=SMALL[:, 20 + half * 4 + kc:21 + half * 4 + kc])
                    else:
                        ts("dve", WO[:, half * 4 + kc, :], WST[:, kc * 1024:(kc + 1) * 1024], SMALL[:, 20 + half * 4 + kc:21 + half * 4 + kc], None, ALU.mult, None,
                           [("WST",), SQ("gssm"), SQ("gattn")], [("WO", half * 4 + kc)])
            def op_a(t_):
                r = rows_of(t_)
                s = t_ % 2
                cs = slice(t_ * 128, t_ * 128 + r)
                dma("sp", XST[:r, s, :], x_src(t_), [], [("XST", s)])
                yak = [("XN", kc, t_, hb) for kc in range(4) for hb in (0, 64)]
                ysk = [("FA", kc, t_) for kc in range(4)]
                e0, e1 = 2 * s, 2 * s + 1
                tt("pool", EPB[:, e0, :].rearrange("p (k t) -> p k t", k=4)[:, :, 0:r], FA[:, :, cs], FA[:, :, cs], ALU.mult, ysk, [("EPB", e0)])
                tt("pool", EPB[:, e1, :].rearrange("p (k t) -> p k t", k=4)[:, :, 0:r], XN[:, 0:4, cs], XN[:, 0:4, cs], ALU.mult, yak, [("EPB", e1)])

            def op_mm(t_):
                r = rows_of(t_)
                s = t_ % 2
                cs = slice(t_ * 128, t_ * 128 + r)
                yak = [("XN", kc, t_, hb) for kc in range(4) for hb in (0, 64)]
                ysk = [("FA", kc, t_) for kc in range(4)]
                for br in range(2):
                    for kc in range(4):
                        mm(PS[4][:r, br:br + 1], EPB[:, 2 * s + br, kc * 128:kc * 128 + r], ONESB[:, 0:1], kc == 0, kc == 3,
                           [("EPB", 2 * s + br), ("ONESB",)], [("PS", 4)])
                for half in range(2):
                    for kc in range(4):
                        mm(PS[half][:r, :], FA[:, kc, cs], WO[:, kc, half * 512:(half + 1) * 512], kc == 0, kc == 3,
                           ysk + [("WO", kc)], [("PS", half)])
                    for kc in range(4):
                        mm(PS[2 + half][:r, :], XN[:, kc, cs], WO[:, 4 + kc, half * 512:(half + 1) * 512], kc == 0, kc == 3,
                           yak + [("WO", 4 + kc)], [("PS", 2 + half)])

            def op_b1(t_):
                r = rows_of(t_)
                s = t_ % 2
                rsk = SQ(("rs2", s))
                rs2 = SMALL[:r, 70 + 16 * s:72 + 16 * s]
                ts("dve", rs2, PS[4][:r, 0:2], 1.0 / 512, EPS, ALU.mult, ALU.add, [("PS", 4)], [rsk])
                act(rs2, rs2, AF.Sqrt, [rsk], [rsk])
                recip(rs2, rs2, [rsk], [rsk])
                for half in range(2):
                    hs = slice(half * 512, (half + 1) * 512)
                    stt("dve", OST[:r, s, hs], PS[half][:r, :], SMALL[:r, 70 + 16 * s:71 + 16 * s], XST[:r, s, hs], ALU.mult, ALU.add,
                        [("PS", half), rsk, ("XST", s)], [("OST", s, half)])
                    stt("dve", XST[:r, s, hs], PS[2 + half][:r, :], SMALL[:r, 71 + 16 * s:72 + 16 * s], OST[:r, s, hs], ALU.mult, ALU.add,
                        [("PS", 2 + half), rsk, ("OST", s, half)], [("XST", s)])

            def op_b2(t_):
                r = rows_of(t_)
                s = t_ % 2
                fk = SQ(("fs", s))
                fsc = SMALL[:r, 104 + s:105 + s]
                act(OST[:r, s, :], XST[:r, s, :], AF.Square, [("XST", s)], [("OST", s, 0), ("OST", s, 1), fk], accum_out=fsc)
                ts("dve", fsc, fsc, 1.0 / D, EPS, ALU.mult, ALU.add, [fk], [fk])
                act(fsc, fsc, AF.Sqrt, [fk], [fk])
                recip(fsc, fsc, [fk], [fk])
                stt("dve", OST[:r, s, :], XST[:r, s, :], fsc, GFB[:r, :], ALU.mult, ALU.mult, [("XST", s), fk, ("GFB",)], [("OST", s, 0), ("OST", s, 1)])
                dstap = y_p[t_ * 128:(t_ + 1) * 128, :] if t_ < 16 else y_s[:, :]
                dma("sp", dstap, OST[:r, s, :], [("OST", s, 0), ("OST", s, 1)], [("out", "y", t_)])

            op_a(0)
            op_mm(0)
            for t_ in range(NT):
                if t_ + 1 < NT:
                    op_a(t_ + 1)
                op_b1(t_)
                if t_ + 1 < NT:
                    op_mm(t_ + 1)
                op_b2(t_)
        except Stop:
            pass

        with (
            nc.semaphore("s_pe") as s_pe, nc.semaphore("s_act") as s_act, nc.semaphore("s_dve") as s_dve,
            nc.semaphore("s_pool") as s_pool, nc.semaphore("s_sp") as s_sp,
        ):
            dstack = ExitStack()
            with dstack:
                dsems = [dstack.enter_context(nc.semaphore("dma%d" % i)) for i in range(P.ndma)]
                with nc.Block() as block:
                    P.emit(nc, block, {"pe": s_pe, "act": s_act, "dve": s_dve, "pool": s_pool, "sp": s_sp}, dsems)
    return nc


_NC_CACHE = {}


def _shard_inputs(inp):
    maps = []
    f32 = lambda a: np.ascontiguousarray(a, dtype=np.float32)
    for c in range(NCORES):
        sl = slice(NB * c, NB * (c + 1))
        maps.append({
            "xp": f32(inp["x_prompt"][c]), "xsm": f32(inp["x_sample"][sl].reshape(NS, D)),
            "ck": f32(inp["cache_k"][0, sl].reshape(NB, 2048, 512)), "cv": f32(inp["cache_v"][0, sl].reshape(NB, 2048, 512)),
            "st_re": f32(inp["state_ssm_re"][0, sl].reshape(NB, 2048)), "st_im": f32(inp["state_ssm_im"][0, sl].reshape(NB, 2048)),
            "g_in": f32(inp["norm_in_g"][0]), "w_in": f32(inp["w_in"][0]),
            "a_re": f32(inp["ssm_A_re"][0].reshape(2048)), "a_im": f32(inp["ssm_A_im"][0].reshape(2048)),
            "log_dt": f32(inp["ssm_log_dt"][0]),
            "b_re": f32(inp["ssm_B_re"][0].reshape(2048, 16)), "b_im": f32(inp["ssm_B_im"][0].reshape(2048, 16)),
            "c_re": f32(inp["ssm_C_re"][0].reshape(512, 64)), "c_im": f32(inp["ssm_C_im"][0].reshape(512, 64)),
            "d_skip": f32(inp["ssm_D"][0]), "w_glu": f32(inp["w_glu"][0]), "b_glu": f32(inp["b_glu"][0]),
            "g_ssm": f32(inp["norm_ssm_g"][0]), "g_attn": f32(inp["norm_attn_g"][0]), "w_out": f32(inp["w_out"][0]),
            "g_fin": f32(inp["final_norm_g"]),
        })
    return maps


def kernel(**inputs):
    inp = {k: np.asarray(v) for k, v in inputs.items()}
    if "nc" not in _NC_CACHE:
        _NC_CACHE["nc"] = build_program()
    nc = _NC_CACHE["nc"]
    maps = _shard_inputs(inp)
    res = run_bass_kernel_spmd(nc, maps, core_ids=list(range(NCORES)))
    R = res.results
    cat = lambda name: np.stack([np.asarray(R[c][name], dtype=np.float32) for c in range(NCORES)], 0)
    y_prompt = cat("y_p").reshape(8, T, D)
    y_sample = cat("y_s").reshape(8 * NB, 4, D)
    k_prompt = cat("k_p").reshape(1, 8, T, 8, 64)
    v_prompt = cat("v_p").reshape(1, 8, T, 8, 64)
    sre_prompt = cat("sre_p").reshape(1, 8, 32, 64)
    sim_prompt = cat("sim_p").reshape(1, 8, 32, 64)
    k_sample = cat("k_s").reshape(1, 8 * NB, 4, 8, 64)
    v_sample = cat("v_s").reshape(1, 8 * NB, 4, 8, 64)
    sre_sample = cat("sre_s").reshape(1, 8 * NB, 32, 64)
    sim_sample = cat("sim_s").reshape(1, 8 * NB, 32, 64)
    return (y_prompt, y_sample, k_prompt, v_prompt, sre_prompt, sim_prompt, k_sample, v_sample, sre_sample, sim_sample)
```
